# Optimizing a Trainium2 kernel written in Bass

```python
import jax, jax.numpy as jnp
from jax import lax
import numpy as np

D_MODEL = 1024
BATCH = 32
SEQ = 2048
DEPTH = 1
DEC_BATCH = 8
DEC_SEQ = 2048
PAST_LEN = 128

D_RNN = 1280
LRU_BLOCKS = 16
LRU_BW = D_RNN // LRU_BLOCKS
LRU_C = 8.0
LRU_CONV = 4
ATT_GROUPS = ((128, 1), (512, 4), (2048, 16))
N_GROUPS = len(ATT_GROUPS)
HEADS_PER_GROUP = 8
HEAD_DIM = 64
ATT_COLS = N_GROUPS * HEADS_PER_GROUP * HEAD_DIM
ATT_OUT = HEADS_PER_GROUP * HEAD_DIM
ROT_DIM = HEAD_DIM // 4
ROPE_THETA = 500000.0
D_FF = 3 * D_MODEL
FFN_CONV = 3
EPS = 1e-6
NEG = -1e30
IN_COLS = 2 * D_RNN + 3 * ATT_COLS + 2 * D_MODEL

kernel_name = 'hybrid_rglru_dilated_attn_encoder'


def _rmsnorm(x, g):
    xf = x.astype(jnp.float32)
    y = xf * lax.rsqrt(jnp.mean(xf * xf, axis=-1, keepdims=True) + EPS)
    return (y * g.astype(jnp.float32)).astype(x.dtype)


def _dwconv(x, w, b):
    K = w.shape[0]
    S = x.shape[1]
    lo = K // 2
    xp = jnp.pad(x, ((0, 0), (lo, K - 1 - lo), (0, 0)))
    out = b
    for j in range(K):
        out = out + xp[:, j:j + S, :] * w[j]
    return out


def _lin_combine(e1, e2):
    a1, b1 = e1
    a2, b2 = e2
    return a1 * a2, a2 * b1 + b2


def _rglru_direction(xc, wa, ba, wx, bx, lam):
    B, S, _ = xc.shape
    xb = xc.reshape(B, S, LRU_BLOCKS, LRU_BW)
    r = jax.nn.sigmoid((jnp.einsum('bsnc,ncd->bsnd', xb, wa).reshape(B, S, D_RNN) + ba).astype(jnp.float32))
    i = jax.nn.sigmoid((jnp.einsum('bsnc,ncd->bsnd', xb, wx).reshape(B, S, D_RNN) + bx).astype(jnp.float32))
    log_a = -LRU_C * r * jax.nn.softplus(-lam.astype(jnp.float32))
    a = jnp.exp(log_a)
    u = jnp.sqrt(-jnp.expm1(2.0 * log_a)) * (i * xc.astype(jnp.float32))
    _, h = lax.associative_scan(_lin_combine, (a, u), axis=1)
    return h


def _rope(x, cos, sin):
    half = ROT_DIM // 2
    xf = x.astype(jnp.float32)
    x1 = xf[..., :half]
    x2 = xf[..., half:ROT_DIM]
    rot = jnp.concatenate([x1 * cos - x2 * sin, x2 * cos + x1 * sin, xf[..., ROT_DIM:]], axis=-1)
    return rot.astype(x.dtype)


def _banded_attention(q, k, v, half):
    L = q.shape[-2]
    C = half
    n = -(-L // C)
    Lp = n * C
    lead = q.shape[:-2]
    nb = len(lead)
    qb = jnp.pad(q, [(0, 0)] * nb + [(0, Lp - L), (0, 0)]).reshape(lead + (n, C, HEAD_DIM))

    def windows(t):
        tb = jnp.pad(t, [(0, 0)] * nb + [(C, Lp - L + C), (0, 0)]).reshape(lead + (n + 2, C, HEAD_DIM))
        return jnp.concatenate([tb[..., :-2, :, :], tb[..., 1:-1, :, :], tb[..., 2:, :, :]], axis=-2)

    kw = windows(k)
    vw = windows(v)
    qpos = jnp.arange(n)[:, None] * C + jnp.arange(C)[None, :]
    kpos = jnp.arange(n)[:, None] * C - C + jnp.arange(3 * C)[None, :]
    kp = kpos[:, None, :]
    mask = (jnp.abs(qpos[:, :, None] - kp) <= half) & (kp >= 0) & (kp < L)
    s = jnp.einsum('...nqd,...nkd->...nqk', qb, kw, preferred_element_type=jnp.float32) * (HEAD_DIM ** -0.5)
    s = jnp.where(mask, s, NEG)
    lse = jax.nn.logsumexp(s, axis=-1)
    p = jnp.exp(s - lse[..., None])
    o = jnp.einsum('...nqk,...nkd->...nqd', p.astype(v.dtype), vw)
    o = o.reshape(lead + (Lp, HEAD_DIM))[..., :L, :]
    lse = lse.reshape(lead + (Lp,))[..., :L]
    return o, lse


def _dilated_group(q, k, v, window, dilation):
    B, S, H, Dh = q.shape
    L = S // dilation
    half = (window // 2) // dilation

    def split(t):
        return t.reshape(B, L, dilation, H, Dh).transpose(0, 2, 3, 1, 4)

    o, lse = _banded_attention(split(q), split(k), split(v), half)
    o = o.transpose(0, 3, 1, 2, 4).reshape(B, S, H, Dh)
    lse = lse.transpose(0, 3, 1, 2).reshape(B, S, H)
    return o, lse


def _layer(x, norm1_g, w_in, lru_conv_w, lru_conv_b, lru_wa, lru_ba, lru_wx, lru_bx, lru_lambda,
           w_lru_out, q_norm_g, k_norm_g, w_att_out, w_o, norm2_g, w_up, ffn_conv_w, ffn_conv_b, w_down):
    B, S, _ = x.shape
    xn = _rmsnorm(x, norm1_g)
    proj = xn @ w_in
    c0 = D_RNN
    c1 = 2 * D_RNN
    c2 = c1 + ATT_COLS
    c3 = c2 + ATT_COLS
    c4 = c3 + ATT_COLS
    lru_x, lru_gate, q, k, v, gates = jnp.split(proj, [c0, c1, c2, c3, c4], axis=-1)

    xc = _dwconv(lru_x, lru_conv_w, lru_conv_b)
    h_fwd = _rglru_direction(xc, lru_wa[0], lru_ba[0], lru_wx[0], lru_bx[0], lru_lambda[0])
    h_bwd = jnp.flip(_rglru_direction(jnp.flip(xc, axis=1), lru_wa[1], lru_ba[1], lru_wx[1], lru_bx[1], lru_lambda[1]), axis=1)
    a_out = (jax.nn.gelu(lru_gate) * (h_fwd + h_bwd).astype(x.dtype)) @ w_lru_out

    shp = (B, S, N_GROUPS, HEADS_PER_GROUP, HEAD_DIM)
    pos = jnp.arange(S, dtype=jnp.float32)
    inv = ROPE_THETA ** (-jnp.arange(0, ROT_DIM, 2, dtype=jnp.float32) / ROT_DIM)
    ang = pos[:, None] * inv[None, :]
    cos = jnp.cos(ang)[:, None, None, :]
    sin = jnp.sin(ang)[:, None, None, :]
    q = _rope(_rmsnorm(q.reshape(shp), q_norm_g[:, None, :]), cos, sin)
    k = _rope(_rmsnorm(k.reshape(shp), k_norm_g[:, None, :]), cos, sin)
    v = v.reshape(shp)
    outs = []
    lses = []
    for g, (window, dilation) in enumerate(ATT_GROUPS):
        o_g, lse_g = _dilated_group(q[:, :, g], k[:, :, g], v[:, :, g], window, dilation)
        outs.append(o_g)
        lses.append(lse_g)
    wts = jax.nn.softmax(jnp.stack(lses, axis=0), axis=0)
    o = jnp.sum(wts[..., None] * jnp.stack(outs, axis=0).astype(jnp.float32), axis=0).astype(x.dtype)
    b_out = o.reshape(B, S, ATT_OUT) @ w_att_out

    g = jax.nn.sigmoid(gates)
    g_a, g_b = jnp.split(g, 2, axis=-1)
    x = x + (g_a * a_out + g_b * b_out) @ w_o

    xn2 = _rmsnorm(x, norm2_g)
    up = xn2 @ w_up
    gate, val = jnp.split(up, 2, axis=-1)
    gate = _dwconv(gate, ffn_conv_w, ffn_conv_b)
    x = x + (jax.nn.gelu(gate) * val) @ w_down
    return x


def _trunk(x, params):
    for l in range(DEPTH):
        x = _layer(x, **{name: p[l] for name, p in params.items()})
    return x


def setup_inputs(seed: int = 0) -> dict:
    key = jax.random.key(seed)
    ks = jax.random.split(key, 24)
    f32 = jnp.float32
    nrm = lambda k, s, sc: jax.random.normal(k, s, f32) * sc
    u = jax.random.uniform(ks[9], (DEPTH, 2, D_RNN), f32, minval=0.9, maxval=0.999)
    a0 = u ** (1.0 / LRU_C)
    lam = jnp.log(a0) - jnp.log1p(-a0)
    return {
        'x_prompt': nrm(ks[0], (BATCH, SEQ, D_MODEL), 1.0),
        'x_sample': nrm(ks[1], (DEC_BATCH, DEC_SEQ, D_MODEL), 1.0),
        'norm1_g': 1.0 + nrm(ks[2], (DEPTH, D_MODEL), 0.02),
        'w_in': nrm(ks[3], (DEPTH, D_MODEL, IN_COLS), D_MODEL ** -0.5),
        'lru_conv_w': nrm(ks[4], (DEPTH, LRU_CONV, D_RNN), LRU_CONV ** -0.5),
        'lru_conv_b': nrm(ks[5], (DEPTH, D_RNN), 0.02),
        'lru_wa': nrm(ks[6], (DEPTH, 2, LRU_BLOCKS, LRU_BW, LRU_BW), LRU_BW ** -0.5),
        'lru_ba': nrm(ks[7], (DEPTH, 2, D_RNN), 0.02),
        'lru_wx': nrm(ks[8], (DEPTH, 2, LRU_BLOCKS, LRU_BW, LRU_BW), LRU_BW ** -0.5),
        'lru_bx': nrm(ks[10], (DEPTH, 2, D_RNN), 0.02),
        'lru_lambda': lam,
        'w_lru_out': nrm(ks[11], (DEPTH, D_RNN, D_MODEL), D_RNN ** -0.5),
        'q_norm_g': 1.0 + nrm(ks[12], (DEPTH, N_GROUPS, HEAD_DIM), 0.02),
        'k_norm_g': 1.0 + nrm(ks[13], (DEPTH, N_GROUPS, HEAD_DIM), 0.02),
        'w_att_out': nrm(ks[14], (DEPTH, ATT_OUT, D_MODEL), ATT_OUT ** -0.5),
        'w_o': nrm(ks[15], (DEPTH, D_MODEL, D_MODEL), D_MODEL ** -0.5),
        'norm2_g': 1.0 + nrm(ks[16], (DEPTH, D_MODEL), 0.02),
        'w_up': nrm(ks[17], (DEPTH, D_MODEL, 2 * D_FF), D_MODEL ** -0.5),
        'ffn_conv_w': nrm(ks[18], (DEPTH, FFN_CONV, D_FF), FFN_CONV ** -0.5),
        'ffn_conv_b': nrm(ks[19], (DEPTH, D_FF), 0.02),
        'w_down': nrm(ks[20], (DEPTH, D_FF, D_MODEL), D_FF ** -0.5),
    }


def reference(x_prompt, x_sample, norm1_g, w_in, lru_conv_w, lru_conv_b, lru_wa, lru_ba, lru_wx, lru_bx,
              lru_lambda, w_lru_out, q_norm_g, k_norm_g, w_att_out, w_o, norm2_g, w_up, ffn_conv_w,
              ffn_conv_b, w_down):
    params = dict(norm1_g=norm1_g, w_in=w_in, lru_conv_w=lru_conv_w, lru_conv_b=lru_conv_b,
                  lru_wa=lru_wa, lru_ba=lru_ba, lru_wx=lru_wx, lru_bx=lru_bx, lru_lambda=lru_lambda,
                  w_lru_out=w_lru_out, q_norm_g=q_norm_g, k_norm_g=k_norm_g, w_att_out=w_att_out,
                  w_o=w_o, norm2_g=norm2_g, w_up=w_up, ffn_conv_w=ffn_conv_w, ffn_conv_b=ffn_conv_b,
                  w_down=w_down)
    y_prompt = _trunk(x_prompt, params)
    y_sample = _trunk(x_sample, params)
    return (y_prompt, y_sample)
```

```python
import contextlib
import os
import numpy as np
import concourse.bass as bass
import concourse.mybir as mybir
from concourse.bass_utils import run_bass_kernel_spmd

F32 = mybir.dt.float32
BF16 = mybir.dt.bfloat16
AF = mybir.ActivationFunctionType
ALU = mybir.AluOpType

ENGS = ("pe", "act", "dve", "pool", "sp")

D = 1024
S_LEN = 2048
D_RNN = 1280
ATT_COLS = 1536
D_FF = 3072
IN_COLS = 9216
C_Q = 2560
C_K = C_Q + ATT_COLS
C_V = C_K + ATT_COLS
C_G = C_V + ATT_COLS
NT = 4
TS = 512
DIL = (1, 4, 16)
EPS = 1e-6
NCORES = 8


class Tk:
    __slots__ = ("w", "rd")

    def __init__(self):
        self.w = None
        self.rd = []


class Op:
    __slots__ = ("eng", "fn", "deps", "need_inc", "val", "dsem", "waits", "cidx")

    def __init__(self, eng, fn):
        self.eng = eng
        self.fn = fn
        self.deps = []
        self.need_inc = False
        self.val = 0
        self.dsem = None
        self.waits = []
        self.cidx = -1


class Sched:
    def __init__(self, nc):
        self.nc = nc
        self.prog = {e: [] for e in ENGS}
        self.dma_keys = {}
        self.pending_bar = {e: None for e in ENGS}

    def op(self, eng, fn, r=(), w=(), dma=None):
        o = Op(eng, fn)
        deps = []
        for t in r:
            if t.w is not None:
                deps.append(t.w)
        for t in w:
            if t.w is not None:
                deps.append(t.w)
            deps.extend(t.rd)
        for t in r:
            t.rd.append(o)
        for t in w:
            t.w = o
            t.rd = []
        if self.pending_bar[eng] is not None:
            deps.extend(self.pending_bar[eng])
            self.pending_bar[eng] = None
        seen = set()
        for d in deps:
            if id(d) not in seen and d is not o:
                seen.add(id(d))
                o.deps.append(d)
        self.prog[eng].append(o)
        if dma is not None:
            o.dsem = dma
            self.dma_keys.setdefault(dma, []).append(o)
        return o

    def dma_multi(self, eng, key, fns, r=(), w=()):
        first = self.op(eng, fns[0], r=r, w=w, dma=key)
        last = first
        for fn in fns[1:]:
            last = self.op(eng, fn, dma=key)
        if last is not first:
            for t in w:
                t.w = last
            for t in r:
                t.rd.append(last)
        return last

    def barrier(self, skip_out=False):
        lasts = []
        for e in ENGS:
            for o in reversed(self.prog[e]):
                if o.dsem is None:
                    lasts.append(o)
                    break
        for k, ops in self.dma_keys.items():
            if str(k).startswith("p_") or (skip_out and str(k).startswith("o")):
                continue
            lasts.append(ops[-1])
        for e in ENGS:
            cur = self.pending_bar[e] or []
            self.pending_bar[e] = cur + lasts

    def finalize(self):
        def chan(o):
            return o.dsem if o.dsem is not None else o.eng

        for k, ops in self.dma_keys.items():
            for i, o in enumerate(ops):
                o.cidx = i
        for e in ENGS:
            i = 0
            for o in self.prog[e]:
                if o.dsem is None:
                    o.cidx = i
                    i += 1
        for e in ENGS:
            kn = {}
            for o in self.prog[e]:
                best = {}
                for d in o.deps:
                    c = chan(d)
                    if d.dsem is None and d.eng == e and e in ("pe", "sp"):
                        continue
                    if kn.get(c, -1) >= d.cidx:
                        continue
                    if c not in best or best[c].cidx < d.cidx:
                        best[c] = d
                for c, d in best.items():
                    kn[c] = d.cidx
                    d.need_inc = True
                    o.waits.append(d)
        for e in ENGS:
            v = 0
            for o in self.prog[e]:
                if o.dsem is None and o.need_inc:
                    v += 1
                    o.val = v
        for k, ops in self.dma_keys.items():
            for i, o in enumerate(ops):
                o.val = 16 * (i + 1)

    def emit(self):
        nc = self.nc
        self.finalize()
        with contextlib.ExitStack() as st:
            esem = {e: st.enter_context(nc.semaphore("s_" + e)) for e in ENGS}
            dsem = {k: st.enter_context(nc.semaphore("d_%s" % (str(k),))) for k in self.dma_keys}
            block = st.enter_context(nc.Block())

            def semof(o):
                return dsem[o.dsem] if o.dsem is not None else esem[o.eng]

            def run(e):
                def body(eng):
                    for o in self.prog[e]:
                        for d in o.waits:
                            eng.wait_ge(semof(d), d.val)
                        inst = o.fn(eng)
                        if o.dsem is not None:
                            inst.then_inc(dsem[o.dsem], 16)
                        elif o.need_inc:
                            inst.then_inc(esem[e], 1)
                return body

            block.tensor(run("pe"))
            block.scalar(run("act"))
            block.vector(run("dve"))
            block.gpsimd(run("pool"))
            block.sync(run("sp"))


class Arena:
    def __init__(self, nc, st, nbytes):
        self.n = nbytes
        self.t = st.enter_context(nc.sbuf_tensor("arena", [128, nbytes // 4], F32))
        self.off = 0

    def seek(self, kb):
        self.off = int(kb * 1024)

    def alloc(self, shape_free, dtype):
        esz = 2 if dtype == BF16 else 4
        n = int(np.prod(shape_free))
        nb = (n * esz + 31) // 32 * 32
        assert self.off + nb <= self.n, ("arena overflow", self.off, nb, self.n)
        a = self.t[:, self.off // 4:(self.off + nb) // 4]
        self.off += nb
        if dtype != F32:
            a = a.bitcast(dtype)
        a = a[:, 0:n]
        if len(shape_free) == 2:
            a = a.rearrange("p (a b) -> p a b", a=shape_free[0])
        elif len(shape_free) == 3:
            a = a.rearrange("p (a b c) -> p a b c", a=shape_free[0], b=shape_free[1])
        return a


def gate_nbrs(c):
    lo_b = (c * 128) // 80
    hi_b = (c * 128 + 127) // 80
    r0 = lo_b * 80
    r1 = hi_b * 80 + 79
    return list(range(r0 // 128, r1 // 128 + 1))


class _Stop(Exception):
    pass


def build_program(nseq, dbg=False, upto=99):
    nc = bass.Bass("TRN2", target_bir_lowering=False)
    S = Sched(nc)

    def MM(out, lhsT, rhs, start, stop, r, w):
        S.op("pe", lambda e: e.matmul(out, lhsT=lhsT, rhs=rhs, start=start, stop=stop), r=r, w=w)

    def TRN(out, in_, ident, r, w):
        S.op("pe", lambda e: e.transpose(out, in_, ident), r=r, w=w)

    def ACT(out, in_, func, r, w, scale=None, bias=None):
        kw = {}
        if scale is not None:
            kw["scale"] = scale
        if bias is not None:
            kw["bias"] = bias
        S.op("act", lambda e: e.activation(out=out, in_=in_, func=func, **kw), r=r, w=w)

    def STT(eng, out, in0, scalar, in1, op0, op1, r, w):
        S.op(eng, lambda e: e.scalar_tensor_tensor(out=out, in0=in0, scalar=scalar, in1=in1, op0=op0, op1=op1), r=r, w=w)

    def TT(eng, out, in0, in1, op, r, w):
        S.op(eng, lambda e: e.tensor_tensor(out=out, in0=in0, in1=in1, op=op), r=r, w=w)

    def TSC(eng, out, in0, scalar1, op0, r, w):
        S.op(eng, lambda e: e.tensor_scalar(out=out, in0=in0, scalar1=scalar1, scalar2=None, op0=op0), r=r, w=w)

    def RECIP(out, in_, r, w):
        S.op("dve", lambda e: e.reciprocal(out=out, in_=in_), r=r, w=w)

    def MEMSET(eng, ap, val, r, w):
        S.op(eng, lambda e: e.memset(ap, val), r=r, w=w)

    def COPY(eng, out, in_, r, w):
        S.op(eng, lambda e: e.tensor_copy(out=out, in_=in_), r=r, w=w)

    def SCAN(out, d0, d1, r, w):
        S.op("dve", lambda e: e.tensor_tensor_scan(out=out, data0=d0, data1=d1, initial=0.0, op0=ALU.mult, op1=ALU.add),
             r=r, w=w)

    def DMA(eng, key, out, in_, r=(), w=()):
        return S.op(eng, lambda e: e.dma_start(out=out, in_=in_), r=r, w=w, dma=key)

    def dma_fn(out, in_):
        return lambda e: e.dma_start(out=out, in_=in_)

    def din(name, shape, dt=F32):
        return nc.dram_tensor(name, list(shape), dt, kind="ExternalInput").ap()

    xT = din("xT", [nseq, D, S_LEN])
    w_in = din("w_in", [D, IN_COLS])
    w_up = din("w_up", [D, 2 * D_FF])
    w_down = din("w_down", [D_FF, D])
    w_o = din("w_o", [D, D])
    w_lo = din("w_lru_out", [D_RNN, D])
    w_ao = din("w_att_out", [512, D])
    lru_w = din("lru_w", [4, 16, 80, 80])
    pvec = din("pvec", [128, 240])
    cbf = din("cbf", [128, 2896])
    csd = din("cs", [2, 128, S_LEN])
    yT = nc.dram_tensor("yT", [nseq, D, S_LEN], F32, kind="ExternalOutput").ap()

    def dscr(name, shape):
        return nc.dram_tensor(name, list(shape), BF16, kind="Internal").ap()

    b_in = dscr("b_in", [D, IN_COLS])
    b_up = dscr("b_up", [D, 2 * D_FF])
    b_down = dscr("b_down", [D_FF, D])
    b_o = dscr("b_o", [D, D])
    b_lo = dscr("b_lo", [D_RNN, D])
    b_ao = dscr("b_ao", [512, D])
    b_g = dscr("b_g", [4, D_RNN, D_RNN])
    dbg_outs = {}

    st = contextlib.ExitStack()
    with st:
        ar = Arena(nc, st, 200 * 1024)
        ps = [st.enter_context(nc.psum_tensor("ps%d" % i, [128, 512], F32)) for i in range(8)]
        pst = [Tk() for _ in range(8)]
        pstate = {"b": 0, "q": 0}

        def bank(excl=()):
            b = pstate["b"]
            while b in excl:
                b = (b + 1) % 8
            pstate["b"] = (b + 1) % 8
            return b

        def quad():
            q = pstate["q"]
            pstate["q"] = 1 - q
            return [4 * q + i for i in range(4)]

        ar.seek(0)
        pv = ar.alloc([240], F32)
        t_pv = Tk()
        der = ar.alloc([96], F32)
        t_der = Tk()
        cb = ar.alloc([2896], BF16)
        t_cb = Tk()
        ring = [(ar.alloc([4096], BF16), Tk()) for _ in range(3)]
        rstate = {"i": 0}
        P_END = ar.off
        assert P_END <= 31 * 1024, P_END

        PV_G1, PV_G2 = 0, 8
        PV_LCW, PV_LCB = 16, 56
        PV_LAM, PV_BA, PV_BX = 66, 86, 106
        PV_FCW, PV_FCB = 126, 198
        PV_QG, PV_QGP = 222, 228
        PV_EPS, PV_ONE = 234, 235
        DR_HC, DR_C2, DR_HBA, DR_HBX = 0, 20, 40, 60
        CB_ONES, CB_BLK, CB_PERM, CB_O64, CB_MASK, CB_L0, CB_L1, CB_ID, CB_IND = 0, 128, 256, 384, 448, 2496, 2624, 2752, 2880

        def col(a, i):
            return a[:, i:i + 1]

        def tl(t):
            return slice(t * TS, (t + 1) * TS)

        t_bin, t_bup, t_bdown, t_bo, t_blo, t_bao, t_bg = (Tk() for _ in range(7))

        def prep_cast(key, dst, src, rows, tk, nsplit):
            step = rows // nsplit
            fns = [dma_fn(dst[i * step:(i + 1) * step, :], src[i * step:(i + 1) * step, :]) for i in range(nsplit)]
            S.dma_multi("pool", key, fns, w=[tk])

        DMA("sp", "c_pv", pv, pvec, w=[t_pv])
        DMA("pool", "c_cb", cb, cbf, w=[t_cb])
        t_bin_grp = [Tk(), Tk(), Tk()]
        BIN_COLS = [(0, 2560), (2560, C_G), (C_G, IN_COLS)]
        for gi_, (c0_, c1_) in enumerate(BIN_COLS[:1]):
            fns_ = [dma_fn(b_in[i * 256:(i + 1) * 256, c0_:c1_], w_in[i * 256:(i + 1) * 256, c0_:c1_]) for i in range(4)]
            S.dma_multi("pool", "p_in%d" % gi_, fns_, w=[t_bin_grp[gi_]])
        zt = ring[0][0]
        MEMSET("dve", zt[:, 0:D_RNN], 0.0, r=[], w=[ring[0][1]])
        fns = [dma_fn(b_g[dg, k * 128:(k + 1) * 128, :], zt[:, 0:D_RNN]) for dg in range(4) for k in range(10)]
        t_bgz = Tk()
        S.dma_multi("sp", "p_gz", fns, r=[ring[0][1]], w=[t_bgz])
        fns = [dma_fn(b_g[dg, b * 80:(b + 1) * 80, b * 80:(b + 1) * 80], lru_w[dg, b]) for dg in range(4) for b in range(16)]
        S.dma_multi("pool", "p_g", fns, r=[t_bgz], w=[t_bg])
        for gi_, (c0_, c1_) in list(enumerate(BIN_COLS))[1:]:
            fns_ = [dma_fn(b_in[i * 256:(i + 1) * 256, c0_:c1_], w_in[i * 256:(i + 1) * 256, c0_:c1_]) for i in range(4)]
            S.dma_multi("pool", "p_in%d" % gi_, fns_, w=[t_bin_grp[gi_]])
        prep_cast("p_lo", b_lo, w_lo, D_RNN, t_blo, 2)
        prep_cast("p_ao", b_ao, w_ao, 512, t_bao, 1)
        prep_cast("p_o", b_o, w_o, D, t_bo, 2)
        prep_cast("p_up", b_up, w_up, D, t_bup, 8)
        prep_cast("p_down", b_down, w_down, D_FF, t_bdown, 4)

        ACT(der[:, 0:20], pv[:, PV_LAM:PV_LAM + 20], AF.Exp, r=[t_pv], w=[t_der], scale=-1.0)
        ACT(der[:, 0:20], der[:, 0:20], AF.Ln, r=[t_pv, t_der], w=[t_der], bias=col(pv, PV_ONE))
        TSC("dve", der[:, DR_C2:DR_C2 + 20], der[:, 0:20], -8.0, ALU.mult, r=[t_der], w=[t_der])
        TSC("dve", der[:, DR_HC:DR_HC + 20], der[:, 0:20], -4.0, ALU.mult, r=[t_der], w=[t_der])
        TSC("dve", der[:, DR_HBA:DR_HBA + 40], pv[:, PV_BA:PV_BA + 40], 0.5, ALU.mult, r=[t_pv, t_der], w=[t_der])

        def load_w(src, t_src, k0, nk, c0, ncols):
            if t_src is t_bin:
                t_src = t_bin_grp[0] if c0 < 2560 else (t_bin_grp[1] if c0 < C_G else t_bin_grp[2])
            slot = rstate["i"] % 3
            rstate["i"] += 1
            buf, tk = ring[slot]
            view = buf[:, 0:nk * ncols].rearrange("p (k c) -> p k c", k=nk)
            srcv = src[k0 * 128:(k0 + nk) * 128, c0:c0 + ncols].rearrange("(k p) c -> p k c", p=128)
            DMA("sp", "w%d" % slot, view, srcv, r=[t_src], w=[tk])
            return view, tk

        def load_gate_w(c):
            nb = gate_nbrs(c)
            nk = len(nb)
            slot = rstate["i"] % 3
            rstate["i"] += 1
            buf, tk = ring[slot]
            view = buf[:, 0:4 * nk * 128].rearrange("p (g k c) -> p g k c", g=4, k=nk)
            fns = []
            for dg in range(4):
                srcv = b_g[dg, nb[0] * 128:(nb[0] + nk) * 128, c * 128:(c + 1) * 128].rearrange("(k p) c -> p k c", p=128)
                fns.append(dma_fn(view[:, dg], srcv))
            S.dma_multi("sp", "w%d" % slot, fns, r=[t_bg], w=[tk])
            return view, tk, nb

        def proj4(wt, t_w, nk, j, src, src_tks, q=None, t_outer=True):
            if q is None:
                q = quad()
            order = [(k, t) for t in range(NT) for k in range(nk)] if t_outer else [(k, t) for k in range(nk) for t in range(NT)]
            for (k, t) in order:
                MM(ps[q[t]][:], wt[:, k, j * 128:(j + 1) * 128], src[:, k, tl(t)], k == 0, k == nk - 1,
                   r=[t_w, src_tks[k][t]], w=[pst[q[t]]])
            return q

        dbgc = {"i": 0}

        def dump(name, ap, tks, shape, dt=F32):
            if not dbg:
                return
            o = nc.dram_tensor("dbg_" + name, list(shape), dt, kind="ExternalOutput").ap()
            dbg_outs[name] = 1
            DMA("sp", "dbg%d" % dbgc["i"], o, ap, r=tks, w=[Tk()])
            dbgc["i"] += 1

        KB_XN = 31
        KB_MA = 63
        KB_SCR = 95
        ar.seek(KB_XN)
        xn = ar.alloc([8, S_LEN], BF16)
        ar.seek(KB_MA)
        ma = ar.alloc([8, S_LEN], BF16)
        ar.seek(136)
        x1 = ar.alloc([8, S_LEN], F32)

        def seq(s):
            t_xn = [[Tk() for _ in range(NT)] for _ in range(8)]
            t_ma = [[Tk() for _ in range(NT)] for _ in range(8)]
            t_x1 = [[Tk() for _ in range(NT)] for _ in range(8)]
            S.barrier(skip_out=True)
            ar.seek(KB_SCR)
            xt = [ar.alloc([8, TS], F32) for _ in range(2)]
            t_xt = [Tk(), Tk()]
            sq = ar.alloc([8, TS], BF16)
            t_sq = Tk()
            assert ar.off <= 136 * 1024
            ar.seek(KB_MA)
            rs = ar.alloc([TS], F32)
            t_rs = Tk()
            for t in range(NT):
                b = t % 2
                DMA("sp", "x%d" % b, xt[b], xT[s, :, tl(t)].rearrange("(k p) n -> p k n", p=128), w=[t_xt[b]])
                ACT(sq, xt[b], AF.Square, r=[t_xt[b]], w=[t_sq])
                pb = bank()
                for k in range(8):
                    MM(ps[pb][:], cb[:, CB_ONES:CB_ONES + 128], sq[:, k, :], k == 0, k == 7, r=[t_sq, t_cb], w=[pst[pb]])
                ACT(rs, ps[pb][:], AF.Sqrt, r=[pst[pb], t_pv], w=[t_rs], bias=col(pv, PV_EPS))
                RECIP(rs, rs, r=[t_rs], w=[t_rs])
                for k in range(8):
                    STT("dve", xn[:, k, tl(t)], xt[b][:, k, :], col(pv, PV_G1 + k), rs, ALU.mult, ALU.mult,
                        r=[t_xt[b], t_rs, t_pv], w=[t_xn[k][t]])
            if s == 0:
                dump("xn", xn, [t_xn[k][t] for k in range(8) for t in range(NT)], [128, 8, S_LEN], BF16)

            if upto <= 1:
                raise _Stop()
            S.barrier()
            ar.seek(KB_MA)
            Abuf = ar.alloc([2, S_LEN], F32)
            S2 = ar.alloc([2, S_LEN], F32)
            ar.seek(KB_SCR)
            y = ar.alloc([10, S_LEN], BF16)
            xcb = ar.alloc([5, S_LEN], BF16)
            lx = ar.alloc([S_LEN + 8], F32)
            U = ar.alloc([2, S_LEN], F32)
            gg = ar.alloc([S_LEN], F32)
            TR = [ar.alloc([TS], F32) for _ in range(2)]
            TI = [ar.alloc([TS], F32) for _ in range(2)]
            t_y = [[Tk() for _ in range(NT)] for _ in range(10)]
            t_lx, t_gg = Tk(), Tk()
            t_A = [[Tk() for _ in range(NT)] for _ in range(2)]
            t_S2 = [[Tk() for _ in range(NT)] for _ in range(2)]
            t_U = [[Tk() for _ in range(NT)] for _ in range(2)]
            A_all = t_A[0] + t_A[1]
            S2_all = t_S2[0] + t_S2[1]
            U_all = t_U[0] + t_U[1]
            t_TR = [Tk(), Tk()]
            t_TI = [Tk(), Tk()]
            MEMSET("dve", lx, 0.0, r=[], w=[t_lx])
            tcnt = 0
            t_xcb = [[Tk() for _ in range(NT)] for _ in range(5)]
            for sb in range(2):
                wt_a, tk_a = load_w(b_in, t_bin, 0, 8, sb * 640, 512)
                wt_b, tk_b = load_w(b_in, t_bin, 0, 8, sb * 640 + 512, 128)
                for cl in range(5):
                    c = sb * 5 + cl
                    wt, tkw, j = (wt_a, tk_a, cl) if cl < 4 else (wt_b, tk_b, 0)
                    q = proj4(wt, tkw, 8, j, xn, t_xn)
                    for t in range(NT):
                        ACT(lx[:, 2 + t * TS:2 + (t + 1) * TS], ps[q[t]][:], AF.Copy, r=[pst[q[t]]], w=[t_lx])
                    tmp = U[:, 0, :]
                    ACT(tmp, lx[:, 0:S_LEN], AF.Identity, r=[t_lx, t_pv], w=t_U[0],
                        scale=col(pv, PV_LCW + c * 4 + 0), bias=col(pv, PV_LCB + c))
                    for jj in (1, 2):
                        STT("dve", tmp, lx[:, jj:jj + S_LEN], col(pv, PV_LCW + c * 4 + jj), tmp, ALU.mult, ALU.add,
                            r=[t_lx, t_pv] + t_U[0], w=t_U[0])
                    STT("dve", xcb[:, cl, :], lx[:, 3:3 + S_LEN], col(pv, PV_LCW + c * 4 + 3), tmp, ALU.mult, ALU.add,
                        r=[t_lx, t_pv] + t_U[0], w=t_xcb[cl])
                if s == 0 and sb == 0:
                    dump("xcb", xcb, [t_xcb[c][t] for c in range(5) for t in range(NT)], [128, 5, S_LEN], BF16)
                if upto <= 2:
                    raise _Stop()
                for cl in range(5):
                    c = sb * 5 + cl
                    gw, tkg, nb = load_gate_w(c)
                    nk = len(nb)
                    for dr in range(2):
                        idx = dr * 10 + c
                        for t in range(NT):
                            pr, pi = bank(), bank()
                            for (pb, gi) in ((pr, 0), (pi, 1)):
                                for ki in range(nk):
                                    icl = nb[ki] - sb * 5
                                    MM(ps[pb][:], gw[:, dr * 2 + gi, ki, :], xcb[:, icl, tl(t)], ki == 0, ki == nk - 1,
                                       r=[tkg, t_xcb[icl][t]], w=[pst[pb]])
                            i2 = tcnt % 2
                            tcnt += 1
                            ACT(TR[i2], ps[pr][:], AF.Tanh, r=[pst[pr], t_der], w=[t_TR[i2]], scale=0.5, bias=col(der, DR_HBA + idx))
                            ACT(Abuf[:, dr, tl(t)], TR[i2], AF.Exp, r=[t_TR[i2], t_der], w=[t_A[dr][t]],
                                scale=col(der, DR_HC + idx), bias=col(der, DR_HC + idx))
                            TT("pool", S2[:, dr, tl(t)], Abuf[:, dr, tl(t)], Abuf[:, dr, tl(t)], ALU.mult,
                               r=[t_A[dr][t]], w=[t_S2[dr][t]])
                            ACT(U[:, dr, tl(t)], ps[pi][:], AF.Tanh, r=[pst[pi], t_der], w=[t_U[dr][t]], scale=0.5,
                                bias=col(der, DR_HBX + idx))
                            STT("dve", U[:, dr, tl(t)], U[:, dr, tl(t)], 1.0, xcb[:, cl, tl(t)], ALU.add, ALU.mult,
                                r=[t_U[dr][t], t_xcb[cl][t]], w=[t_U[dr][t]])
                    ACT(S2, S2, AF.Sqrt, r=[t_pv] + S2_all, w=S2_all, scale=-1.0, bias=col(pv, PV_ONE))
                    STT("dve", U, U, 0.5, S2, ALU.mult, ALU.mult, r=U_all + S2_all, w=U_all)
                    SCAN(S2[:, 0, :], Abuf[:, 0, :], U[:, 0, :], r=t_A[0] + t_U[0] + t_S2[0], w=t_S2[0])
                    SCAN(S2[:, 1, ::-1], Abuf[:, 1, ::-1], U[:, 1, ::-1], r=t_A[1] + t_U[1] + t_S2[1], w=t_S2[1])
                    wtg, tkwg = load_w(b_in, t_bin, 0, 8, D_RNN + c * 128, 128)
                    q = proj4(wtg, tkwg, 8, 0, xn, t_xn)
                    for t in range(NT):
                        ACT(gg[:, tl(t)], ps[q[t]][:], AF.Gelu_apprx_tanh, r=[pst[q[t]]], w=[t_gg])
                    TT("dve", lx[:, 2:2 + S_LEN], S2[:, 0, :], S2[:, 1, :], ALU.add, r=S2_all, w=[t_lx])
                    TT("dve", y[:, c, :], lx[:, 2:2 + S_LEN], gg, ALU.mult, r=[t_lx, t_gg], w=t_y[c])
            if s == 0:
                dump("y", y, [t_y[c][t] for c in range(10) for t in range(NT)], [128, 10, S_LEN], BF16)
            if upto <= 3:
                raise _Stop()
            S.barrier()
            TG = U[:, 0, :]
            t_TG = t_U[0][0]
            for oc2 in range(4):
                wl, tkl = load_w(b_lo, t_blo, 0, 10, oc2 * 256, 256)
                wg_, tkg_ = load_w(b_in, t_bin, 0, 8, C_G + oc2 * 256, 256)
                for j in range(2):
                    oc = oc2 * 2 + j
                    qg_ = proj4(wg_, tkg_, 8, j, xn, t_xn)
                    for t in range(NT):
                        ACT(TG[:, tl(t)], ps[qg_[t]][:], AF.Tanh, r=[pst[qg_[t]]], w=[t_TG], scale=0.5)
                    qa = proj4(wl, tkl, 10, j, y, t_y)
                    for t in range(NT):
                        STT("dve", ma[:, oc, tl(t)], TG[:, tl(t)], 1.0, ps[qa[t]][:], ALU.add, ALU.mult,
                            r=[t_TG, pst[qa[t]]], w=[t_ma[oc][t]])
            if s == 0:
                dump("ma", ma, [t_ma[c][t] for c in range(8) for t in range(NT)], [128, 8, S_LEN], BF16)

            if upto <= 4.05:
                raise _Stop()
            S.barrier()
            ar.seek(KB_SCR)
            cosT = ar.alloc([S_LEN], F32)
            sinT = ar.alloc([S_LEN], F32)
            accN = ar.alloc([S_LEN], F32)
            accD = ar.alloc([S_LEN], F32)
            QT = ar.alloc([S_LEN + 128], BF16)
            KTh = [ar.alloc([S_LEN + 128], BF16) for _ in range(2)]
            Vth = [ar.alloc([32, 128], BF16) for _ in range(2)]
            VT = ar.alloc([S_LEN + 128], BF16)
            t_VT = Tk()
            obf = ar.alloc([4, S_LEN], BF16)
            sqb = [ar.alloc([TS], BF16) for _ in range(2)]
            qrb = [ar.alloc([TS], BF16) for _ in range(2)]
            srp = [ar.alloc([8], F32) for _ in range(2)]
            Bc = [ar.alloc([8, 64], BF16) for _ in range(2)]
            t1b = [ar.alloc([TS], F32) for _ in range(4)]
            t2b = [ar.alloc([TS], F32) for _ in range(2)]
            Pb = [ar.alloc([TS], BF16) for _ in range(4)]
            assert ar.off <= 200 * 1024
            t_cs, t_accN, t_accD, t_QT, t_KT, t_V = Tk(), Tk(), Tk(), Tk(), Tk(), Tk()
            t_obf = [[Tk() for _ in range(NT)] for _ in range(4)]
            t_sqb, t_qrb, t_rst, t_t2, t_Bc = ([Tk(), Tk()] for _ in range(5))
            t_t1 = [Tk() for _ in range(4)]
            t_P = [Tk(), Tk(), Tk(), Tk()]
            t_Vt = [Tk() for _ in range(32)]
            S.dma_multi("sp", "cs", [dma_fn(cosT, csd[0]), dma_fn(sinT, csd[1])], w=[t_cs])
            MEMSET("pool", QT, 0.0, r=[], w=[t_QT])
            MEMSET("pool", VT, 0.0, r=[], w=[t_VT])
            for h_ in range(2):
                MEMSET("pool", KTh[h_], 0.0, r=[], w=[t_KT])
                MEMSET("pool", Vth[h_], 0.0, r=[], w=t_Vt)
            ncnt = [0]
            pcnt = [0]
            bB = [0]
            bA = [0]

            def bankB():
                v = 4 + bB[0] % 4
                bB[0] += 1
                return v

            def bankA():
                v = bA[0] % 4
                bA[0] += 1
                return v

            QA = [0, 1, 2, 3]

            def combine_piece(hp_, t):
                RECIP(accD[:, tl(t)], accD[:, tl(t)], r=[t_accD], w=[t_accD])
                if t == NT - 1:
                    TT("pool", obf[:, hp_, :], accN, accD, ALU.mult, r=[t_accN, t_accD], w=t_obf[hp_])

            def combine(hp_):
                for t in range(NT):
                    combine_piece(hp_, t)

            for hp in range(4):
                for g in range(3):
                    d = DIL[g]
                    L = S_LEN // d
                    ntl = L // 128 + 1
                    wq, tkq = load_w(b_in, t_bin, 0, 8, C_Q + g * 512 + hp * 128, 128)
                    wk, tkk = load_w(b_in, t_bin, 0, 8, C_K + g * 512 + hp * 128, 128)
                    wv, tkv = load_w(b_in, t_bin, 0, 8, C_V + g * 512 + hp * 128, 128)
                    QB = [4, 5, 6, 7]

                    def v_proj():
                        qv = proj4(wv, tkv, 8, 0, xn, t_xn, q=QB)
                        body = VT[:, 64:64 + S_LEN]
                        for t in range(NT):
                            if d == 1:
                                ov = body[:, tl(t)]
                                iv = ps[qv[t]][:]
                            else:
                                w_ = TS // d
                                ov = body.rearrange("p (m q) -> p m q", m=d)[:, :, t * w_:(t + 1) * w_]
                                iv = ps[qv[t]][:].rearrange("p (j m) -> p m j", m=d)
                            ACT(ov, iv, AF.Copy, r=[pst[qv[t]]], w=[t_VT])

                    def v_tiles():
                        ntiles = d * ntl
                        for g0 in range(0, ntiles, 8):
                            n = min(8, ntiles - g0)
                            pb = bankB()
                            psv = ps[pb][:].bitcast(BF16)
                            for si in range(n):
                                tid = g0 + si
                                c0 = (tid // ntl) * L + 128 * (tid % ntl)
                                TRN(psv[:, si * 128:(si + 1) * 128], VT[:, c0:c0 + 128], cb[:, CB_ID:CB_ID + 128],
                                    r=[t_VT, t_cb], w=[pst[pb]])
                            for h_ in range(2):
                                ACT(Vth[h_][:, g0:g0 + n, h_ * 64:(h_ + 1) * 64],
                                    psv[:, 0:n * 128].rearrange("p (s c) -> p s c", s=n)[:, :, h_ * 64:(h_ + 1) * 64], AF.Copy,
                                    r=[pst[pb]], w=[t_Vt[tid] for tid in range(g0, g0 + n)])

                    st_ = {}

                    def S0(which, q, t):
                        gi = which * 3 + g
                        i2 = t % 2
                        ACT(sqb[i2], ps[q[t]][:], AF.Square, r=[pst[q[t]]], w=[t_sqb[i2]])
                        ACT(qrb[i2], ps[q[t]][:], AF.Copy, r=[pst[q[t]]], w=[t_qrb[i2]])
                        STT("dve", t1b[t], ps[q[t]][:], col(pv, PV_QG + gi), cosT[:, tl(t)], ALU.mult, ALU.mult,
                            r=[pst[q[t]], t_pv, t_cs, t_qrb[i2]], w=[t_t1[t]])

                    def PEn(which, t):
                        i2 = t % 2
                        pm, pr = bankB(), bankB()
                        st_[(which, t)] = (pm, pr)
                        MM(ps[pr][:], cb[:, CB_PERM:CB_PERM + 128], qrb[i2], True, True, r=[t_cb, t_qrb[i2]], w=[pst[pr]])
                        for bl in range(4):
                            MM(ps[pm][:, 2 * bl:2 * bl + 2], sqb[i2][:, bl * 128:(bl + 1) * 128], cb[:, CB_IND:CB_IND + 2],
                               True, True, r=[t_cb, t_sqb[i2]], w=[pst[pm]])

                    def rest(which, pair):
                        gi = which * 3 + g
                        t_dst = t_QT if which == 0 else t_KT
                        for t in pair:
                            i2 = t % 2
                            pm, pr = st_[(which, t)]
                            STT("dve", t2b[i2], ps[pr][:], col(pv, PV_QGP + gi), sinT[:, tl(t)], ALU.mult, ALU.mult,
                                r=[pst[pr], t_pv, t_cs], w=[t_t2[i2]])
                            ACT(srp[i2], ps[pm][:, 0:8], AF.Sqrt, r=[pst[pm], t_pv], w=[t_rst[i2]], bias=col(pv, PV_EPS))
                        for t in pair:
                            i2 = t % 2
                            RECIP(srp[i2], srp[i2], r=[t_rst[i2]], w=[t_rst[i2]])
                            TT("pool", t1b[t], t1b[t], t2b[i2], ALU.add, r=[t_t1[t], t_t2[i2]], w=[t_t1[t]])
                        for t in pair:
                            i2 = t % 2
                            ACT(Bc[i2], srp[i2].unsqueeze(2).broadcast_to([128, 8, 64]), AF.Copy, r=[t_rst[i2]], w=[t_Bc[i2]])
                        for t in pair:
                            i2 = t % 2
                            pm, pr = st_[(which, t)]
                            pbc = ps[pm][:].bitcast(BF16)[:, 32:32 + TS]
                            for bl in range(4):
                                TRN(pbc[:, bl * 128:(bl + 1) * 128], Bc[i2][:, 2 * bl:2 * bl + 2, :].rearrange("p a b -> p (a b)"),
                                    cb[:, CB_ID:CB_ID + 128], r=[t_Bc[i2], t_cb, pst[pm]], w=[pst[pm]])
                        for t in pair:
                            pm, pr = st_[(which, t)]
                            pbc = ps[pm][:].bitcast(BF16)[:, 32:32 + TS]
                            parts = [(QT, 0, 128)] if which == 0 else [(KTh[0], 0, 64), (KTh[1], 64, 128)]
                            for (dstb, p0, p1) in parts:
                                body = dstb[p0:p1, 64:64 + S_LEN]
                                if d == 1:
                                    ov = body[:, tl(t)]
                                    i0v = t1b[t][p0:p1]
                                    i1v = pbc[p0:p1]
                                else:
                                    w_ = TS // d
                                    ov = body.rearrange("p (m q) -> p m q", m=d)[:, :, t * w_:(t + 1) * w_]
                                    i0v = t1b[t][p0:p1].rearrange("p (j m) -> p m j", m=d)
                                    i1v = pbc[p0:p1].rearrange("p (j m) -> p m j", m=d)
                                TT("dve", ov, i0v, i1v, ALU.mult, r=[t_t1[t], pst[pm]], w=[t_dst])

                    def v_evac(qv):
                        body = VT[:, 64:64 + S_LEN]
                        for t in range(NT):
                            if d == 1:
                                ov = body[:, tl(t)]
                                iv = ps[qv[t]][:]
                            else:
                                w_ = TS // d
                                ov = body.rearrange("p (m q) -> p m q", m=d)[:, :, t * w_:(t + 1) * w_]
                                iv = ps[qv[t]][:].rearrange("p (j m) -> p m j", m=d)
                            ACT(ov, iv, AF.Copy, r=[pst[qv[t]]], w=[t_VT])

                    qq = proj4(wq, tkq, 8, 0, xn, t_xn, q=QA, t_outer=True)
                    S0(0, qq, 0)
                    S0(0, qq, 1)
                    PEn(0, 0)
                    PEn(0, 1)
                    S0(0, qq, 2)
                    S0(0, qq, 3)
                    dfr = (g == 0 and hp > 0)
                    qk_ = proj4(wk, tkk, 8, 0, xn, t_xn, q=QA, t_outer=True)
                    if dfr:
                        combine_piece(hp - 1, 0)
                    rest(0, (0, 1))
                    PEn(0, 2)
                    PEn(0, 3)
                    if dfr:
                        combine_piece(hp - 1, 1)
                    rest(0, (2, 3))
                    if dfr:
                        combine_piece(hp - 1, 2)
                    S0(1, qk_, 0)
                    S0(1, qk_, 1)
                    PEn(1, 0)
                    PEn(1, 1)
                    S0(1, qk_, 2)
                    S0(1, qk_, 3)
                    if dfr:
                        combine_piece(hp - 1, 3)
                    qv_ = proj4(wv, tkv, 8, 0, xn, t_xn, q=QA, t_outer=True)
                    rest(1, (0, 1))
                    v_evac(qv_)
                    v_tiles()
                    PEn(1, 2)
                    PEn(1, 3)
                    rest(1, (2, 3))
                    if s == 0 and hp == 0:
                        dump("QT%d" % g, QT, [t_QT], [128, S_LEN + 128], BF16)
                        dump("KT%d" % g, KTh[0], [t_KT], [128, S_LEN + 128], BF16)
                    blocks = []
                    for qb in range(16):
                        pi0 = qb * 128
                        m = pi0 // L
                        i0 = pi0 % L
                        first = (i0 == 0)
                        last = (i0 == L - 128)
                        mk = (3 if last else 1) if first else (2 if last else 0)
                        tA = m * ntl + i0 // 128
                        blocks.append(dict(pi0=pi0, mk=mk, tA=tA, tB=tA + 1, kA=pi0, kB=pi0 + 128))

                    def s_stage(bk):
                        psb = bankA()
                        for h in range(2):
                            for (seg, kc0) in ((2 * h, bk["kA"]), (2 * h + 1, bk["kB"])):
                                MM(ps[psb][:, seg * 128:(seg + 1) * 128], KTh[h][:, kc0:kc0 + 128],
                                   QT[:, 64 + bk["pi0"]:64 + bk["pi0"] + 128], True, True, r=[t_KT, t_QT], w=[pst[psb]])
                        p3 = pcnt[0] % 4
                        pcnt[0] += 1
                        bk["p3"] = p3
                        ACT(Pb[p3], ps[psb][:], AF.Exp, r=[pst[psb]], w=[t_P[p3]], scale=0.125)
                        mk = bk["mk"]
                        TT("pool" if os.environ.get("K_PMASK") else "dve", Pb[p3], Pb[p3],
                           cb[:, CB_MASK + mk * 512:CB_MASK + (mk + 1) * 512], ALU.mult, r=[t_P[p3], t_cb], w=[t_P[p3]])

                    def pv_stage(bk, pn, pd, qi):
                        p3 = bk["p3"]
                        for (seg, h, tid) in ((0, 0, bk["tA"]), (1, 0, bk["tB"]), (2, 1, bk["tA"]), (3, 1, bk["tB"])):
                            MM(ps[pn][:, qi * 128:(qi + 1) * 128], Vth[h][:, tid, :],
                               Pb[p3][:, seg * 128:(seg + 1) * 128], seg == 0, seg == 3, r=[t_Vt[tid], t_P[p3]], w=[pst[pn]])
                        for (seg, h) in ((0, 0), (1, 0), (2, 1), (3, 1)):
                            cl_ = CB_L0 if h == 0 else CB_L1
                            MM(ps[pd][:, qi * 128:(qi + 1) * 128], cb[:, cl_:cl_ + 128],
                               Pb[p3][:, seg * 128:(seg + 1) * 128], seg == 0, seg == 3, r=[t_cb, t_P[p3]], w=[pst[pd]])

                    s_stage(blocks[0])
                    s_stage(blocks[1])
                    s_stage(blocks[2])
                    for qb in range(16):
                        qb4, qi = qb // 4, qb % 4
                        pn, pd = (4, 5) if qb4 % 2 == 0 else (6, 7)
                        pv_stage(blocks[qb], pn, pd, qi)
                        if qb + 3 < 16:
                            s_stage(blocks[qb + 3])
                        if qi == 3:
                            for (acc, t_acc, pb) in ((accN, t_accN, pn), (accD, t_accD, pd)):
                                if d == 1:
                                    av = acc[:, qb4 * 512:(qb4 + 1) * 512]
                                    pv_ = ps[pb][:]
                                elif d == 4:
                                    av = acc[:, qb4:S_LEN:4]
                                    pv_ = ps[pb][:]
                                else:
                                    av = acc.rearrange("p (j m) -> p m j", m=16)[:, 4 * qb4:4 * qb4 + 4, :]
                                    pv_ = ps[pb][:].rearrange("p (m j) -> p m j", m=4)
                                if g == 0:
                                    COPY("dve", av, pv_, r=[pst[pb]], w=[t_acc])
                                else:
                                    TT("dve", av, pv_, av, ALU.add, r=[pst[pb], t_acc], w=[t_acc])
            combine(3)
            if s == 0:
                dump("obf", obf, [t_obf[c][t] for c in range(4) for t in range(NT)], [128, 4, S_LEN], BF16)
            TG = accN
            t_TG = t_accN
            TB = accD
            t_TB = t_accD
            for oc2 in range(4):
                wa_, tka_ = load_w(b_ao, t_bao, 0, 4, oc2 * 256, 256)
                wg_, tkg_ = load_w(b_in, t_bin, 0, 8, C_G + D + oc2 * 256, 256)
                for j in range(2):
                    oc = oc2 * 2 + j
                    qg_ = proj4(wg_, tkg_, 8, j, xn, t_xn)
                    for t in range(NT):
                        ACT(TG[:, tl(t)], ps[qg_[t]][:], AF.Tanh, r=[pst[qg_[t]]], w=[t_TG], scale=0.5)
                    qa = proj4(wa_, tka_, 4, j, obf, t_obf)
                    for t in range(NT):
                        STT("dve", TB[:, tl(t)], TG[:, tl(t)], 1.0, ps[qa[t]][:], ALU.add, ALU.mult,
                            r=[t_TG, pst[qa[t]]], w=[t_TB])
                        TT("pool", ma[:, oc, tl(t)], ma[:, oc, tl(t)], TB[:, tl(t)], ALU.add,
                           r=[t_TB, t_ma[oc][t]], w=[t_ma[oc][t]])
            if s == 0:
                dump("m", ma, [t_ma[c][t] for c in range(8) for t in range(NT)], [128, 8, S_LEN], BF16)

            if upto <= 5:
                raise _Stop()
            S.barrier()
            for oc4 in range(2):
                wo_, tko_ = load_w(b_o, t_bo, 0, 8, oc4 * 512, 512)
                for j in range(4):
                    oc = oc4 * 4 + j
                    DMA("sp", "xr%d" % oc, x1[:, oc, :], xT[s, oc * 128:(oc + 1) * 128, :], w=t_x1[oc])
                    q = proj4(wo_, tko_, 8, j, ma, t_ma)
                    for t in range(NT):
                        STT("dve", x1[:, oc, tl(t)], ps[q[t]][:], 0.5, x1[:, oc, tl(t)], ALU.mult, ALU.add,
                            r=[pst[q[t]], t_x1[oc][t]], w=[t_x1[oc][t]])
            if s == 0:
                dump("x1", x1, [t_x1[c][t] for c in range(8) for t in range(NT)], [128, 8, S_LEN], F32)

            if upto <= 6:
                raise _Stop()
            xn2 = xn
            t_xn2 = t_xn
            ar.seek(KB_MA)
            hh = ar.alloc([12, S_LEN], BF16)
            graw = ar.alloc([S_LEN + 8], F32)
            gcv = ar.alloc([S_LEN], F32)
            assert ar.off <= 136 * 1024, ar.off
            t_hh = [[Tk() for _ in range(NT)] for _ in range(12)]
            t_graw, t_gcv = Tk(), Tk()
            sq2 = graw.bitcast(BF16)[:, 0:8 * TS].rearrange("p (k n) -> p k n", k=8)
            rs2 = gcv[:, 0:TS]
            for t in range(NT):
                ACT(sq2, x1[:, :, tl(t)], AF.Square, r=[t_x1[k][t] for k in range(8)], w=[t_graw])
                pb = bank()
                for k in range(8):
                    MM(ps[pb][:], cb[:, CB_ONES:CB_ONES + 128], sq2[:, k, :], k == 0, k == 7, r=[t_graw, t_cb], w=[pst[pb]])
                ACT(rs2, ps[pb][:], AF.Sqrt, r=[pst[pb], t_pv], w=[t_gcv], bias=col(pv, PV_EPS))
                RECIP(rs2, rs2, r=[t_gcv], w=[t_gcv])
                for k in range(8):
                    STT("dve", xn2[:, k, tl(t)], x1[:, k, tl(t)], col(pv, PV_G2 + k), rs2, ALU.mult, ALU.mult,
                        r=[t_x1[k][t], t_gcv, t_pv], w=[t_xn2[k][t]])
            MEMSET("dve", graw, 0.0, r=[t_gcv], w=[t_graw])
            for half in range(2):
                for grp in range(3):
                    cbase = half * 12 + grp * 4
                    wgt, tkgt = load_w(b_up, t_bup, 0, 8, cbase * 128, 512)
                    wvl, tkvl = load_w(b_up, t_bup, 0, 8, D_FF + cbase * 128, 512)
                    for j in range(4):
                        c = cbase + j
                        i = grp * 4 + j
                        q = proj4(wgt, tkgt, 8, j, xn2, t_xn2)
                        for t in range(NT):
                            ACT(graw[:, 1 + t * TS:1 + (t + 1) * TS], ps[q[t]][:], AF.Copy, r=[pst[q[t]]], w=[t_graw])
                        ACT(gcv, graw[:, 0:S_LEN], AF.Identity, r=[t_graw, t_pv], w=[t_gcv],
                            scale=col(pv, PV_FCW + c * 3 + 0), bias=col(pv, PV_FCB + c))
                        for jj in (1, 2):
                            STT("dve", gcv, graw[:, jj:jj + S_LEN], col(pv, PV_FCW + c * 3 + jj), gcv, ALU.mult, ALU.add,
                                r=[t_graw, t_gcv, t_pv], w=[t_gcv])
                        ACT(gcv, gcv, AF.Gelu_apprx_tanh, r=[t_gcv], w=[t_gcv])
                        q2 = proj4(wvl, tkvl, 8, j, xn2, t_xn2)
                        for t in range(NT):
                            TT("dve", hh[:, i, tl(t)], ps[q2[t]][:], gcv[:, tl(t)], ALU.mult, r=[pst[q2[t]], t_gcv],
                               w=[t_hh[i][t]] + ([t_ma[i][t]] if i < 8 else []))
                if s == 0 and half == 0:
                    dump("hh", hh, [t_hh[c][t] for c in range(12) for t in range(NT)], [128, 12, S_LEN], BF16)
                for oc2 in range(4):
                    wd_, tkd_ = load_w(b_down, t_bdown, half * 12, 12, oc2 * 256, 256)
                    for j in range(2):
                        oc = oc2 * 2 + j
                        q = proj4(wd_, tkd_, 12, j, hh, t_hh)
                        for t in range(NT):
                            TT("dve", x1[:, oc, tl(t)], ps[q[t]][:], x1[:, oc, tl(t)], ALU.add,
                               r=[pst[q[t]], t_x1[oc][t]], w=[t_x1[oc][t]])
                        if half == 1:
                            DMA("act", "o%d_%d" % (s % 2, oc), yT[s, oc * 128:(oc + 1) * 128, :], x1[:, oc, :], r=t_x1[oc], w=[Tk()])

        for s in range(nseq):
            try:
                if upto >= 1:
                    seq(s)
            except _Stop:
                pass
        S.barrier()
        fin = ar.t[:, KB_SCR * 256:KB_SCR * 256 + 1]
        MEMSET("dve", fin, 0.0, r=[], w=[Tk()])
        S.emit()
    return nc, list(dbg_outs.keys())


def _consts():
    cbf = np.zeros((128, 2896), np.float32)
    cbf[:, 0:128] = 1.0 / 1024.0
    blk = np.zeros((128, 128), np.float32)
    blk[0:64, 0:64] = 1.0 / 64.0
    blk[64:128, 64:128] = 1.0 / 64.0
    cbf[:, 128:256] = blk
    perm = np.zeros((128, 128), np.float32)
    for m in range(128):
        dl = m % 64
        if dl < 8:
            perm[m + 8, m] = 1.0
        elif dl < 16:
            perm[m - 8, m] = 1.0
    cbf[:, 256:384] = perm
    cbf[:, 384:448] = 1.0
    cbf[:, 2496:2496 + 64] = 1.0
    cbf[:, 2624 + 64:2624 + 128] = 1.0
    cbf[:, 2752:2880] = np.eye(128, dtype=np.float32)
    cbf[0:64, 2880] = 1.0 / 64.0
    cbf[64:128, 2881] = 1.0 / 64.0
    kk = np.arange(128)[:, None]
    qq = np.arange(128)[None, :]
    A_gen = (kk >= qq)
    B_gen = (kk <= qq)
    A_first = (kk >= np.maximum(qq, 64))
    B_last = (kk <= np.minimum(qq, 63))
    combos = [(A_gen, B_gen), (A_first, B_gen), (A_gen, B_last), (A_first, B_last)]
    for i, (a, b) in enumerate(combos):
        tile = np.concatenate([a, b, a, b], axis=1).astype(np.float32)
        cbf[:, 448 + i * 512:448 + (i + 1) * 512] = tile
    pos = np.arange(S_LEN, dtype=np.float32)
    inv = (np.float32(500000.0) ** (-np.arange(0, 16, 2, dtype=np.float32) / np.float32(16))).astype(np.float32)
    ang = pos[:, None] * inv[None, :]
    cos = np.cos(ang).astype(np.float32)
    sin = np.sin(ang).astype(np.float32)
    cs = np.zeros((2, 128, S_LEN), np.float32)
    cs[0] = 1.0
    for p in range(128):
        dl = p % 64
        if dl < 8:
            cs[0, p] = cos[:, dl]
            cs[1, p] = -sin[:, dl]
        elif dl < 16:
            cs[0, p] = cos[:, dl - 8]
            cs[1, p] = sin[:, dl - 8]
    return cbf, cs


def _pvec(inp):
    pv = np.zeros((128, 240), np.float32)

    def pm(v, n):
        return np.ascontiguousarray(np.asarray(v, np.float32).reshape(n, 128).T)

    pv[:, 0:8] = pm(inp["norm1_g"][0], 8)
    pv[:, 8:16] = pm(inp["norm2_g"][0], 8)
    lcw = np.asarray(inp["lru_conv_w"][0], np.float32)
    pv[:, 16:56] = np.ascontiguousarray(lcw.reshape(4, 10, 128).transpose(2, 1, 0)).reshape(128, 40)
    pv[:, 56:66] = pm(inp["lru_conv_b"][0], 10)
    for (off, name) in ((66, "lru_lambda"), (86, "lru_ba"), (106, "lru_bx")):
        v = np.asarray(inp[name][0], np.float32)
        pv[:, off:off + 20] = np.ascontiguousarray(v.reshape(2, 10, 128).transpose(2, 0, 1)).reshape(128, 20)
    fcw = np.asarray(inp["ffn_conv_w"][0], np.float32)
    pv[:, 126:198] = np.ascontiguousarray(fcw.reshape(3, 24, 128).transpose(2, 1, 0)).reshape(128, 72)
    pv[:, 198:222] = pm(inp["ffn_conv_b"][0], 24)
    src = np.arange(64)
    src[0:8] = np.arange(8, 16)
    src[8:16] = np.arange(0, 8)
    for which, name in enumerate(("q_norm_g", "k_norm_g")):
        gq = np.asarray(inp[name][0], np.float32)
        for g in range(3):
            pv[:, 222 + which * 3 + g] = np.tile(gq[g], 2)
            pv[:, 228 + which * 3 + g] = np.tile(gq[g][src], 2)
    pv[:, 234] = EPS
    pv[:, 235] = 1.0
    return pv


_PROG = {}


def kernel(**inp):
    xp = np.asarray(inp["x_prompt"], np.float32)
    xs = np.asarray(inp["x_sample"], np.float32)
    nseq = 5
    if nseq not in _PROG:
        _PROG[nseq] = build_program(nseq)[0]
    nc = _PROG[nseq]
    cbf, cs = _consts()
    pv = _pvec(inp)
    lru_w = np.ascontiguousarray(np.stack([inp["lru_wa"][0][0], inp["lru_wx"][0][0], inp["lru_wa"][0][1], inp["lru_wx"][0][1]],
                                          axis=0).astype(np.float32))
    shared = {
        "w_in": np.ascontiguousarray(inp["w_in"][0], np.float32),
        "w_up": np.ascontiguousarray(inp["w_up"][0], np.float32),
        "w_down": np.ascontiguousarray(inp["w_down"][0], np.float32),
        "w_o": np.ascontiguousarray(inp["w_o"][0], np.float32),
        "w_lru_out": np.ascontiguousarray(inp["w_lru_out"][0], np.float32),
        "w_att_out": np.ascontiguousarray(inp["w_att_out"][0], np.float32),
        "lru_w": lru_w, "pvec": pv, "cbf": cbf, "cs": cs,
    }
    in_maps = []
    for i in range(NCORES):
        seqs = [xp[4 * i + j] for j in range(4)] + [xs[i]]
        xT = np.ascontiguousarray(np.stack([q.T for q in seqs], axis=0))
        m = dict(shared)
        m["xT"] = xT
        in_maps.append(m)
    res = run_bass_kernel_spmd(nc, in_maps, core_ids=list(range(NCORES)))
    yp = np.empty_like(xp)
    ys = np.empty_like(xs)
    for i in range(NCORES):
        yT = np.asarray(res.results[i]["yT"])
        for j in range(4):
            yp[4 * i + j] = yT[j].T
        ys[i] = yT[4].T
    return (yp, ys)
```

```python
import contextlib
import os
import numpy as np
import concourse.bass as bass
import concourse.mybir as mybir
from concourse.bass_utils import run_bass_kernel_spmd

F32 = mybir.dt.float32
BF16 = mybir.dt.bfloat16
AF = mybir.ActivationFunctionType
ALU = mybir.AluOpType

ENGS = ("pe", "act", "dve", "pool", "sp")

D = 1024
S_LEN = 2048
D_RNN = 1280
ATT_COLS = 1536
D_FF = 3072
IN_COLS = 9216
C_Q = 2560
C_K = C_Q + ATT_COLS
C_V = C_K + ATT_COLS
C_G = C_V + ATT_COLS
NT = 4
TS = 512
DIL = (1, 4, 16)
EPS = 1e-6
NCORES = 8


class Tk:
    __slots__ = ("w", "rd")

    def __init__(self):
        self.w = None
        self.rd = []


class Op:
    __slots__ = ("eng", "fn", "deps", "need_inc", "val", "dsem", "waits", "cidx")

    def __init__(self, eng, fn):
        self.eng = eng
        self.fn = fn
        self.deps = []
        self.need_inc = False
        self.val = 0
        self.dsem = None
        self.waits = []
        self.cidx = -1


class Sched:
    def __init__(self, nc):
        self.nc = nc
        self.prog = {e: [] for e in ENGS}
        self.dma_keys = {}
        self.pending_bar = {e: None for e in ENGS}

    def op(self, eng, fn, r=(), w=(), dma=None, nobar=False):
        o = Op(eng, fn)
        deps = []
        for t in r:
            if t.w is not None:
                deps.append(t.w)
        for t in w:
            if t.w is not None:
                deps.append(t.w)
            deps.extend(t.rd)
        for t in r:
            t.rd.append(o)
        for t in w:
            t.w = o
            t.rd = []
        if self.pending_bar[eng] is not None and not nobar:
            deps.extend(self.pending_bar[eng])
            self.pending_bar[eng] = None
        seen = set()
        for d in deps:
            if id(d) not in seen and d is not o:
                seen.add(id(d))
                o.deps.append(d)
        self.prog[eng].append(o)
        if dma is not None:
            o.dsem = dma
            self.dma_keys.setdefault(dma, []).append(o)
        return o

    def dma_multi(self, eng, key, fns, r=(), w=(), nobar=False):
        first = self.op(eng, fns[0], r=r, w=w, dma=key, nobar=nobar)
        last = first
        for fn in fns[1:]:
            last = self.op(eng, fn, dma=key, nobar=nobar)
        if last is not first:
            for t in w:
                t.w = last
            for t in r:
                t.rd.append(last)
        return last

    def barrier(self, skip_out=False):
        lasts = []
        for e in ENGS:
            for o in reversed(self.prog[e]):
                if o.dsem is None:
                    lasts.append(o)
                    break
        for k, ops in self.dma_keys.items():
            if str(k).startswith("p_") or (skip_out and str(k).startswith("o")):
                continue
            lasts.append(ops[-1])
        for e in ENGS:
            cur = self.pending_bar[e] or []
            self.pending_bar[e] = cur + lasts

    def finalize(self):
        def chan(o):
            return o.dsem if o.dsem is not None else o.eng

        for k, ops in self.dma_keys.items():
            for i, o in enumerate(ops):
                o.cidx = i
        for e in ENGS:
            i = 0
            for o in self.prog[e]:
                if o.dsem is None:
                    o.cidx = i
                    i += 1
        for e in ENGS:
            kn = {}
            for o in self.prog[e]:
                best = {}
                for d in o.deps:
                    c = chan(d)
                    if d.dsem is None and d.eng == e and e in ("pe", "sp"):
                        continue
                    if kn.get(c, -1) >= d.cidx:
                        continue
                    if c not in best or best[c].cidx < d.cidx:
                        best[c] = d
                for c, d in best.items():
                    kn[c] = d.cidx
                    d.need_inc = True
                    o.waits.append(d)
        for e in ENGS:
            v = 0
            for o in self.prog[e]:
                if o.dsem is None and o.need_inc:
                    v += 1
                    o.val = v
        for k, ops in self.dma_keys.items():
            for i, o in enumerate(ops):
                o.val = 16 * (i + 1)

    def emit(self):
        nc = self.nc
        self.finalize()
        with contextlib.ExitStack() as st:
            esem = {e: st.enter_context(nc.semaphore("s_" + e)) for e in ENGS}
            dsem = {k: st.enter_context(nc.semaphore("d_%s" % (str(k),))) for k in self.dma_keys}
            block = st.enter_context(nc.Block())

            def semof(o):
                return dsem[o.dsem] if o.dsem is not None else esem[o.eng]

            def run(e):
                def body(eng):
                    for o in self.prog[e]:
                        for d in o.waits:
                            eng.wait_ge(semof(d), d.val)
                        inst = o.fn(eng)
                        if o.dsem is not None:
                            inst.then_inc(dsem[o.dsem], 16)
                        elif o.need_inc:
                            inst.then_inc(esem[e], 1)
                return body

            block.tensor(run("pe"))
            block.scalar(run("act"))
            block.vector(run("dve"))
            block.gpsimd(run("pool"))
            block.sync(run("sp"))


class Arena:
    def __init__(self, nc, st, nbytes):
        self.n = nbytes
        self.t = st.enter_context(nc.sbuf_tensor("arena", [128, nbytes // 4], F32))
        self.off = 0

    def seek(self, kb):
        self.off = int(kb * 1024)

    def alloc(self, shape_free, dtype):
        esz = 2 if dtype == BF16 else 4
        n = int(np.prod(shape_free))
        nb = (n * esz + 31) // 32 * 32
        assert self.off + nb <= self.n, ("arena overflow", self.off, nb, self.n)
        a = self.t[:, self.off // 4:(self.off + nb) // 4]
        self.off += nb
        if dtype != F32:
            a = a.bitcast(dtype)
        a = a[:, 0:n]
        if len(shape_free) == 2:
            a = a.rearrange("p (a b) -> p a b", a=shape_free[0])
        elif len(shape_free) == 3:
            a = a.rearrange("p (a b c) -> p a b c", a=shape_free[0], b=shape_free[1])
        return a


def gate_nbrs(c):
    lo_b = (c * 128) // 80
    hi_b = (c * 128 + 127) // 80
    r0 = lo_b * 80
    r1 = hi_b * 80 + 79
    return list(range(r0 // 128, r1 // 128 + 1))


class _Stop(Exception):
    pass


def build_program(nseq, dbg=False, upto=99):
    nc = bass.Bass("TRN2", target_bir_lowering=False)
    S = Sched(nc)

    def MM(out, lhsT, rhs, start, stop, r, w):
        S.op("pe", lambda e: e.matmul(out, lhsT=lhsT, rhs=rhs, start=start, stop=stop), r=r, w=w)

    def TRN(out, in_, ident, r, w):
        S.op("pe", lambda e: e.transpose(out, in_, ident), r=r, w=w)

    def ACT(out, in_, func, r, w, scale=None, bias=None):
        kw = {}
        if scale is not None:
            kw["scale"] = scale
        if bias is not None:
            kw["bias"] = bias
        S.op("act", lambda e: e.activation(out=out, in_=in_, func=func, **kw), r=r, w=w)

    def STT(eng, out, in0, scalar, in1, op0, op1, r, w):
        S.op(eng, lambda e: e.scalar_tensor_tensor(out=out, in0=in0, scalar=scalar, in1=in1, op0=op0, op1=op1), r=r, w=w)

    def TT(eng, out, in0, in1, op, r, w):
        S.op(eng, lambda e: e.tensor_tensor(out=out, in0=in0, in1=in1, op=op), r=r, w=w)

    def TSC(eng, out, in0, scalar1, op0, r, w):
        S.op(eng, lambda e: e.tensor_scalar(out=out, in0=in0, scalar1=scalar1, scalar2=None, op0=op0), r=r, w=w)

    def RECIP(out, in_, r, w):
        S.op("dve", lambda e: e.reciprocal(out=out, in_=in_), r=r, w=w)

    def MEMSET(eng, ap, val, r, w):
        S.op(eng, lambda e: e.memset(ap, val), r=r, w=w)

    def COPY(eng, out, in_, r, w):
        S.op(eng, lambda e: e.tensor_copy(out=out, in_=in_), r=r, w=w)

    def SCAN(out, d0, d1, r, w):
        S.op("dve", lambda e: e.tensor_tensor_scan(out=out, data0=d0, data1=d1, initial=0.0, op0=ALU.mult, op1=ALU.add),
             r=r, w=w)

    def DMA(eng, key, out, in_, r=(), w=()):
        return S.op(eng, lambda e: e.dma_start(out=out, in_=in_), r=r, w=w, dma=key)

    def dma_fn(out, in_):
        return lambda e: e.dma_start(out=out, in_=in_)

    def din(name, shape, dt=F32):
        return nc.dram_tensor(name, list(shape), dt, kind="ExternalInput").ap()

    xT = din("xT", [nseq, D, S_LEN])
    w_in = din("w_in", [D, IN_COLS])
    w_up = din("w_up", [D, 2 * D_FF])
    w_down = din("w_down", [D_FF, D])
    w_o = din("w_o", [D, D])
    w_lo = din("w_lru_out", [D_RNN, D])
    w_ao = din("w_att_out", [512, D])
    lru_w = din("lru_w", [4, 16, 80, 80])
    pvec = din("pvec", [128, 240])
    cbf = din("cbf", [128, 2896])
    csd = din("cs", [2, 128, S_LEN])
    yT = nc.dram_tensor("yT", [nseq, D, S_LEN], F32, kind="ExternalOutput").ap()

    def dscr(name, shape):
        return nc.dram_tensor(name, list(shape), BF16, kind="Internal").ap()

    b_in = dscr("b_in", [D, IN_COLS])
    b_up = dscr("b_up", [D, 2 * D_FF])
    b_down = dscr("b_down", [D_FF, D])
    b_o = dscr("b_o", [D, D])
    b_lo = dscr("b_lo", [D_RNN, D])
    b_ao = dscr("b_ao", [512, D])
    b_g = dscr("b_g", [4, D_RNN, D_RNN])
    dbg_outs = {}

    st = contextlib.ExitStack()
    with st:
        ar = Arena(nc, st, 200 * 1024)
        ps = [st.enter_context(nc.psum_tensor("ps%d" % i, [128, 512], F32)) for i in range(8)]
        pst = [Tk() for _ in range(8)]
        pstate = {"b": 0, "q": 0}

        def bank(excl=()):
            b = pstate["b"]
            while b in excl:
                b = (b + 1) % 8
            pstate["b"] = (b + 1) % 8
            return b

        def quad():
            q = pstate["q"]
            pstate["q"] = 1 - q
            return [4 * q + i for i in range(4)]

        ar.seek(0)
        pv = ar.alloc([240], F32)
        t_pv = Tk()
        der = ar.alloc([96], F32)
        t_der = Tk()
        cb = ar.alloc([2896], BF16)
        t_cb = Tk()
        ring = [(ar.alloc([4096], BF16), Tk()) for _ in range(3)]
        rstate = {"i": 0}
        P_END = ar.off
        assert P_END <= 31 * 1024, P_END

        PV_G1, PV_G2 = 0, 8
        PV_LCW, PV_LCB = 16, 56
        PV_LAM, PV_BA, PV_BX = 66, 86, 106
        PV_FCW, PV_FCB = 126, 198
        PV_QG, PV_QGP = 222, 228
        PV_EPS, PV_ONE = 234, 235
        DR_HC, DR_C2, DR_HBA, DR_HBX = 0, 20, 40, 60
        CB_ONES, CB_BLK, CB_PERM, CB_O64, CB_MASK, CB_L0, CB_L1, CB_ID, CB_IND = 0, 128, 256, 384, 448, 2496, 2624, 2752, 2880

        def col(a, i):
            return a[:, i:i + 1]

        def tl(t):
            return slice(t * TS, (t + 1) * TS)

        t_bin, t_bup, t_bdown, t_bo, t_blo, t_bao, t_bg = (Tk() for _ in range(7))

        def prep_cast(key, dst, src, rows, tk, nsplit):
            step = rows // nsplit
            fns = [dma_fn(dst[i * step:(i + 1) * step, :], src[i * step:(i + 1) * step, :]) for i in range(nsplit)]
            S.dma_multi("pool", key, fns, w=[tk])

        DMA("sp", "c_pv", pv, pvec, w=[t_pv])
        DMA("pool", "c_cb", cb, cbf, w=[t_cb])
        t_bin_grp = [Tk(), Tk(), Tk()]
        BIN_COLS = [(0, 2560), (2560, C_G), (C_G, IN_COLS)]
        for gi_, (c0_, c1_) in enumerate(BIN_COLS[:1]):
            fns_ = [dma_fn(b_in[i * 256:(i + 1) * 256, c0_:c1_], w_in[i * 256:(i + 1) * 256, c0_:c1_]) for i in range(4)]
            S.dma_multi("pool", "p_in%d" % gi_, fns_, w=[t_bin_grp[gi_]])
        zt = ring[0][0]
        MEMSET("dve", zt[:, 0:D_RNN], 0.0, r=[], w=[ring[0][1]])
        fns = [dma_fn(b_g[dg, k * 128:(k + 1) * 128, :], zt[:, 0:D_RNN]) for dg in range(4) for k in range(10)]
        t_bgz = Tk()
        S.dma_multi("sp", "p_gz", fns, r=[ring[0][1]], w=[t_bgz])
        fns = [dma_fn(b_g[dg, b * 80:(b + 1) * 80, b * 80:(b + 1) * 80], lru_w[dg, b]) for dg in range(4) for b in range(16)]
        S.dma_multi("pool", "p_g", fns, r=[t_bgz], w=[t_bg])
        for gi_, (c0_, c1_) in list(enumerate(BIN_COLS))[1:]:
            fns_ = [dma_fn(b_in[i * 256:(i + 1) * 256, c0_:c1_], w_in[i * 256:(i + 1) * 256, c0_:c1_]) for i in range(4)]
            S.dma_multi("pool", "p_in%d" % gi_, fns_, w=[t_bin_grp[gi_]])
        prep_cast("p_lo", b_lo, w_lo, D_RNN, t_blo, 2)
        prep_cast("p_ao", b_ao, w_ao, 512, t_bao, 1)
        prep_cast("p_o", b_o, w_o, D, t_bo, 2)
        prep_cast("p_up", b_up, w_up, D, t_bup, 8)
        prep_cast("p_down", b_down, w_down, D_FF, t_bdown, 4)

        ACT(der[:, 0:20], pv[:, PV_LAM:PV_LAM + 20], AF.Exp, r=[t_pv], w=[t_der], scale=-1.0)
        ACT(der[:, 0:20], der[:, 0:20], AF.Ln, r=[t_pv, t_der], w=[t_der], bias=col(pv, PV_ONE))
        TSC("dve", der[:, DR_C2:DR_C2 + 20], der[:, 0:20], -8.0, ALU.mult, r=[t_der], w=[t_der])
        TSC("dve", der[:, DR_HC:DR_HC + 20], der[:, 0:20], -4.0, ALU.mult, r=[t_der], w=[t_der])
        TSC("dve", der[:, DR_HBA:DR_HBA + 40], pv[:, PV_BA:PV_BA + 40], 0.5, ALU.mult, r=[t_pv, t_der], w=[t_der])

        def load_w(src, t_src, k0, nk, c0, ncols):
            if t_src is t_bin:
                t_src = t_bin_grp[0] if c0 < 2560 else (t_bin_grp[1] if c0 < C_G else t_bin_grp[2])
            slot = rstate["i"] % 3
            rstate["i"] += 1
            buf, tk = ring[slot]
            view = buf[:, 0:nk * ncols].rearrange("p (k c) -> p k c", k=nk)
            srcv = src[k0 * 128:(k0 + nk) * 128, c0:c0 + ncols].rearrange("(k p) c -> p k c", p=128)
            S.op("sp", dma_fn(view, srcv), r=[t_src], w=[tk], dma="w%d" % slot, nobar=True)
            return view, tk

        def load_gate_w(c):
            nb = gate_nbrs(c)
            nk = len(nb)
            slot = rstate["i"] % 3
            rstate["i"] += 1
            buf, tk = ring[slot]
            view = buf[:, 0:4 * nk * 128].rearrange("p (g k c) -> p g k c", g=4, k=nk)
            fns = []
            for dg in range(4):
                srcv = b_g[dg, nb[0] * 128:(nb[0] + nk) * 128, c * 128:(c + 1) * 128].rearrange("(k p) c -> p k c", p=128)
                fns.append(dma_fn(view[:, dg], srcv))
            S.dma_multi("sp", "w%d" % slot, fns, r=[t_bg], w=[tk], nobar=True)
            return view, tk, nb

        def proj4(wt, t_w, nk, j, src, src_tks, q=None, t_outer=True):
            if q is None:
                q = quad()
            order = [(k, t) for t in range(NT) for k in range(nk)] if t_outer else [(k, t) for k in range(nk) for t in range(NT)]
            for (k, t) in order:
                MM(ps[q[t]][:], wt[:, k, j * 128:(j + 1) * 128], src[:, k, tl(t)], k == 0, k == nk - 1,
                   r=[t_w, src_tks[k][t]], w=[pst[q[t]]])
            return q

        dbgc = {"i": 0}

        def dump(name, ap, tks, shape, dt=F32):
            if not dbg:
                return
            o = nc.dram_tensor("dbg_" + name, list(shape), dt, kind="ExternalOutput").ap()
            dbg_outs[name] = 1
            DMA("sp", "dbg%d" % dbgc["i"], o, ap, r=tks, w=[Tk()])
            dbgc["i"] += 1

        KB_XN = 31
        KB_MA = 63
        KB_SCR = 95
        ar.seek(KB_XN)
        xn = ar.alloc([8, S_LEN], BF16)
        ar.seek(KB_MA)
        ma = ar.alloc([8, S_LEN], BF16)
        ar.seek(136)
        x1 = ar.alloc([8, S_LEN], F32)

        def seq(s):
            t_xn = [[Tk() for _ in range(NT)] for _ in range(8)]
            t_ma = [[Tk() for _ in range(NT)] for _ in range(8)]
            t_x1 = [[Tk() for _ in range(NT)] for _ in range(8)]
            S.barrier(skip_out=True)
            ar.seek(KB_SCR)
            xt = [ar.alloc([8, TS], F32) for _ in range(2)]
            t_xt = [Tk(), Tk()]
            sq = ar.alloc([8, TS], BF16)
            t_sq = Tk()
            assert ar.off <= 136 * 1024
            ar.seek(KB_MA)
            rs = ar.alloc([TS], F32)
            t_rs = Tk()
            for t in range(NT):
                b = t % 2
                DMA("sp", "x%d" % b, xt[b], xT[s, :, tl(t)].rearrange("(k p) n -> p k n", p=128), w=[t_xt[b]])
                ACT(sq, xt[b], AF.Square, r=[t_xt[b]], w=[t_sq])
                pb = bank()
                for k in range(8):
                    MM(ps[pb][:], cb[:, CB_ONES:CB_ONES + 128], sq[:, k, :], k == 0, k == 7, r=[t_sq, t_cb], w=[pst[pb]])
                ACT(rs, ps[pb][:], AF.Sqrt, r=[pst[pb], t_pv], w=[t_rs], bias=col(pv, PV_EPS))
                RECIP(rs, rs, r=[t_rs], w=[t_rs])
                for k in range(8):
                    STT("dve", xn[:, k, tl(t)], xt[b][:, k, :], col(pv, PV_G1 + k), rs, ALU.mult, ALU.mult,
                        r=[t_xt[b], t_rs, t_pv], w=[t_xn[k][t]])
            if s == 0:
                dump("xn", xn, [t_xn[k][t] for k in range(8) for t in range(NT)], [128, 8, S_LEN], BF16)

            if upto <= 1:
                raise _Stop()
            S.barrier()
            ar.seek(KB_MA)
            Abuf = ar.alloc([2, S_LEN], F32)
            S2 = ar.alloc([2, S_LEN], F32)
            ar.seek(KB_SCR)
            y = ar.alloc([10, S_LEN], BF16)
            xcb = ar.alloc([5, S_LEN], BF16)
            lx = ar.alloc([S_LEN + 8], F32)
            U = ar.alloc([2, S_LEN], F32)
            gg = ar.alloc([S_LEN], F32)
            TR = [ar.alloc([TS], F32) for _ in range(2)]
            TI = [ar.alloc([TS], F32) for _ in range(2)]
            t_y = [[Tk() for _ in range(NT)] for _ in range(10)]
            t_lx, t_gg = Tk(), Tk()
            t_A = [[Tk() for _ in range(NT)] for _ in range(2)]
            t_S2 = [[Tk() for _ in range(NT)] for _ in range(2)]
            t_U = [[Tk() for _ in range(NT)] for _ in range(2)]
            A_all = t_A[0] + t_A[1]
            S2_all = t_S2[0] + t_S2[1]
            U_all = t_U[0] + t_U[1]
            t_TR = [Tk(), Tk()]
            t_TI = [Tk(), Tk()]
            MEMSET("dve", lx, 0.0, r=[], w=[t_lx])
            tcnt = 0
            t_xcb = [[Tk() for _ in range(NT)] for _ in range(5)]
            for sb in range(2):
                wt_a, tk_a = load_w(b_in, t_bin, 0, 8, sb * 640, 512)
                wt_b, tk_b = load_w(b_in, t_bin, 0, 8, sb * 640 + 512, 128)
                for cl in range(5):
                    c = sb * 5 + cl
                    wt, tkw, j = (wt_a, tk_a, cl) if cl < 4 else (wt_b, tk_b, 0)
                    q = proj4(wt, tkw, 8, j, xn, t_xn)
                    for t in range(NT):
                        ACT(lx[:, 2 + t * TS:2 + (t + 1) * TS], ps[q[t]][:], AF.Copy, r=[pst[q[t]]], w=[t_lx])
                    tmp = U[:, 0, :]
                    ACT(tmp, lx[:, 0:S_LEN], AF.Identity, r=[t_lx, t_pv], w=t_U[0],
                        scale=col(pv, PV_LCW + c * 4 + 0), bias=col(pv, PV_LCB + c))
                    for jj in (1, 2):
                        STT("dve", tmp, lx[:, jj:jj + S_LEN], col(pv, PV_LCW + c * 4 + jj), tmp, ALU.mult, ALU.add,
                            r=[t_lx, t_pv] + t_U[0], w=t_U[0])
                    STT("dve", xcb[:, cl, :], lx[:, 3:3 + S_LEN], col(pv, PV_LCW + c * 4 + 3), tmp, ALU.mult, ALU.add,
                        r=[t_lx, t_pv] + t_U[0], w=t_xcb[cl])
                if s == 0 and sb == 0:
                    dump("xcb", xcb, [t_xcb[c][t] for c in range(5) for t in range(NT)], [128, 5, S_LEN], BF16)
                if upto <= 2:
                    raise _Stop()
                for cl in range(5):
                    c = sb * 5 + cl
                    gw, tkg, nb = load_gate_w(c)
                    nk = len(nb)
                    for dr in range(2):
                        idx = dr * 10 + c
                        for t in range(NT):
                            pr, pi = bank(), bank()
                            for (pb, gi) in ((pr, 0), (pi, 1)):
                                for ki in range(nk):
                                    icl = nb[ki] - sb * 5
                                    MM(ps[pb][:], gw[:, dr * 2 + gi, ki, :], xcb[:, icl, tl(t)], ki == 0, ki == nk - 1,
                                       r=[tkg, t_xcb[icl][t]], w=[pst[pb]])
                            i2 = tcnt % 2
                            tcnt += 1
                            ACT(TR[i2], ps[pr][:], AF.Tanh, r=[pst[pr], t_der], w=[t_TR[i2]], scale=0.5, bias=col(der, DR_HBA + idx))
                            ACT(Abuf[:, dr, tl(t)], TR[i2], AF.Exp, r=[t_TR[i2], t_der], w=[t_A[dr][t]],
                                scale=col(der, DR_HC + idx), bias=col(der, DR_HC + idx))
                            TT("pool", S2[:, dr, tl(t)], Abuf[:, dr, tl(t)], Abuf[:, dr, tl(t)], ALU.mult,
                               r=[t_A[dr][t]], w=[t_S2[dr][t]])
                            ACT(U[:, dr, tl(t)], ps[pi][:], AF.Tanh, r=[pst[pi], t_der], w=[t_U[dr][t]], scale=0.5,
                                bias=col(der, DR_HBX + idx))
                            STT("dve", U[:, dr, tl(t)], U[:, dr, tl(t)], 1.0, xcb[:, cl, tl(t)], ALU.add, ALU.mult,
                                r=[t_U[dr][t], t_xcb[cl][t]], w=[t_U[dr][t]])
                    ACT(S2, S2, AF.Sqrt, r=[t_pv] + S2_all, w=S2_all, scale=-1.0, bias=col(pv, PV_ONE))
                    STT("dve", U, U, 0.5, S2, ALU.mult, ALU.mult, r=U_all + S2_all, w=U_all)
                    SCAN(S2[:, 0, :], Abuf[:, 0, :], U[:, 0, :], r=t_A[0] + t_U[0] + t_S2[0], w=t_S2[0])
                    SCAN(S2[:, 1, ::-1], Abuf[:, 1, ::-1], U[:, 1, ::-1], r=t_A[1] + t_U[1] + t_S2[1], w=t_S2[1])
                    wtg, tkwg = load_w(b_in, t_bin, 0, 8, D_RNN + c * 128, 128)
                    q = proj4(wtg, tkwg, 8, 0, xn, t_xn)
                    for t in range(NT):
                        ACT(gg[:, tl(t)], ps[q[t]][:], AF.Gelu_apprx_tanh, r=[pst[q[t]]], w=[t_gg])
                    TT("dve", lx[:, 2:2 + S_LEN], S2[:, 0, :], S2[:, 1, :], ALU.add, r=S2_all, w=[t_lx])
                    TT("dve", y[:, c, :], lx[:, 2:2 + S_LEN], gg, ALU.mult, r=[t_lx, t_gg], w=t_y[c])
            if s == 0:
                dump("y", y, [t_y[c][t] for c in range(10) for t in range(NT)], [128, 10, S_LEN], BF16)
            if upto <= 3:
                raise _Stop()
            S.barrier()
            TG = U[:, 0, :]
            t_TG = t_U[0][0]
            for oc2 in range(4):
                wl, tkl = load_w(b_lo, t_blo, 0, 10, oc2 * 256, 256)
                wg_, tkg_ = load_w(b_in, t_bin, 0, 8, C_G + oc2 * 256, 256)
                for j in range(2):
                    oc = oc2 * 2 + j
                    qg_ = proj4(wg_, tkg_, 8, j, xn, t_xn)
                    for t in range(NT):
                        ACT(TG[:, tl(t)], ps[qg_[t]][:], AF.Tanh, r=[pst[qg_[t]]], w=[t_TG], scale=0.5)
                    qa = proj4(wl, tkl, 10, j, y, t_y)
                    for t in range(NT):
                        STT("dve", ma[:, oc, tl(t)], TG[:, tl(t)], 1.0, ps[qa[t]][:], ALU.add, ALU.mult,
                            r=[t_TG, pst[qa[t]]], w=[t_ma[oc][t]])
            if s == 0:
                dump("ma", ma, [t_ma[c][t] for c in range(8) for t in range(NT)], [128, 8, S_LEN], BF16)

            if upto <= 4.05:
                raise _Stop()
            S.barrier()
            ar.seek(KB_SCR)
            cosT = ar.alloc([S_LEN], F32)
            sinT = ar.alloc([S_LEN], F32)
            accN = ar.alloc([S_LEN], F32)
            accD = ar.alloc([S_LEN], F32)
            QT = ar.alloc([S_LEN + 128], BF16)
            KTh = [ar.alloc([S_LEN + 128], BF16) for _ in range(2)]
            Vth = [ar.alloc([32, 128], BF16) for _ in range(2)]
            VT = ar.alloc([S_LEN + 128], BF16)
            t_VT = Tk()
            obf = ar.alloc([4, S_LEN], BF16)
            sqb = [ar.alloc([TS], BF16) for _ in range(2)]
            qrb = [ar.alloc([TS], BF16) for _ in range(2)]
            srp = [ar.alloc([8], F32) for _ in range(2)]
            Bc = [ar.alloc([8, 64], BF16) for _ in range(2)]
            t1b = [ar.alloc([TS], F32) for _ in range(4)]
            t2b = [ar.alloc([TS], F32) for _ in range(2)]
            Pb = [ar.alloc([TS], BF16) for _ in range(4)]
            assert ar.off <= 200 * 1024
            t_cs, t_accN, t_accD, t_QT, t_KT, t_V = Tk(), Tk(), Tk(), Tk(), Tk(), Tk()
            t_obf = [[Tk() for _ in range(NT)] for _ in range(4)]
            t_sqb, t_qrb, t_rst, t_t2, t_Bc = ([Tk(), Tk()] for _ in range(5))
            t_t1 = [Tk() for _ in range(4)]
            t_P = [Tk(), Tk(), Tk(), Tk()]
            t_Vt = [Tk() for _ in range(32)]
            S.dma_multi("act", "cs", [dma_fn(cosT, csd[0]), dma_fn(sinT, csd[1])], w=[t_cs])
            MEMSET("pool", QT, 0.0, r=[], w=[t_QT])
            MEMSET("pool", VT, 0.0, r=[], w=[t_VT])
            for h_ in range(2):
                MEMSET("pool", KTh[h_], 0.0, r=[], w=[t_KT])
                MEMSET("pool", Vth[h_], 0.0, r=[], w=t_Vt)
            ncnt = [0]
            pcnt = [0]
            bB = [0]
            bA = [0]

            def bankB():
                v = 4 + bB[0] % 4
                bB[0] += 1
                return v

            def bankA():
                v = bA[0] % 4
                bA[0] += 1
                return v

            QA = [0, 1, 2, 3]

            def combine_piece(hp_, t):
                RECIP(accD[:, tl(t)], accD[:, tl(t)], r=[t_accD], w=[t_accD])
                if t == NT - 1:
                    TT("pool", obf[:, hp_, :], accN, accD, ALU.mult, r=[t_accN, t_accD], w=t_obf[hp_])

            def combine(hp_):
                for t in range(NT):
                    combine_piece(hp_, t)

            for hp in range(4):
                for g in range(3):
                    d = DIL[g]
                    L = S_LEN // d
                    ntl = L // 128 + 1
                    wq, tkq = load_w(b_in, t_bin, 0, 8, C_Q + g * 512 + hp * 128, 128)
                    wk, tkk = load_w(b_in, t_bin, 0, 8, C_K + g * 512 + hp * 128, 128)
                    wv, tkv = load_w(b_in, t_bin, 0, 8, C_V + g * 512 + hp * 128, 128)
                    QB = [4, 5, 6, 7]

                    def v_proj():
                        qv = proj4(wv, tkv, 8, 0, xn, t_xn, q=QB)
                        body = VT[:, 64:64 + S_LEN]
                        for t in range(NT):
                            if d == 1:
                                ov = body[:, tl(t)]
                                iv = ps[qv[t]][:]
                            else:
                                w_ = TS // d
                                ov = body.rearrange("p (m q) -> p m q", m=d)[:, :, t * w_:(t + 1) * w_]
                                iv = ps[qv[t]][:].rearrange("p (j m) -> p m j", m=d)
                            ACT(ov, iv, AF.Copy, r=[pst[qv[t]]], w=[t_VT])

                    def v_tiles():
                        ntiles = d * ntl
                        for g0 in range(0, ntiles, 8):
                            n = min(8, ntiles - g0)
                            pb = bankB()
                            psv = ps[pb][:].bitcast(BF16)
                            for si in range(n):
                                tid = g0 + si
                                c0 = (tid // ntl) * L + 128 * (tid % ntl)
                                TRN(psv[:, si * 128:(si + 1) * 128], VT[:, c0:c0 + 128], cb[:, CB_ID:CB_ID + 128],
                                    r=[t_VT, t_cb], w=[pst[pb]])
                            for h_ in range(2):
                                ACT(Vth[h_][:, g0:g0 + n, h_ * 64:(h_ + 1) * 64],
                                    psv[:, 0:n * 128].rearrange("p (s c) -> p s c", s=n)[:, :, h_ * 64:(h_ + 1) * 64], AF.Copy,
                                    r=[pst[pb]], w=[t_Vt[tid] for tid in range(g0, g0 + n)])

                    st_ = {}

                    def S0(which, q, t):
                        gi = which * 3 + g
                        i2 = t % 2
                        ACT(sqb[i2], ps[q[t]][:], AF.Square, r=[pst[q[t]]], w=[t_sqb[i2]])
                        ACT(qrb[i2], ps[q[t]][:], AF.Copy, r=[pst[q[t]]], w=[t_qrb[i2]])
                        STT("dve", t1b[t], ps[q[t]][:], col(pv, PV_QG + gi), cosT[:, tl(t)], ALU.mult, ALU.mult,
                            r=[pst[q[t]], t_pv, t_cs, t_qrb[i2]], w=[t_t1[t]])

                    def PEn(which, t):
                        i2 = t % 2
                        pm, pr = bankB(), bankB()
                        st_[(which, t)] = (pm, pr)
                        MM(ps[pr][:], cb[:, CB_PERM:CB_PERM + 128], qrb[i2], True, True, r=[t_cb, t_qrb[i2]], w=[pst[pr]])
                        for bl in range(4):
                            MM(ps[pm][:, 2 * bl:2 * bl + 2], sqb[i2][:, bl * 128:(bl + 1) * 128], cb[:, CB_IND:CB_IND + 2],
                               True, True, r=[t_cb, t_sqb[i2]], w=[pst[pm]])

                    def rest(which, pair):
                        gi = which * 3 + g
                        t_dst = t_QT if which == 0 else t_KT
                        for t in pair:
                            i2 = t % 2
                            pm, pr = st_[(which, t)]
                            STT("dve", t2b[i2], ps[pr][:], col(pv, PV_QGP + gi), sinT[:, tl(t)], ALU.mult, ALU.mult,
                                r=[pst[pr], t_pv, t_cs], w=[t_t2[i2]])
                            ACT(srp[i2], ps[pm][:, 0:8], AF.Sqrt, r=[pst[pm], t_pv], w=[t_rst[i2]], bias=col(pv, PV_EPS))
                        for t in pair:
                            i2 = t % 2
                            RECIP(srp[i2], srp[i2], r=[t_rst[i2]], w=[t_rst[i2]])
                            TT("pool", t1b[t], t1b[t], t2b[i2], ALU.add, r=[t_t1[t], t_t2[i2]], w=[t_t1[t]])
                        for t in pair:
                            i2 = t % 2
                            ACT(Bc[i2], srp[i2].unsqueeze(2).broadcast_to([128, 8, 64]), AF.Copy, r=[t_rst[i2]], w=[t_Bc[i2]])
                        for t in pair:
                            i2 = t % 2
                            pm, pr = st_[(which, t)]
                            pbc = ps[pm][:].bitcast(BF16)[:, 32:32 + TS]
                            for bl in range(4):
                                TRN(pbc[:, bl * 128:(bl + 1) * 128], Bc[i2][:, 2 * bl:2 * bl + 2, :].rearrange("p a b -> p (a b)"),
                                    cb[:, CB_ID:CB_ID + 128], r=[t_Bc[i2], t_cb, pst[pm]], w=[pst[pm]])
                        for t in pair:
                            pm, pr = st_[(which, t)]
                            pbc = ps[pm][:].bitcast(BF16)[:, 32:32 + TS]
                            parts = [(QT, 0, 128)] if which == 0 else [(KTh[0], 0, 64), (KTh[1], 64, 128)]
                            for (dstb, p0, p1) in parts:
                                body = dstb[p0:p1, 64:64 + S_LEN]
                                if d == 1:
                                    ov = body[:, tl(t)]
                                    i0v = t1b[t][p0:p1]
                                    i1v = pbc[p0:p1]
                                else:
                                    w_ = TS // d
                                    ov = body.rearrange("p (m q) -> p m q", m=d)[:, :, t * w_:(t + 1) * w_]
                                    i0v = t1b[t][p0:p1].rearrange("p (j m) -> p m j", m=d)
                                    i1v = pbc[p0:p1].rearrange("p (j m) -> p m j", m=d)
                                TT("dve", ov, i0v, i1v, ALU.mult, r=[t_t1[t], pst[pm]], w=[t_dst])

                    def v_evac(qv):
                        body = VT[:, 64:64 + S_LEN]
                        for t in range(NT):
                            if d == 1:
                                ov = body[:, tl(t)]
                                iv = ps[qv[t]][:]
                            else:
                                w_ = TS // d
                                ov = body.rearrange("p (m q) -> p m q", m=d)[:, :, t * w_:(t + 1) * w_]
                                iv = ps[qv[t]][:].rearrange("p (j m) -> p m j", m=d)
                            ACT(ov, iv, AF.Copy, r=[pst[qv[t]]], w=[t_VT])

                    qq = proj4(wq, tkq, 8, 0, xn, t_xn, q=QA, t_outer=True)
                    S0(0, qq, 0)
                    S0(0, qq, 1)
                    PEn(0, 0)
                    PEn(0, 1)
                    S0(0, qq, 2)
                    S0(0, qq, 3)
                    dfr = (g == 0 and hp > 0)
                    qk_ = proj4(wk, tkk, 8, 0, xn, t_xn, q=QA, t_outer=True)
                    if dfr:
                        combine_piece(hp - 1, 0)
                    rest(0, (0, 1))
                    PEn(0, 2)
                    PEn(0, 3)
                    if dfr:
                        combine_piece(hp - 1, 1)
                    rest(0, (2, 3))
                    if dfr:
                        combine_piece(hp - 1, 2)
                    S0(1, qk_, 0)
                    S0(1, qk_, 1)
                    PEn(1, 0)
                    PEn(1, 1)
                    S0(1, qk_, 2)
                    S0(1, qk_, 3)
                    if dfr:
                        combine_piece(hp - 1, 3)
                    qv_ = proj4(wv, tkv, 8, 0, xn, t_xn, q=QA, t_outer=True)
                    rest(1, (0, 1))
                    v_evac(qv_)
                    v_tiles()
                    PEn(1, 2)
                    PEn(1, 3)
                    rest(1, (2, 3))
                    if s == 0 and hp == 0:
                        dump("QT%d" % g, QT, [t_QT], [128, S_LEN + 128], BF16)
                        dump("KT%d" % g, KTh[0], [t_KT], [128, S_LEN + 128], BF16)
                    blocks = []
                    for qb in range(16):
                        pi0 = qb * 128
                        m = pi0 // L
                        i0 = pi0 % L
                        first = (i0 == 0)
                        last = (i0 == L - 128)
                        mk = (3 if last else 1) if first else (2 if last else 0)
                        tA = m * ntl + i0 // 128
                        blocks.append(dict(pi0=pi0, mk=mk, tA=tA, tB=tA + 1, kA=pi0, kB=pi0 + 128))

                    def s_stage(bk):
                        psb = bankA()
                        for h in range(2):
                            for (seg, kc0) in ((2 * h, bk["kA"]), (2 * h + 1, bk["kB"])):
                                MM(ps[psb][:, seg * 128:(seg + 1) * 128], KTh[h][:, kc0:kc0 + 128],
                                   QT[:, 64 + bk["pi0"]:64 + bk["pi0"] + 128], True, True, r=[t_KT, t_QT], w=[pst[psb]])
                        p3 = pcnt[0] % 4
                        pcnt[0] += 1
                        bk["p3"] = p3
                        ACT(Pb[p3], ps[psb][:], AF.Exp, r=[pst[psb]], w=[t_P[p3]], scale=0.125)
                        mk = bk["mk"]
                        TT("pool" if os.environ.get("K_PMASK") else "dve", Pb[p3], Pb[p3],
                           cb[:, CB_MASK + mk * 512:CB_MASK + (mk + 1) * 512], ALU.mult, r=[t_P[p3], t_cb], w=[t_P[p3]])

                    def pv_stage(bk, pn, pd, qi):
                        p3 = bk["p3"]
                        for (seg, h, tid) in ((0, 0, bk["tA"]), (1, 0, bk["tB"]), (2, 1, bk["tA"]), (3, 1, bk["tB"])):
                            MM(ps[pn][:, qi * 128:(qi + 1) * 128], Vth[h][:, tid, :],
                               Pb[p3][:, seg * 128:(seg + 1) * 128], seg == 0, seg == 3, r=[t_Vt[tid], t_P[p3]], w=[pst[pn]])
                        for (seg, h) in ((0, 0), (1, 0), (2, 1), (3, 1)):
                            cl_ = CB_L0 if h == 0 else CB_L1
                            MM(ps[pd][:, qi * 128:(qi + 1) * 128], cb[:, cl_:cl_ + 128],
                               Pb[p3][:, seg * 128:(seg + 1) * 128], seg == 0, seg == 3, r=[t_cb, t_P[p3]], w=[pst[pd]])

                    s_stage(blocks[0])
                    s_stage(blocks[1])
                    s_stage(blocks[2])
                    for qb in range(16):
                        qb4, qi = qb // 4, qb % 4
                        pn, pd = (4, 5) if qb4 % 2 == 0 else (6, 7)
                        pv_stage(blocks[qb], pn, pd, qi)
                        if qb + 3 < 16:
                            s_stage(blocks[qb + 3])
                        if qi == 3:
                            for (acc, t_acc, pb) in ((accN, t_accN, pn), (accD, t_accD, pd)):
                                if d == 1:
                                    av = acc[:, qb4 * 512:(qb4 + 1) * 512]
                                    pv_ = ps[pb][:]
                                elif d == 4:
                                    av = acc[:, qb4:S_LEN:4]
                                    pv_ = ps[pb][:]
                                else:
                                    av = acc.rearrange("p (j m) -> p m j", m=16)[:, 4 * qb4:4 * qb4 + 4, :]
                                    pv_ = ps[pb][:].rearrange("p (m j) -> p m j", m=4)
                                if g == 0:
                                    COPY("dve", av, pv_, r=[pst[pb]], w=[t_acc])
                                else:
                                    TT("dve", av, pv_, av, ALU.add, r=[pst[pb], t_acc], w=[t_acc])
            combine(3)
            if s == 0:
                dump("obf", obf, [t_obf[c][t] for c in range(4) for t in range(NT)], [128, 4, S_LEN], BF16)
            TG = accN
            t_TG = t_accN
            TB = accD
            t_TB = t_accD
            for oc2 in range(4):
                wa_, tka_ = load_w(b_ao, t_bao, 0, 4, oc2 * 256, 256)
                wg_, tkg_ = load_w(b_in, t_bin, 0, 8, C_G + D + oc2 * 256, 256)
                for j in range(2):
                    oc = oc2 * 2 + j
                    qg_ = proj4(wg_, tkg_, 8, j, xn, t_xn)
                    for t in range(NT):
                        ACT(TG[:, tl(t)], ps[qg_[t]][:], AF.Tanh, r=[pst[qg_[t]]], w=[t_TG], scale=0.5)
                    qa = proj4(wa_, tka_, 4, j, obf, t_obf)
                    for t in range(NT):
                        STT("dve", TB[:, tl(t)], TG[:, tl(t)], 1.0, ps[qa[t]][:], ALU.add, ALU.mult,
                            r=[t_TG, pst[qa[t]]], w=[t_TB])
                        TT("pool", ma[:, oc, tl(t)], ma[:, oc, tl(t)], TB[:, tl(t)], ALU.add,
                           r=[t_TB, t_ma[oc][t]], w=[t_ma[oc][t]])
            if s == 0:
                dump("m", ma, [t_ma[c][t] for c in range(8) for t in range(NT)], [128, 8, S_LEN], BF16)

            if upto <= 5:
                raise _Stop()
            S.barrier()
            for oc4 in range(2):
                wo_, tko_ = load_w(b_o, t_bo, 0, 8, oc4 * 512, 512)
                for j in range(4):
                    oc = oc4 * 4 + j
                    DMA("sp", "xr%d" % oc, x1[:, oc, :], xT[s, oc * 128:(oc + 1) * 128, :], w=t_x1[oc])
                    q = proj4(wo_, tko_, 8, j, ma, t_ma)
                    for t in range(NT):
                        STT("dve", x1[:, oc, tl(t)], ps[q[t]][:], 0.5, x1[:, oc, tl(t)], ALU.mult, ALU.add,
                            r=[pst[q[t]], t_x1[oc][t]], w=[t_x1[oc][t]])
            if s == 0:
                dump("x1", x1, [t_x1[c][t] for c in range(8) for t in range(NT)], [128, 8, S_LEN], F32)

            if upto <= 6:
                raise _Stop()
            xn2 = xn
            t_xn2 = t_xn
            ar.seek(KB_MA)
            hh = ar.alloc([12, S_LEN], BF16)
            graw = ar.alloc([S_LEN + 8], F32)
            gcv = ar.alloc([S_LEN], F32)
            assert ar.off <= 136 * 1024, ar.off
            t_hh = [[Tk() for _ in range(NT)] for _ in range(12)]
            t_graw, t_gcv = Tk(), Tk()
            sq2 = graw.bitcast(BF16)[:, 0:8 * TS].rearrange("p (k n) -> p k n", k=8)
            rs2 = gcv[:, 0:TS]
            for t in range(NT):
                ACT(sq2, x1[:, :, tl(t)], AF.Square, r=[t_x1[k][t] for k in range(8)], w=[t_graw])
                pb = bank()
                for k in range(8):
                    MM(ps[pb][:], cb[:, CB_ONES:CB_ONES + 128], sq2[:, k, :], k == 0, k == 7, r=[t_graw, t_cb], w=[pst[pb]])
                ACT(rs2, ps[pb][:], AF.Sqrt, r=[pst[pb], t_pv], w=[t_gcv], bias=col(pv, PV_EPS))
                RECIP(rs2, rs2, r=[t_gcv], w=[t_gcv])
                for k in range(8):
                    STT("dve", xn2[:, k, tl(t)], x1[:, k, tl(t)], col(pv, PV_G2 + k), rs2, ALU.mult, ALU.mult,
                        r=[t_x1[k][t], t_gcv, t_pv], w=[t_xn2[k][t]])
            MEMSET("dve", graw, 0.0, r=[t_gcv], w=[t_graw])
            for half in range(2):
                for grp in range(3):
                    cbase = half * 12 + grp * 4
                    wgt, tkgt = load_w(b_up, t_bup, 0, 8, cbase * 128, 512)
                    wvl, tkvl = load_w(b_up, t_bup, 0, 8, D_FF + cbase * 128, 512)
                    for j in range(4):
                        c = cbase + j
                        i = grp * 4 + j
                        q = proj4(wgt, tkgt, 8, j, xn2, t_xn2)
                        for t in range(NT):
                            ACT(graw[:, 1 + t * TS:1 + (t + 1) * TS], ps[q[t]][:], AF.Copy, r=[pst[q[t]]], w=[t_graw])
                        ACT(gcv, graw[:, 0:S_LEN], AF.Identity, r=[t_graw, t_pv], w=[t_gcv],
                            scale=col(pv, PV_FCW + c * 3 + 0), bias=col(pv, PV_FCB + c))
                        for jj in (1, 2):
                            STT("dve", gcv, graw[:, jj:jj + S_LEN], col(pv, PV_FCW + c * 3 + jj), gcv, ALU.mult, ALU.add,
                                r=[t_graw, t_gcv, t_pv], w=[t_gcv])
                        ACT(gcv, gcv, AF.Gelu_apprx_tanh, r=[t_gcv], w=[t_gcv])
                        q2 = proj4(wvl, tkvl, 8, j, xn2, t_xn2)
                        for t in range(NT):
                            TT("dve", hh[:, i, tl(t)], ps[q2[t]][:], gcv[:, tl(t)], ALU.mult, r=[pst[q2[t]], t_gcv],
                               w=[t_hh[i][t]] + ([t_ma[i][t]] if i < 8 else []))
                if s == 0 and half == 0:
                    dump("hh", hh, [t_hh[c][t] for c in range(12) for t in range(NT)], [128, 12, S_LEN], BF16)
                for oc2 in range(4):
                    wd_, tkd_ = load_w(b_down, t_bdown, half * 12, 12, oc2 * 256, 256)
                    for j in range(2):
                        oc = oc2 * 2 + j
                        q = proj4(wd_, tkd_, 12, j, hh, t_hh)
                        for t in range(NT):
                            TT("dve", x1[:, oc, tl(t)], ps[q[t]][:], x1[:, oc, tl(t)], ALU.add,
                               r=[pst[q[t]], t_x1[oc][t]], w=[t_x1[oc][t]])
                        if half == 1:
                            DMA("act", "o%d_%d" % (s % 2, oc), yT[s, oc * 128:(oc + 1) * 128, :], x1[:, oc, :], r=t_x1[oc], w=[Tk()])

        for s in range(nseq):
            try:
                if upto >= 1:
                    seq(s)
            except _Stop:
                pass
        S.barrier()
        fin = ar.t[:, KB_SCR * 256:KB_SCR * 256 + 1]
        MEMSET("dve", fin, 0.0, r=[], w=[Tk()])
        S.emit()
    return nc, list(dbg_outs.keys())


def _consts():
    cbf = np.zeros((128, 2896), np.float32)
    cbf[:, 0:128] = 1.0 / 1024.0
    blk = np.zeros((128, 128), np.float32)
    blk[0:64, 0:64] = 1.0 / 64.0
    blk[64:128, 64:128] = 1.0 / 64.0
    cbf[:, 128:256] = blk
    perm = np.zeros((128, 128), np.float32)
    for m in range(128):
        dl = m % 64
        if dl < 8:
            perm[m + 8, m] = 1.0
        elif dl < 16:
            perm[m - 8, m] = 1.0
    cbf[:, 256:384] = perm
    cbf[:, 384:448] = 1.0
    cbf[:, 2496:2496 + 64] = 1.0
    cbf[:, 2624 + 64:2624 + 128] = 1.0
    cbf[:, 2752:2880] = np.eye(128, dtype=np.float32)
    cbf[0:64, 2880] = 1.0 / 64.0
    cbf[64:128, 2881] = 1.0 / 64.0
    kk = np.arange(128)[:, None]
    qq = np.arange(128)[None, :]
    A_gen = (kk >= qq)
    B_gen = (kk <= qq)
    A_first = (kk >= np.maximum(qq, 64))
    B_last = (kk <= np.minimum(qq, 63))
    combos = [(A_gen, B_gen), (A_first, B_gen), (A_gen, B_last), (A_first, B_last)]
    for i, (a, b) in enumerate(combos):
        tile = np.concatenate([a, b, a, b], axis=1).astype(np.float32)
        cbf[:, 448 + i * 512:448 + (i + 1) * 512] = tile
    pos = np.arange(S_LEN, dtype=np.float32)
    inv = (np.float32(500000.0) ** (-np.arange(0, 16, 2, dtype=np.float32) / np.float32(16))).astype(np.float32)
    ang = pos[:, None] * inv[None, :]
    cos = np.cos(ang).astype(np.float32)
    sin = np.sin(ang).astype(np.float32)
    cs = np.zeros((2, 128, S_LEN), np.float32)
    cs[0] = 1.0
    for p in range(128):
        dl = p % 64
        if dl < 8:
            cs[0, p] = cos[:, dl]
            cs[1, p] = -sin[:, dl]
        elif dl < 16:
            cs[0, p] = cos[:, dl - 8]
            cs[1, p] = sin[:, dl - 8]
    return cbf, cs


def _pvec(inp):
    pv = np.zeros((128, 240), np.float32)

    def pm(v, n):
        return np.ascontiguousarray(np.asarray(v, np.float32).reshape(n, 128).T)

    pv[:, 0:8] = pm(inp["norm1_g"][0], 8)
    pv[:, 8:16] = pm(inp["norm2_g"][0], 8)
    lcw = np.asarray(inp["lru_conv_w"][0], np.float32)
    pv[:, 16:56] = np.ascontiguousarray(lcw.reshape(4, 10, 128).transpose(2, 1, 0)).reshape(128, 40)
    pv[:, 56:66] = pm(inp["lru_conv_b"][0], 10)
    for (off, name) in ((66, "lru_lambda"), (86, "lru_ba"), (106, "lru_bx")):
        v = np.asarray(inp[name][0], np.float32)
        pv[:, off:off + 20] = np.ascontiguousarray(v.reshape(2, 10, 128).transpose(2, 0, 1)).reshape(128, 20)
    fcw = np.asarray(inp["ffn_conv_w"][0], np.float32)
    pv[:, 126:198] = np.ascontiguousarray(fcw.reshape(3, 24, 128).transpose(2, 1, 0)).reshape(128, 72)
    pv[:, 198:222] = pm(inp["ffn_conv_b"][0], 24)
    src = np.arange(64)
    src[0:8] = np.arange(8, 16)
    src[8:16] = np.arange(0, 8)
    for which, name in enumerate(("q_norm_g", "k_norm_g")):
        gq = np.asarray(inp[name][0], np.float32)
        for g in range(3):
            pv[:, 222 + which * 3 + g] = np.tile(gq[g], 2)
            pv[:, 228 + which * 3 + g] = np.tile(gq[g][src], 2)
    pv[:, 234] = EPS
    pv[:, 235] = 1.0
    return pv


_PROG = {}


def kernel(**inp):
    xp = np.asarray(inp["x_prompt"], np.float32)
    xs = np.asarray(inp["x_sample"], np.float32)
    nseq = 5
    if nseq not in _PROG:
        _PROG[nseq] = build_program(nseq)[0]
    nc = _PROG[nseq]
    cbf, cs = _consts()
    pv = _pvec(inp)
    lru_w = np.ascontiguousarray(np.stack([inp["lru_wa"][0][0], inp["lru_wx"][0][0], inp["lru_wa"][0][1], inp["lru_wx"][0][1]],
                                          axis=0).astype(np.float32))
    shared = {
        "w_in": np.ascontiguousarray(inp["w_in"][0], np.float32),
        "w_up": np.ascontiguousarray(inp["w_up"][0], np.float32),
        "w_down": np.ascontiguousarray(inp["w_down"][0], np.float32),
        "w_o": np.ascontiguousarray(inp["w_o"][0], np.float32),
        "w_lru_out": np.ascontiguousarray(inp["w_lru_out"][0], np.float32),
        "w_att_out": np.ascontiguousarray(inp["w_att_out"][0], np.float32),
        "lru_w": lru_w, "pvec": pv, "cbf": cbf, "cs": cs,
    }
    in_maps = []
    for i in range(NCORES):
        seqs = [xp[4 * i + j] for j in range(4)] + [xs[i]]
        xT = np.ascontiguousarray(np.stack([q.T for q in seqs], axis=0))
        m = dict(shared)
        m["xT"] = xT
        in_maps.append(m)
    res = run_bass_kernel_spmd(nc, in_maps, core_ids=list(range(NCORES)))
    yp = np.empty_like(xp)
    ys = np.empty_like(xs)
    for i in range(NCORES):
        yT = np.asarray(res.results[i]["yT"])
        for j in range(4):
            yp[4 * i + j] = yT[j].T
        ys[i] = yT[4].T
    return (yp, ys)
```

```python
import contextlib
import os
import numpy as np
import concourse.bass as bass
import concourse.mybir as mybir
from concourse.bass_utils import run_bass_kernel_spmd

F32 = mybir.dt.float32
BF16 = mybir.dt.bfloat16
AF = mybir.ActivationFunctionType
ALU = mybir.AluOpType

ENGS = ("pe", "act", "dve", "pool", "sp")

D = 1024
S_LEN = 2048
D_RNN = 1280
ATT_COLS = 1536
D_FF = 3072
IN_COLS = 9216
C_Q = 2560
C_K = C_Q + ATT_COLS
C_V = C_K + ATT_COLS
C_G = C_V + ATT_COLS
NT = 4
TS = 512
DIL = (1, 4, 16)
EPS = 1e-6
NCORES = 8


class Tk:
    __slots__ = ("w", "rd")

    def __init__(self):
        self.w = None
        self.rd = []


class Op:
    __slots__ = ("eng", "fn", "deps", "need_inc", "val", "dsem", "waits", "cidx")

    def __init__(self, eng, fn):
        self.eng = eng
        self.fn = fn
        self.deps = []
        self.need_inc = False
        self.val = 0
        self.dsem = None
        self.waits = []
        self.cidx = -1


class Sched:
    def __init__(self, nc):
        self.nc = nc
        self.prog = {e: [] for e in ENGS}
        self.dma_keys = {}
        self.pending_bar = {e: None for e in ENGS}

    def op(self, eng, fn, r=(), w=(), dma=None, nobar=False):
        o = Op(eng, fn)
        deps = []
        for t in r:
            if t.w is not None:
                deps.append(t.w)
        for t in w:
            if t.w is not None:
                deps.append(t.w)
            deps.extend(t.rd)
        for t in r:
            t.rd.append(o)
        for t in w:
            t.w = o
            t.rd = []
        if self.pending_bar[eng] is not None and not nobar:
            deps.extend(self.pending_bar[eng])
            self.pending_bar[eng] = None
        seen = set()
        for d in deps:
            if id(d) not in seen and d is not o:
                seen.add(id(d))
                o.deps.append(d)
        self.prog[eng].append(o)
        if dma is not None:
            o.dsem = dma
            self.dma_keys.setdefault(dma, []).append(o)
        return o

    def dma_multi(self, eng, key, fns, r=(), w=(), nobar=False):
        first = self.op(eng, fns[0], r=r, w=w, dma=key, nobar=nobar)
        last = first
        for fn in fns[1:]:
            last = self.op(eng, fn, dma=key, nobar=nobar)
        if last is not first:
            for t in w:
                t.w = last
            for t in r:
                t.rd.append(last)
        return last

    def barrier(self, skip_out=False):
        lasts = []
        for e in ENGS:
            for o in reversed(self.prog[e]):
                if o.dsem is None:
                    lasts.append(o)
                    break
        for k, ops in self.dma_keys.items():
            if str(k).startswith("p_") or (skip_out and str(k).startswith("o")):
                continue
            lasts.append(ops[-1])
        for e in ENGS:
            cur = self.pending_bar[e] or []
            self.pending_bar[e] = cur + lasts

    def finalize(self):
        def chan(o):
            return o.dsem if o.dsem is not None else o.eng

        for k, ops in self.dma_keys.items():
            for i, o in enumerate(ops):
                o.cidx = i
        for e in ENGS:
            i = 0
            for o in self.prog[e]:
                if o.dsem is None:
                    o.cidx = i
                    i += 1
        for e in ENGS:
            kn = {}
            for o in self.prog[e]:
                best = {}
                for d in o.deps:
                    c = chan(d)
                    if d.dsem is None and d.eng == e and e in ("pe", "sp"):
                        continue
                    if kn.get(c, -1) >= d.cidx:
                        continue
                    if c not in best or best[c].cidx < d.cidx:
                        best[c] = d
                for c, d in best.items():
                    kn[c] = d.cidx
                    d.need_inc = True
                    o.waits.append(d)
        for e in ENGS:
            v = 0
            for o in self.prog[e]:
                if o.dsem is None and o.need_inc:
                    v += 1
                    o.val = v
        for k, ops in self.dma_keys.items():
            for i, o in enumerate(ops):
                o.val = 16 * (i + 1)

    def emit(self):
        nc = self.nc
        self.finalize()
        with contextlib.ExitStack() as st:
            esem = {e: st.enter_context(nc.semaphore("s_" + e)) for e in ENGS}
            dsem = {k: st.enter_context(nc.semaphore("d_%s" % (str(k),))) for k in self.dma_keys}
            block = st.enter_context(nc.Block())

            def semof(o):
                return dsem[o.dsem] if o.dsem is not None else esem[o.eng]

            def run(e):
                def body(eng):
                    for o in self.prog[e]:
                        for d in o.waits:
                            eng.wait_ge(semof(d), d.val)
                        inst = o.fn(eng)
                        if o.dsem is not None:
                            inst.then_inc(dsem[o.dsem], 16)
                        elif o.need_inc:
                            inst.then_inc(esem[e], 1)
                return body

            block.tensor(run("pe"))
            block.scalar(run("act"))
            block.vector(run("dve"))
            block.gpsimd(run("pool"))
            block.sync(run("sp"))


class Arena:
    def __init__(self, nc, st, nbytes):
        self.n = nbytes
        self.t = st.enter_context(nc.sbuf_tensor("arena", [128, nbytes // 4], F32))
        self.off = 0

    def seek(self, kb):
        self.off = int(kb * 1024)

    def alloc(self, shape_free, dtype):
        esz = 2 if dtype == BF16 else 4
        n = int(np.prod(shape_free))
        nb = (n * esz + 31) // 32 * 32
        assert self.off + nb <= self.n, ("arena overflow", self.off, nb, self.n)
        a = self.t[:, self.off // 4:(self.off + nb) // 4]
        self.off += nb
        if dtype != F32:
            a = a.bitcast(dtype)
        a = a[:, 0:n]
        if len(shape_free) == 2:
            a = a.rearrange("p (a b) -> p a b", a=shape_free[0])
        elif len(shape_free) == 3:
            a = a.rearrange("p (a b c) -> p a b c", a=shape_free[0], b=shape_free[1])
        return a


def gate_nbrs(c):
    lo_b = (c * 128) // 80
    hi_b = (c * 128 + 127) // 80
    r0 = lo_b * 80
    r1 = hi_b * 80 + 79
    return list(range(r0 // 128, r1 // 128 + 1))


class _Stop(Exception):
    pass


def build_program(nseq, dbg=False, upto=99):
    nc = bass.Bass("TRN2", target_bir_lowering=False)
    S = Sched(nc)

    def MM(out, lhsT, rhs, start, stop, r, w):
        S.op("pe", lambda e: e.matmul(out, lhsT=lhsT, rhs=rhs, start=start, stop=stop), r=r, w=w)

    def TRN(out, in_, ident, r, w):
        S.op("pe", lambda e: e.transpose(out, in_, ident), r=r, w=w)

    def ACT(out, in_, func, r, w, scale=None, bias=None):
        kw = {}
        if scale is not None:
            kw["scale"] = scale
        if bias is not None:
            kw["bias"] = bias
        S.op("act", lambda e: e.activation(out=out, in_=in_, func=func, **kw), r=r, w=w)

    def STT(eng, out, in0, scalar, in1, op0, op1, r, w):
        S.op(eng, lambda e: e.scalar_tensor_tensor(out=out, in0=in0, scalar=scalar, in1=in1, op0=op0, op1=op1), r=r, w=w)

    def TT(eng, out, in0, in1, op, r, w):
        S.op(eng, lambda e: e.tensor_tensor(out=out, in0=in0, in1=in1, op=op), r=r, w=w)

    def TSC(eng, out, in0, scalar1, op0, r, w):
        S.op(eng, lambda e: e.tensor_scalar(out=out, in0=in0, scalar1=scalar1, scalar2=None, op0=op0), r=r, w=w)

    def RECIP(out, in_, r, w):
        S.op("dve", lambda e: e.reciprocal(out=out, in_=in_), r=r, w=w)

    def MEMSET(eng, ap, val, r, w):
        S.op(eng, lambda e: e.memset(ap, val), r=r, w=w)

    def COPY(eng, out, in_, r, w):
        S.op(eng, lambda e: e.tensor_copy(out=out, in_=in_), r=r, w=w)

    def SCAN(out, d0, d1, r, w):
        S.op("dve", lambda e: e.tensor_tensor_scan(out=out, data0=d0, data1=d1, initial=0.0, op0=ALU.mult, op1=ALU.add),
             r=r, w=w)

    def DMA(eng, key, out, in_, r=(), w=()):
        return S.op(eng, lambda e: e.dma_start(out=out, in_=in_), r=r, w=w, dma=key)

    def dma_fn(out, in_):
        return lambda e: e.dma_start(out=out, in_=in_)

    def din(name, shape, dt=F32):
        return nc.dram_tensor(name, list(shape), dt, kind="ExternalInput").ap()

    xT = din("xT", [nseq, D, S_LEN])
    w_in = din("w_in", [D, IN_COLS])
    w_up = din("w_up", [D, 2 * D_FF])
    w_down = din("w_down", [D_FF, D])
    w_o = din("w_o", [D, D])
    w_lo = din("w_lru_out", [D_RNN, D])
    w_ao = din("w_att_out", [512, D])
    lru_w = din("lru_w", [4, 16, 80, 80])
    pvec = din("pvec", [128, 240])
    cbf = din("cbf", [128, 2896])
    csd = din("cs", [2, 128, S_LEN])
    yT = nc.dram_tensor("yT", [nseq, D, S_LEN], F32, kind="ExternalOutput").ap()

    def dscr(name, shape):
        return nc.dram_tensor(name, list(shape), BF16, kind="Internal").ap()

    b_in = dscr("b_in", [D, IN_COLS])
    b_up = dscr("b_up", [D, 2 * D_FF])
    b_down = dscr("b_down", [D_FF, D])
    b_o = dscr("b_o", [D, D])
    b_lo = dscr("b_lo", [D_RNN, D])
    b_ao = dscr("b_ao", [512, D])
    b_g = dscr("b_g", [4, D_RNN, D_RNN])
    dbg_outs = {}

    st = contextlib.ExitStack()
    with st:
        ar = Arena(nc, st, 200 * 1024)
        ps = [st.enter_context(nc.psum_tensor("ps%d" % i, [128, 512], F32)) for i in range(8)]
        pst = [Tk() for _ in range(8)]
        pstate = {"b": 0, "q": 0}

        def bank(excl=()):
            b = pstate["b"]
            while b in excl:
                b = (b + 1) % 8
            pstate["b"] = (b + 1) % 8
            return b

        def quad():
            q = pstate["q"]
            pstate["q"] = 1 - q
            return [4 * q + i for i in range(4)]

        ar.seek(0)
        pv = ar.alloc([240], F32)
        t_pv = Tk()
        der = ar.alloc([96], F32)
        t_der = Tk()
        cb = ar.alloc([2896], BF16)
        t_cb = Tk()
        ring = [(ar.alloc([4096], BF16), Tk()) for _ in range(3)]
        rstate = {"i": 0}
        P_END = ar.off
        assert P_END <= 31 * 1024, P_END

        PV_G1, PV_G2 = 0, 8
        PV_LCW, PV_LCB = 16, 56
        PV_LAM, PV_BA, PV_BX = 66, 86, 106
        PV_FCW, PV_FCB = 126, 198
        PV_QG, PV_QGP = 222, 228
        PV_EPS, PV_ONE = 234, 235
        DR_HC, DR_C2, DR_HBA, DR_HBX = 0, 20, 40, 60
        CB_ONES, CB_BLK, CB_PERM, CB_O64, CB_MASK, CB_L0, CB_L1, CB_ID, CB_IND = 0, 128, 256, 384, 448, 2496, 2624, 2752, 2880

        def col(a, i):
            return a[:, i:i + 1]

        def tl(t):
            return slice(t * TS, (t + 1) * TS)

        t_bin, t_bup, t_bdown, t_bo, t_blo, t_bao, t_bg = (Tk() for _ in range(7))

        def prep_cast(key, dst, src, rows, tk, nsplit):
            step = rows // nsplit
            fns = [dma_fn(dst[i * step:(i + 1) * step, :], src[i * step:(i + 1) * step, :]) for i in range(nsplit)]
            S.dma_multi("pool", key, fns, w=[tk])

        DMA("sp", "c_pv", pv, pvec, w=[t_pv])
        DMA("pool", "c_cb", cb, cbf, w=[t_cb])
        t_bin_grp = [Tk(), Tk(), Tk()]
        BIN_COLS = [(0, 2560), (2560, C_G), (C_G, IN_COLS)]
        for gi_, (c0_, c1_) in enumerate(BIN_COLS[:1]):
            fns_ = [dma_fn(b_in[i * 256:(i + 1) * 256, c0_:c1_], w_in[i * 256:(i + 1) * 256, c0_:c1_]) for i in range(4)]
            S.dma_multi("pool", "p_in%d" % gi_, fns_, w=[t_bin_grp[gi_]])
        zt = ring[0][0]
        MEMSET("dve", zt[:, 0:D_RNN], 0.0, r=[], w=[ring[0][1]])
        fns = [dma_fn(b_g[dg, k * 128:(k + 1) * 128, :], zt[:, 0:D_RNN]) for dg in range(4) for k in range(10)]
        t_bgz = Tk()
        S.dma_multi("sp", "p_gz", fns, r=[ring[0][1]], w=[t_bgz])
        fns = [dma_fn(b_g[dg, b * 80:(b + 1) * 80, b * 80:(b + 1) * 80], lru_w[dg, b]) for dg in range(4) for b in range(16)]
        S.dma_multi("pool", "p_g", fns, r=[t_bgz], w=[t_bg])
        for gi_, (c0_, c1_) in list(enumerate(BIN_COLS))[1:]:
            fns_ = [dma_fn(b_in[i * 256:(i + 1) * 256, c0_:c1_], w_in[i * 256:(i + 1) * 256, c0_:c1_]) for i in range(4)]
            S.dma_multi("pool", "p_in%d" % gi_, fns_, w=[t_bin_grp[gi_]])
        prep_cast("p_lo", b_lo, w_lo, D_RNN, t_blo, 2)
        prep_cast("p_ao", b_ao, w_ao, 512, t_bao, 1)
        prep_cast("p_o", b_o, w_o, D, t_bo, 2)
        prep_cast("p_up", b_up, w_up, D, t_bup, 8)
        prep_cast("p_down", b_down, w_down, D_FF, t_bdown, 4)

        ACT(der[:, 0:20], pv[:, PV_LAM:PV_LAM + 20], AF.Exp, r=[t_pv], w=[t_der], scale=-1.0)
        ACT(der[:, 0:20], der[:, 0:20], AF.Ln, r=[t_pv, t_der], w=[t_der], bias=col(pv, PV_ONE))
        TSC("dve", der[:, DR_C2:DR_C2 + 20], der[:, 0:20], -8.0, ALU.mult, r=[t_der], w=[t_der])
        TSC("dve", der[:, DR_HC:DR_HC + 20], der[:, 0:20], -4.0, ALU.mult, r=[t_der], w=[t_der])
        TSC("dve", der[:, DR_HBA:DR_HBA + 40], pv[:, PV_BA:PV_BA + 40], 0.5, ALU.mult, r=[t_pv, t_der], w=[t_der])

        def load_w(src, t_src, k0, nk, c0, ncols):
            if t_src is t_bin:
                t_src = t_bin_grp[0] if c0 < 2560 else (t_bin_grp[1] if c0 < C_G else t_bin_grp[2])
            slot = rstate["i"] % 3
            rstate["i"] += 1
            buf, tk = ring[slot]
            view = buf[:, 0:nk * ncols].rearrange("p (k c) -> p k c", k=nk)
            srcv = src[k0 * 128:(k0 + nk) * 128, c0:c0 + ncols].rearrange("(k p) c -> p k c", p=128)
            S.op("sp", dma_fn(view, srcv), r=[t_src], w=[tk], dma="w%d" % slot, nobar=True)
            return view, tk

        def load_gate_w(c):
            nb = gate_nbrs(c)
            nk = len(nb)
            slot = rstate["i"] % 3
            rstate["i"] += 1
            buf, tk = ring[slot]
            view = buf[:, 0:4 * nk * 128].rearrange("p (g k c) -> p g k c", g=4, k=nk)
            fns = []
            for dg in range(4):
                srcv = b_g[dg, nb[0] * 128:(nb[0] + nk) * 128, c * 128:(c + 1) * 128].rearrange("(k p) c -> p k c", p=128)
                fns.append(dma_fn(view[:, dg], srcv))
            S.dma_multi("sp", "w%d" % slot, fns, r=[t_bg], w=[tk], nobar=True)
            return view, tk, nb

        def proj4(wt, t_w, nk, j, src, src_tks, q=None, t_outer=True):
            if q is None:
                q = quad()
            order = [(k, t) for t in range(NT) for k in range(nk)] if t_outer else [(k, t) for k in range(nk) for t in range(NT)]
            for (k, t) in order:
                MM(ps[q[t]][:], wt[:, k, j * 128:(j + 1) * 128], src[:, k, tl(t)], k == 0, k == nk - 1,
                   r=[t_w, src_tks[k][t]], w=[pst[q[t]]])
            return q

        dbgc = {"i": 0}

        def dump(name, ap, tks, shape, dt=F32):
            if not dbg:
                return
            o = nc.dram_tensor("dbg_" + name, list(shape), dt, kind="ExternalOutput").ap()
            dbg_outs[name] = 1
            DMA("sp", "dbg%d" % dbgc["i"], o, ap, r=tks, w=[Tk()])
            dbgc["i"] += 1

        KB_XN = 31
        KB_MA = 63
        KB_SCR = 95
        ar.seek(KB_XN)
        xn = ar.alloc([8, S_LEN], BF16)
        ar.seek(KB_MA)
        ma = ar.alloc([8, S_LEN], BF16)
        ar.seek(136)
        x1 = ar.alloc([8, S_LEN], F32)

        def seq(s):
            t_xn = [[Tk() for _ in range(NT)] for _ in range(8)]
            t_ma = [[Tk() for _ in range(NT)] for _ in range(8)]
            t_x1 = [[Tk() for _ in range(NT)] for _ in range(8)]
            S.barrier(skip_out=True)
            ar.seek(KB_SCR)
            xt = [ar.alloc([8, TS], F32) for _ in range(2)]
            t_xt = [Tk(), Tk()]
            sq = ar.alloc([8, TS], BF16)
            t_sq = Tk()
            assert ar.off <= 136 * 1024
            ar.seek(KB_MA)
            rs = ar.alloc([TS], F32)
            t_rs = Tk()
            for t in range(NT):
                b = t % 2
                DMA("sp", "x%d" % b, xt[b], xT[s, :, tl(t)].rearrange("(k p) n -> p k n", p=128), w=[t_xt[b]])
                ACT(sq, xt[b], AF.Square, r=[t_xt[b]], w=[t_sq])
                pb = bank()
                for k in range(8):
                    MM(ps[pb][:], cb[:, CB_ONES:CB_ONES + 128], sq[:, k, :], k == 0, k == 7, r=[t_sq, t_cb], w=[pst[pb]])
                ACT(rs, ps[pb][:], AF.Sqrt, r=[pst[pb], t_pv], w=[t_rs], bias=col(pv, PV_EPS))
                RECIP(rs, rs, r=[t_rs], w=[t_rs])
                for k in range(8):
                    STT("dve", xn[:, k, tl(t)], xt[b][:, k, :], col(pv, PV_G1 + k), rs, ALU.mult, ALU.mult,
                        r=[t_xt[b], t_rs, t_pv], w=[t_xn[k][t]])
            if s == 0:
                dump("xn", xn, [t_xn[k][t] for k in range(8) for t in range(NT)], [128, 8, S_LEN], BF16)

            if upto <= 1:
                raise _Stop()
            S.barrier()
            ar.seek(KB_MA)
            Abuf = ar.alloc([2, S_LEN], F32)
            S2 = ar.alloc([2, S_LEN], F32)
            ar.seek(KB_SCR)
            y = ar.alloc([10, S_LEN], BF16)
            xcb = ar.alloc([5, S_LEN], BF16)
            lx = ar.alloc([S_LEN + 8], F32)
            U = ar.alloc([2, S_LEN], F32)
            gg = ar.alloc([S_LEN], F32)
            TR = [ar.alloc([TS], F32) for _ in range(2)]
            TI = [ar.alloc([TS], F32) for _ in range(2)]
            t_y = [[Tk() for _ in range(NT)] for _ in range(10)]
            t_lx, t_gg = Tk(), Tk()
            t_A = [[Tk() for _ in range(NT)] for _ in range(2)]
            t_S2 = [[Tk() for _ in range(NT)] for _ in range(2)]
            t_U = [[Tk() for _ in range(NT)] for _ in range(2)]
            A_all = t_A[0] + t_A[1]
            S2_all = t_S2[0] + t_S2[1]
            U_all = t_U[0] + t_U[1]
            t_TR = [Tk(), Tk()]
            t_TI = [Tk(), Tk()]
            MEMSET("dve", lx, 0.0, r=[], w=[t_lx])
            tcnt = 0
            t_xcb = [[Tk() for _ in range(NT)] for _ in range(5)]
            for sb in range(2):
                wt_a, tk_a = load_w(b_in, t_bin, 0, 8, sb * 640, 512)
                wt_b, tk_b = load_w(b_in, t_bin, 0, 8, sb * 640 + 512, 128)
                for cl in range(5):
                    c = sb * 5 + cl
                    wt, tkw, j = (wt_a, tk_a, cl) if cl < 4 else (wt_b, tk_b, 0)
                    q = proj4(wt, tkw, 8, j, xn, t_xn)
                    for t in range(NT):
                        ACT(lx[:, 2 + t * TS:2 + (t + 1) * TS], ps[q[t]][:], AF.Copy, r=[pst[q[t]]], w=[t_lx])
                    tmp = U[:, 0, :]
                    ACT(tmp, lx[:, 0:S_LEN], AF.Identity, r=[t_lx, t_pv], w=t_U[0],
                        scale=col(pv, PV_LCW + c * 4 + 0), bias=col(pv, PV_LCB + c))
                    for jj in (1, 2):
                        STT("dve", tmp, lx[:, jj:jj + S_LEN], col(pv, PV_LCW + c * 4 + jj), tmp, ALU.mult, ALU.add,
                            r=[t_lx, t_pv] + t_U[0], w=t_U[0])
                    STT("dve", xcb[:, cl, :], lx[:, 3:3 + S_LEN], col(pv, PV_LCW + c * 4 + 3), tmp, ALU.mult, ALU.add,
                        r=[t_lx, t_pv] + t_U[0], w=t_xcb[cl])
                if s == 0 and sb == 0:
                    dump("xcb", xcb, [t_xcb[c][t] for c in range(5) for t in range(NT)], [128, 5, S_LEN], BF16)
                if upto <= 2:
                    raise _Stop()
                for cl in range(5):
                    c = sb * 5 + cl
                    gw, tkg, nb = load_gate_w(c)
                    nk = len(nb)
                    for dr in range(2):
                        idx = dr * 10 + c
                        for t in range(NT):
                            pr, pi = bank(), bank()
                            for (pb, gi) in ((pr, 0), (pi, 1)):
                                for ki in range(nk):
                                    icl = nb[ki] - sb * 5
                                    MM(ps[pb][:], gw[:, dr * 2 + gi, ki, :], xcb[:, icl, tl(t)], ki == 0, ki == nk - 1,
                                       r=[tkg, t_xcb[icl][t]], w=[pst[pb]])
                            i2 = tcnt % 2
                            tcnt += 1
                            ACT(TR[i2], ps[pr][:], AF.Tanh, r=[pst[pr], t_der], w=[t_TR[i2]], scale=0.5, bias=col(der, DR_HBA + idx))
                            ACT(Abuf[:, dr, tl(t)], TR[i2], AF.Exp, r=[t_TR[i2], t_der], w=[t_A[dr][t]],
                                scale=col(der, DR_HC + idx), bias=col(der, DR_HC + idx))
                            TT("pool", S2[:, dr, tl(t)], Abuf[:, dr, tl(t)], Abuf[:, dr, tl(t)], ALU.mult,
                               r=[t_A[dr][t]], w=[t_S2[dr][t]])
                            ACT(U[:, dr, tl(t)], ps[pi][:], AF.Tanh, r=[pst[pi], t_der], w=[t_U[dr][t]], scale=0.5,
                                bias=col(der, DR_HBX + idx))
                            STT("dve", U[:, dr, tl(t)], U[:, dr, tl(t)], 1.0, xcb[:, cl, tl(t)], ALU.add, ALU.mult,
                                r=[t_U[dr][t], t_xcb[cl][t]], w=[t_U[dr][t]])
                    ACT(S2, S2, AF.Sqrt, r=[t_pv] + S2_all, w=S2_all, scale=-1.0, bias=col(pv, PV_ONE))
                    STT("dve", U, U, 0.5, S2, ALU.mult, ALU.mult, r=U_all + S2_all, w=U_all)
                    SCAN(S2[:, 0, :], Abuf[:, 0, :], U[:, 0, :], r=t_A[0] + t_U[0] + t_S2[0], w=t_S2[0])
                    SCAN(S2[:, 1, ::-1], Abuf[:, 1, ::-1], U[:, 1, ::-1], r=t_A[1] + t_U[1] + t_S2[1], w=t_S2[1])
                    wtg, tkwg = load_w(b_in, t_bin, 0, 8, D_RNN + c * 128, 128)
                    q = proj4(wtg, tkwg, 8, 0, xn, t_xn)
                    for t in range(NT):
                        ACT(gg[:, tl(t)], ps[q[t]][:], AF.Gelu_apprx_tanh, r=[pst[q[t]]], w=[t_gg])
                    TT("dve", lx[:, 2:2 + S_LEN], S2[:, 0, :], S2[:, 1, :], ALU.add, r=S2_all, w=[t_lx])
                    TT("dve", y[:, c, :], lx[:, 2:2 + S_LEN], gg, ALU.mult, r=[t_lx, t_gg], w=t_y[c])
            if s == 0:
                dump("y", y, [t_y[c][t] for c in range(10) for t in range(NT)], [128, 10, S_LEN], BF16)
            if upto <= 3:
                raise _Stop()
            S.barrier()
            TG = U[:, 0, :]
            t_TG = t_U[0][0]
            for oc2 in range(4):
                wl, tkl = load_w(b_lo, t_blo, 0, 10, oc2 * 256, 256)
                wg_, tkg_ = load_w(b_in, t_bin, 0, 8, C_G + oc2 * 256, 256)
                for j in range(2):
                    oc = oc2 * 2 + j
                    qg_ = proj4(wg_, tkg_, 8, j, xn, t_xn)
                    for t in range(NT):
                        ACT(TG[:, tl(t)], ps[qg_[t]][:], AF.Tanh, r=[pst[qg_[t]]], w=[t_TG], scale=0.5)
                    qa = proj4(wl, tkl, 10, j, y, t_y)
                    for t in range(NT):
                        STT("dve", ma[:, oc, tl(t)], TG[:, tl(t)], 1.0, ps[qa[t]][:], ALU.add, ALU.mult,
                            r=[t_TG, pst[qa[t]]], w=[t_ma[oc][t]])
            if s == 0:
                dump("ma", ma, [t_ma[c][t] for c in range(8) for t in range(NT)], [128, 8, S_LEN], BF16)

            if upto <= 4.05:
                raise _Stop()
            S.barrier()
            ar.seek(KB_SCR)
            cosT = ar.alloc([S_LEN], F32)
            sinT = ar.alloc([S_LEN], F32)
            accN = ar.alloc([S_LEN], F32)
            accD = ar.alloc([S_LEN], F32)
            QT = ar.alloc([S_LEN + 128], BF16)
            KTh = [ar.alloc([S_LEN + 128], BF16) for _ in range(2)]
            Vth = [ar.alloc([32, 128], BF16) for _ in range(2)]
            VT = ar.alloc([S_LEN + 128], BF16)
            t_VT = Tk()
            obf = ar.alloc([4, S_LEN], BF16)
            sqb = [ar.alloc([TS], BF16) for _ in range(2)]
            qrb = [ar.alloc([TS], BF16) for _ in range(2)]
            srp = [ar.alloc([8], F32) for _ in range(2)]
            Bc = [ar.alloc([8, 64], BF16) for _ in range(2)]
            t1b = [ar.alloc([TS], F32) for _ in range(4)]
            t2b = [ar.alloc([TS], F32) for _ in range(2)]
            Pb = [ar.alloc([TS], BF16) for _ in range(4)]
            assert ar.off <= 200 * 1024
            t_cs, t_accN, t_accD, t_QT, t_KT, t_V = Tk(), Tk(), Tk(), Tk(), Tk(), Tk()
            t_obf = [[Tk() for _ in range(NT)] for _ in range(4)]
            t_sqb, t_qrb, t_rst, t_t2, t_Bc = ([Tk(), Tk()] for _ in range(5))
            t_t1 = [Tk() for _ in range(4)]
            t_P = [Tk(), Tk(), Tk(), Tk()]
            t_Vt = [Tk() for _ in range(32)]
            S.dma_multi("act", "cs", [dma_fn(cosT, csd[0]), dma_fn(sinT, csd[1])], w=[t_cs])
            MEMSET("pool", QT, 0.0, r=[], w=[t_QT])
            MEMSET("pool", VT, 0.0, r=[], w=[t_VT])
            for h_ in range(2):
                MEMSET("pool", KTh[h_], 0.0, r=[], w=[t_KT])
                MEMSET("pool", Vth[h_], 0.0, r=[], w=t_Vt)
            ncnt = [0]
            pcnt = [0]
            bB = [0]
            bA = [0]

            def bankB():
                v = 4 + bB[0] % 4
                bB[0] += 1
                return v

            def bankA():
                v = bA[0] % 4
                bA[0] += 1
                return v

            QA = [0, 1, 2, 3]

            def combine_piece(hp_, t):
                RECIP(accD[:, tl(t)], accD[:, tl(t)], r=[t_accD], w=[t_accD])
                if t == NT - 1:
                    TT("pool", obf[:, hp_, :], accN, accD, ALU.mult, r=[t_accN, t_accD], w=t_obf[hp_])

            def combine(hp_):
                for t in range(NT):
                    combine_piece(hp_, t)

            for hp in range(4):
                for g in range(3):
                    d = DIL[g]
                    L = S_LEN // d
                    ntl = L // 128 + 1
                    wq, tkq = load_w(b_in, t_bin, 0, 8, C_Q + g * 512 + hp * 128, 128)
                    wk, tkk = load_w(b_in, t_bin, 0, 8, C_K + g * 512 + hp * 128, 128)
                    wv, tkv = load_w(b_in, t_bin, 0, 8, C_V + g * 512 + hp * 128, 128)
                    QB = [4, 5, 6, 7]

                    def v_proj():
                        qv = proj4(wv, tkv, 8, 0, xn, t_xn, q=QB)
                        body = VT[:, 64:64 + S_LEN]
                        for t in range(NT):
                            if d == 1:
                                ov = body[:, tl(t)]
                                iv = ps[qv[t]][:]
                            else:
                                w_ = TS // d
                                ov = body.rearrange("p (m q) -> p m q", m=d)[:, :, t * w_:(t + 1) * w_]
                                iv = ps[qv[t]][:].rearrange("p (j m) -> p m j", m=d)
                            ACT(ov, iv, AF.Copy, r=[pst[qv[t]]], w=[t_VT])

                    def v_tiles():
                        ntiles = d * ntl
                        for g0 in range(0, ntiles, 8):
                            n = min(8, ntiles - g0)
                            pb = bankB()
                            psv = ps[pb][:].bitcast(BF16)
                            for si in range(n):
                                tid = g0 + si
                                c0 = (tid // ntl) * L + 128 * (tid % ntl)
                                TRN(psv[:, si * 128:(si + 1) * 128], VT[:, c0:c0 + 128], cb[:, CB_ID:CB_ID + 128],
                                    r=[t_VT, t_cb], w=[pst[pb]])
                            for h_ in range(2):
                                ACT(Vth[h_][:, g0:g0 + n, h_ * 64:(h_ + 1) * 64],
                                    psv[:, 0:n * 128].rearrange("p (s c) -> p s c", s=n)[:, :, h_ * 64:(h_ + 1) * 64], AF.Copy,
                                    r=[pst[pb]], w=[t_Vt[tid] for tid in range(g0, g0 + n)])

                    st_ = {}

                    def S0(which, q, t):
                        gi = which * 3 + g
                        i2 = t % 2
                        ACT(sqb[i2], ps[q[t]][:], AF.Square, r=[pst[q[t]]], w=[t_sqb[i2]])
                        ACT(qrb[i2], ps[q[t]][:], AF.Copy, r=[pst[q[t]]], w=[t_qrb[i2]])
                        STT("dve", t1b[t], ps[q[t]][:], col(pv, PV_QG + gi), cosT[:, tl(t)], ALU.mult, ALU.mult,
                            r=[pst[q[t]], t_pv, t_cs, t_qrb[i2]], w=[t_t1[t]])

                    def PEn(which, t):
                        i2 = t % 2
                        pm, pr = bankB(), bankB()
                        st_[(which, t)] = (pm, pr)
                        MM(ps[pr][:], cb[:, CB_PERM:CB_PERM + 128], qrb[i2], True, True, r=[t_cb, t_qrb[i2]], w=[pst[pr]])
                        for bl in range(4):
                            MM(ps[pm][:, 2 * bl:2 * bl + 2], sqb[i2][:, bl * 128:(bl + 1) * 128], cb[:, CB_IND:CB_IND + 2],
                               True, True, r=[t_cb, t_sqb[i2]], w=[pst[pm]])

                    def rest(which, pair):
                        gi = which * 3 + g
                        t_dst = t_QT if which == 0 else t_KT
                        for t in pair:
                            i2 = t % 2
                            pm, pr = st_[(which, t)]
                            STT("dve", t2b[i2], ps[pr][:], col(pv, PV_QGP + gi), sinT[:, tl(t)], ALU.mult, ALU.mult,
                                r=[pst[pr], t_pv, t_cs], w=[t_t2[i2]])
                            ACT(srp[i2], ps[pm][:, 0:8], AF.Sqrt, r=[pst[pm], t_pv], w=[t_rst[i2]], bias=col(pv, PV_EPS))
                        for t in pair:
                            i2 = t % 2
                            RECIP(srp[i2], srp[i2], r=[t_rst[i2]], w=[t_rst[i2]])
                            TT("pool", t1b[t], t1b[t], t2b[i2], ALU.add, r=[t_t1[t], t_t2[i2]], w=[t_t1[t]])
                        for t in pair:
                            i2 = t % 2
                            ACT(Bc[i2], srp[i2].unsqueeze(2).broadcast_to([128, 8, 64]), AF.Copy, r=[t_rst[i2]], w=[t_Bc[i2]])
                        for t in pair:
                            i2 = t % 2
                            pm, pr = st_[(which, t)]
                            pbc = ps[pm][:].bitcast(BF16)[:, 32:32 + TS]
                            for bl in range(4):
                                TRN(pbc[:, bl * 128:(bl + 1) * 128], Bc[i2][:, 2 * bl:2 * bl + 2, :].rearrange("p a b -> p (a b)"),
                                    cb[:, CB_ID:CB_ID + 128], r=[t_Bc[i2], t_cb, pst[pm]], w=[pst[pm]])
                        for t in pair:
                            pm, pr = st_[(which, t)]
                            pbc = ps[pm][:].bitcast(BF16)[:, 32:32 + TS]
                            parts = [(QT, 0, 128)] if which == 0 else [(KTh[0], 0, 64), (KTh[1], 64, 128)]
                            for (dstb, p0, p1) in parts:
                                body = dstb[p0:p1, 64:64 + S_LEN]
                                if d == 1:
                                    ov = body[:, tl(t)]
                                    i0v = t1b[t][p0:p1]
                                    i1v = pbc[p0:p1]
                                else:
                                    w_ = TS // d
                                    ov = body.rearrange("p (m q) -> p m q", m=d)[:, :, t * w_:(t + 1) * w_]
                                    i0v = t1b[t][p0:p1].rearrange("p (j m) -> p m j", m=d)
                                    i1v = pbc[p0:p1].rearrange("p (j m) -> p m j", m=d)
                                TT("dve", ov, i0v, i1v, ALU.mult, r=[t_t1[t], pst[pm]], w=[t_dst])

                    def v_evac(qv):
                        body = VT[:, 64:64 + S_LEN]
                        for t in range(NT):
                            if d == 1:
                                ov = body[:, tl(t)]
                                iv = ps[qv[t]][:]
                            else:
                                w_ = TS // d
                                ov = body.rearrange("p (m q) -> p m q", m=d)[:, :, t * w_:(t + 1) * w_]
                                iv = ps[qv[t]][:].rearrange("p (j m) -> p m j", m=d)
                            ACT(ov, iv, AF.Copy, r=[pst[qv[t]]], w=[t_VT])

                    qq = proj4(wq, tkq, 8, 0, xn, t_xn, q=QA, t_outer=True)
                    S0(0, qq, 0)
                    S0(0, qq, 1)
                    PEn(0, 0)
                    PEn(0, 1)
                    S0(0, qq, 2)
                    S0(0, qq, 3)
                    dfr = (g == 0 and hp > 0)
                    qk_ = proj4(wk, tkk, 8, 0, xn, t_xn, q=QA, t_outer=True)
                    if dfr:
                        combine_piece(hp - 1, 0)
                    rest(0, (0, 1))
                    PEn(0, 2)
                    PEn(0, 3)
                    if dfr:
                        combine_piece(hp - 1, 1)
                    rest(0, (2, 3))
                    if dfr:
                        combine_piece(hp - 1, 2)
                    S0(1, qk_, 0)
                    S0(1, qk_, 1)
                    PEn(1, 0)
                    PEn(1, 1)
                    S0(1, qk_, 2)
                    S0(1, qk_, 3)
                    if dfr:
                        combine_piece(hp - 1, 3)
                    qv_ = proj4(wv, tkv, 8, 0, xn, t_xn, q=QA, t_outer=True)
                    rest(1, (0, 1))
                    v_evac(qv_)
                    v_tiles()
                    PEn(1, 2)
                    PEn(1, 3)
                    rest(1, (2, 3))
                    if s == 0 and hp == 0:
                        dump("QT%d" % g, QT, [t_QT], [128, S_LEN + 128], BF16)
                        dump("KT%d" % g, KTh[0], [t_KT], [128, S_LEN + 128], BF16)
                    blocks = []
                    for qb in range(16):
                        pi0 = qb * 128
                        m = pi0 // L
                        i0 = pi0 % L
                        first = (i0 == 0)
                        last = (i0 == L - 128)
                        mk = (3 if last else 1) if first else (2 if last else 0)
                        tA = m * ntl + i0 // 128
                        blocks.append(dict(pi0=pi0, mk=mk, tA=tA, tB=tA + 1, kA=pi0, kB=pi0 + 128))

                    def s_stage(bk):
                        psb = bankA()
                        for h in range(2):
                            for (seg, kc0) in ((2 * h, bk["kA"]), (2 * h + 1, bk["kB"])):
                                MM(ps[psb][:, seg * 128:(seg + 1) * 128], KTh[h][:, kc0:kc0 + 128],
                                   QT[:, 64 + bk["pi0"]:64 + bk["pi0"] + 128], True, True, r=[t_KT, t_QT], w=[pst[psb]])
                        p3 = pcnt[0] % 4
                        pcnt[0] += 1
                        bk["p3"] = p3
                        ACT(Pb[p3], ps[psb][:], AF.Exp, r=[pst[psb]], w=[t_P[p3]], scale=0.125)
                        mk = bk["mk"]
                        TT("pool" if os.environ.get("K_PMASK") else "dve", Pb[p3], Pb[p3],
                           cb[:, CB_MASK + mk * 512:CB_MASK + (mk + 1) * 512], ALU.mult, r=[t_P[p3], t_cb], w=[t_P[p3]])

                    def pv_stage(bk, pn, pd, qi):
                        p3 = bk["p3"]
                        for (seg, h, tid) in ((0, 0, bk["tA"]), (1, 0, bk["tB"]), (2, 1, bk["tA"]), (3, 1, bk["tB"])):
                            MM(ps[pn][:, qi * 128:(qi + 1) * 128], Vth[h][:, tid, :],
                               Pb[p3][:, seg * 128:(seg + 1) * 128], seg == 0, seg == 3, r=[t_Vt[tid], t_P[p3]], w=[pst[pn]])
                        for (seg, h) in ((0, 0), (1, 0), (2, 1), (3, 1)):
                            cl_ = CB_L0 if h == 0 else CB_L1
                            MM(ps[pd][:, qi * 128:(qi + 1) * 128], cb[:, cl_:cl_ + 128],
                               Pb[p3][:, seg * 128:(seg + 1) * 128], seg == 0, seg == 3, r=[t_cb, t_P[p3]], w=[pst[pd]])

                    s_stage(blocks[0])
                    s_stage(blocks[1])
                    s_stage(blocks[2])
                    for qb in range(16):
                        qb4, qi = qb // 4, qb % 4
                        pn, pd = (4, 5) if qb4 % 2 == 0 else (6, 7)
                        pv_stage(blocks[qb], pn, pd, qi)
                        if qb + 3 < 16:
                            s_stage(blocks[qb + 3])
                        if qi == 3:
                            for (acc, t_acc, pb) in ((accN, t_accN, pn), (accD, t_accD, pd)):
                                if d == 1:
                                    av = acc[:, qb4 * 512:(qb4 + 1) * 512]
                                    pv_ = ps[pb][:]
                                elif d == 4:
                                    av = acc[:, qb4:S_LEN:4]
                                    pv_ = ps[pb][:]
                                else:
                                    av = acc.rearrange("p (j m) -> p m j", m=16)[:, 4 * qb4:4 * qb4 + 4, :]
                                    pv_ = ps[pb][:].rearrange("p (m j) -> p m j", m=4)
                                if g == 0:
                                    COPY("dve", av, pv_, r=[pst[pb]], w=[t_acc])
                                else:
                                    TT("dve", av, pv_, av, ALU.add, r=[pst[pb], t_acc], w=[t_acc])
            combine(3)
            if s == 0:
                dump("obf", obf, [t_obf[c][t] for c in range(4) for t in range(NT)], [128, 4, S_LEN], BF16)
            TG = accN
            t_TG = t_accN
            TB = accD
            t_TB = t_accD
            for oc2 in range(4):
                wa_, tka_ = load_w(b_ao, t_bao, 0, 4, oc2 * 256, 256)
                wg_, tkg_ = load_w(b_in, t_bin, 0, 8, C_G + D + oc2 * 256, 256)
                for j in range(2):
                    oc = oc2 * 2 + j
                    qg_ = proj4(wg_, tkg_, 8, j, xn, t_xn)
                    for t in range(NT):
                        ACT(TG[:, tl(t)], ps[qg_[t]][:], AF.Tanh, r=[pst[qg_[t]]], w=[t_TG], scale=0.5)
                    qa = proj4(wa_, tka_, 4, j, obf, t_obf)
                    for t in range(NT):
                        STT("dve", TB[:, tl(t)], TG[:, tl(t)], 1.0, ps[qa[t]][:], ALU.add, ALU.mult,
                            r=[t_TG, pst[qa[t]]], w=[t_TB])
                        TT("pool", ma[:, oc, tl(t)], ma[:, oc, tl(t)], TB[:, tl(t)], ALU.add,
                           r=[t_TB, t_ma[oc][t]], w=[t_ma[oc][t]])
            if s == 0:
                dump("m", ma, [t_ma[c][t] for c in range(8) for t in range(NT)], [128, 8, S_LEN], BF16)

            if upto <= 5:
                raise _Stop()
            S.barrier()
            xn2 = xn
            t_xn2 = t_xn
            ar.seek(KB_MA)
            hh = ar.alloc([12, S_LEN], BF16)
            graw = ar.alloc([S_LEN + 8], F32)
            gcv = ar.alloc([S_LEN], F32)
            assert ar.off <= 136 * 1024, ar.off
            t_hh = [[Tk() for _ in range(NT)] for _ in range(12)]
            t_graw, t_gcv = Tk(), Tk()
            sq2 = graw.bitcast(BF16)[:, 0:8 * TS].rearrange("p (k n) -> p k n", k=8)
            rs2 = gcv[:, 0:TS]
            wo_ = [load_w(b_o, t_bo, 0, 8, h_ * 512, 512) for h_ in range(2)]

            def norm2_tile(t):
                ACT(sq2, x1[:, :, tl(t)], AF.Square, r=[t_x1[k][t] for k in range(8)], w=[t_graw])
                pb = bank()
                for k in range(8):
                    MM(ps[pb][:], cb[:, CB_ONES:CB_ONES + 128], sq2[:, k, :], k == 0, k == 7, r=[t_graw, t_cb], w=[pst[pb]])
                ACT(rs2, ps[pb][:], AF.Sqrt, r=[pst[pb], t_pv], w=[t_gcv], bias=col(pv, PV_EPS))
                RECIP(rs2, rs2, r=[t_gcv], w=[t_gcv])
                for k in range(8):
                    STT("dve", xn2[:, k, tl(t)], x1[:, k, tl(t)], col(pv, PV_G2 + k), rs2, ALU.mult, ALU.mult,
                        r=[t_x1[k][t], t_gcv, t_pv], w=[t_xn2[k][t]])

            for oc in range(8):
                DMA("sp", "xr%d" % oc, x1[:, oc, :], xT[s, oc * 128:(oc + 1) * 128, :], w=t_x1[oc])
            for t in range(NT):
                for oc in range(8):
                    wt_, tk_ = wo_[oc // 4]
                    j = oc % 4
                    pb = bank()
                    for k in range(8):
                        MM(ps[pb][:], wt_[:, k, j * 128:(j + 1) * 128], ma[:, k, tl(t)], k == 0, k == 7,
                           r=[tk_, t_ma[k][t]], w=[pst[pb]])
                    STT("dve", x1[:, oc, tl(t)], ps[pb][:], 0.5, x1[:, oc, tl(t)], ALU.mult, ALU.add,
                        r=[pst[pb], t_x1[oc][t]], w=[t_x1[oc][t]])
                if t > 0:
                    norm2_tile(t - 1)
            norm2_tile(NT - 1)
            if s == 0:
                dump("x1", x1, [t_x1[c][t] for c in range(8) for t in range(NT)], [128, 8, S_LEN], F32)

            if upto <= 6:
                raise _Stop()
            MEMSET("dve", graw, 0.0, r=[t_gcv], w=[t_graw])
            for half in range(2):
                for grp in range(3):
                    cbase = half * 12 + grp * 4
                    wgt, tkgt = load_w(b_up, t_bup, 0, 8, cbase * 128, 512)
                    wvl, tkvl = load_w(b_up, t_bup, 0, 8, D_FF + cbase * 128, 512)
                    for j in range(4):
                        c = cbase + j
                        i = grp * 4 + j
                        q = proj4(wgt, tkgt, 8, j, xn2, t_xn2)
                        for t in range(NT):
                            ACT(graw[:, 1 + t * TS:1 + (t + 1) * TS], ps[q[t]][:], AF.Copy, r=[pst[q[t]]], w=[t_graw])
                        ACT(gcv, graw[:, 0:S_LEN], AF.Identity, r=[t_graw, t_pv], w=[t_gcv],
                            scale=col(pv, PV_FCW + c * 3 + 0), bias=col(pv, PV_FCB + c))
                        for jj in (1, 2):
                            STT("dve", gcv, graw[:, jj:jj + S_LEN], col(pv, PV_FCW + c * 3 + jj), gcv, ALU.mult, ALU.add,
                                r=[t_graw, t_gcv, t_pv], w=[t_gcv])
                        ACT(gcv, gcv, AF.Gelu_apprx_tanh, r=[t_gcv], w=[t_gcv])
                        q2 = proj4(wvl, tkvl, 8, j, xn2, t_xn2)
                        for t in range(NT):
                            TT("dve", hh[:, i, tl(t)], ps[q2[t]][:], gcv[:, tl(t)], ALU.mult, r=[pst[q2[t]], t_gcv],
                               w=[t_hh[i][t]] + ([t_ma[i][t]] if i < 8 else []))
                if s == 0 and half == 0:
                    dump("hh", hh, [t_hh[c][t] for c in range(12) for t in range(NT)], [128, 12, S_LEN], BF16)
                for oc2 in range(4):
                    wd_, tkd_ = load_w(b_down, t_bdown, half * 12, 12, oc2 * 256, 256)
                    for j in range(2):
                        oc = oc2 * 2 + j
                        q = proj4(wd_, tkd_, 12, j, hh, t_hh)
                        for t in range(NT):
                            TT("dve", x1[:, oc, tl(t)], ps[q[t]][:], x1[:, oc, tl(t)], ALU.add,
                               r=[pst[q[t]], t_x1[oc][t]], w=[t_x1[oc][t]])
                        if half == 1:
                            DMA("act", "o%d_%d" % (s % 2, oc), yT[s, oc * 128:(oc + 1) * 128, :], x1[:, oc, :], r=t_x1[oc], w=[Tk()])

        for s in range(nseq):
            try:
                if upto >= 1:
                    seq(s)
            except _Stop:
                pass
        S.barrier()
        fin = ar.t[:, KB_SCR * 256:KB_SCR * 256 + 1]
        MEMSET("dve", fin, 0.0, r=[], w=[Tk()])
        S.emit()
    return nc, list(dbg_outs.keys())


def _consts():
    cbf = np.zeros((128, 2896), np.float32)
    cbf[:, 0:128] = 1.0 / 1024.0
    blk = np.zeros((128, 128), np.float32)
    blk[0:64, 0:64] = 1.0 / 64.0
    blk[64:128, 64:128] = 1.0 / 64.0
    cbf[:, 128:256] = blk
    perm = np.zeros((128, 128), np.float32)
    for m in range(128):
        dl = m % 64
        if dl < 8:
            perm[m + 8, m] = 1.0
        elif dl < 16:
            perm[m - 8, m] = 1.0
    cbf[:, 256:384] = perm
    cbf[:, 384:448] = 1.0
    cbf[:, 2496:2496 + 64] = 1.0
    cbf[:, 2624 + 64:2624 + 128] = 1.0
    cbf[:, 2752:2880] = np.eye(128, dtype=np.float32)
    cbf[0:64, 2880] = 1.0 / 64.0
    cbf[64:128, 2881] = 1.0 / 64.0
    kk = np.arange(128)[:, None]
    qq = np.arange(128)[None, :]
    A_gen = (kk >= qq)
    B_gen = (kk <= qq)
    A_first = (kk >= np.maximum(qq, 64))
    B_last = (kk <= np.minimum(qq, 63))
    combos = [(A_gen, B_gen), (A_first, B_gen), (A_gen, B_last), (A_first, B_last)]
    for i, (a, b) in enumerate(combos):
        tile = np.concatenate([a, b, a, b], axis=1).astype(np.float32)
        cbf[:, 448 + i * 512:448 + (i + 1) * 512] = tile
    pos = np.arange(S_LEN, dtype=np.float32)
    inv = (np.float32(500000.0) ** (-np.arange(0, 16, 2, dtype=np.float32) / np.float32(16))).astype(np.float32)
    ang = pos[:, None] * inv[None, :]
    cos = np.cos(ang).astype(np.float32)
    sin = np.sin(ang).astype(np.float32)
    cs = np.zeros((2, 128, S_LEN), np.float32)
    cs[0] = 1.0
    for p in range(128):
        dl = p % 64
        if dl < 8:
            cs[0, p] = cos[:, dl]
            cs[1, p] = -sin[:, dl]
        elif dl < 16:
            cs[0, p] = cos[:, dl - 8]
            cs[1, p] = sin[:, dl - 8]
    return cbf, cs


def _pvec(inp):
    pv = np.zeros((128, 240), np.float32)

    def pm(v, n):
        return np.ascontiguousarray(np.asarray(v, np.float32).reshape(n, 128).T)

    pv[:, 0:8] = pm(inp["norm1_g"][0], 8)
    pv[:, 8:16] = pm(inp["norm2_g"][0], 8)
    lcw = np.asarray(inp["lru_conv_w"][0], np.float32)
    pv[:, 16:56] = np.ascontiguousarray(lcw.reshape(4, 10, 128).transpose(2, 1, 0)).reshape(128, 40)
    pv[:, 56:66] = pm(inp["lru_conv_b"][0], 10)
    for (off, name) in ((66, "lru_lambda"), (86, "lru_ba"), (106, "lru_bx")):
        v = np.asarray(inp[name][0], np.float32)
        pv[:, off:off + 20] = np.ascontiguousarray(v.reshape(2, 10, 128).transpose(2, 0, 1)).reshape(128, 20)
    fcw = np.asarray(inp["ffn_conv_w"][0], np.float32)
    pv[:, 126:198] = np.ascontiguousarray(fcw.reshape(3, 24, 128).transpose(2, 1, 0)).reshape(128, 72)
    pv[:, 198:222] = pm(inp["ffn_conv_b"][0], 24)
    src = np.arange(64)
    src[0:8] = np.arange(8, 16)
    src[8:16] = np.arange(0, 8)
    for which, name in enumerate(("q_norm_g", "k_norm_g")):
        gq = np.asarray(inp[name][0], np.float32)
        for g in range(3):
            pv[:, 222 + which * 3 + g] = np.tile(gq[g], 2)
            pv[:, 228 + which * 3 + g] = np.tile(gq[g][src], 2)
    pv[:, 234] = EPS
    pv[:, 235] = 1.0
    return pv


_PROG = {}


def kernel(**inp):
    xp = np.asarray(inp["x_prompt"], np.float32)
    xs = np.asarray(inp["x_sample"], np.float32)
    nseq = 5
    if nseq not in _PROG:
        _PROG[nseq] = build_program(nseq)[0]
    nc = _PROG[nseq]
    cbf, cs = _consts()
    pv = _pvec(inp)
    lru_w = np.ascontiguousarray(np.stack([inp["lru_wa"][0][0], inp["lru_wx"][0][0], inp["lru_wa"][0][1], inp["lru_wx"][0][1]],
                                          axis=0).astype(np.float32))
    shared = {
        "w_in": np.ascontiguousarray(inp["w_in"][0], np.float32),
        "w_up": np.ascontiguousarray(inp["w_up"][0], np.float32),
        "w_down": np.ascontiguousarray(inp["w_down"][0], np.float32),
        "w_o": np.ascontiguousarray(inp["w_o"][0], np.float32),
        "w_lru_out": np.ascontiguousarray(inp["w_lru_out"][0], np.float32),
        "w_att_out": np.ascontiguousarray(inp["w_att_out"][0], np.float32),
        "lru_w": lru_w, "pvec": pv, "cbf": cbf, "cs": cs,
    }
    in_maps = []
    for i in range(NCORES):
        seqs = [xp[4 * i + j] for j in range(4)] + [xs[i]]
        xT = np.ascontiguousarray(np.stack([q.T for q in seqs], axis=0))
        m = dict(shared)
        m["xT"] = xT
        in_maps.append(m)
    res = run_bass_kernel_spmd(nc, in_maps, core_ids=list(range(NCORES)))
    yp = np.empty_like(xp)
    ys = np.empty_like(xs)
    for i in range(NCORES):
        yT = np.asarray(res.results[i]["yT"])
        for j in range(4):
            yp[4 * i + j] = yT[j].T
        ys[i] = yT[4].T
    return (yp, ys)
```

```python
import contextlib
import os
import numpy as np
import concourse.bass as bass
import concourse.mybir as mybir
from concourse.bass_utils import run_bass_kernel_spmd

F32 = mybir.dt.float32
BF16 = mybir.dt.bfloat16
AF = mybir.ActivationFunctionType
ALU = mybir.AluOpType

ENGS = ("pe", "act", "dve", "pool", "sp")

D = 1024
S_LEN = 2048
D_RNN = 1280
ATT_COLS = 1536
D_FF = 3072
IN_COLS = 9216
C_Q = 2560
C_K = C_Q + ATT_COLS
C_V = C_K + ATT_COLS
C_G = C_V + ATT_COLS
NT = 4
TS = 512
DIL = (1, 4, 16)
EPS = 1e-6
NCORES = 8


class Tk:
    __slots__ = ("w", "rd")

    def __init__(self):
        self.w = None
        self.rd = []


class Op:
    __slots__ = ("eng", "fn", "deps", "need_inc", "val", "dsem", "waits", "cidx")

    def __init__(self, eng, fn):
        self.eng = eng
        self.fn = fn
        self.deps = []
        self.need_inc = False
        self.val = 0
        self.dsem = None
        self.waits = []
        self.cidx = -1


class Sched:
    def __init__(self, nc):
        self.nc = nc
        self.prog = {e: [] for e in ENGS}
        self.dma_keys = {}
        self.pending_bar = {e: None for e in ENGS}

    def op(self, eng, fn, r=(), w=(), dma=None, nobar=False):
        o = Op(eng, fn)
        deps = []
        for t in r:
            if t.w is not None:
                deps.append(t.w)
        for t in w:
            if t.w is not None:
                deps.append(t.w)
            deps.extend(t.rd)
        for t in r:
            t.rd.append(o)
        for t in w:
            t.w = o
            t.rd = []
        if self.pending_bar[eng] is not None and not nobar:
            deps.extend(self.pending_bar[eng])
            self.pending_bar[eng] = None
        seen = set()
        for d in deps:
            if id(d) not in seen and d is not o:
                seen.add(id(d))
                o.deps.append(d)
        self.prog[eng].append(o)
        if dma is not None:
            o.dsem = dma
            self.dma_keys.setdefault(dma, []).append(o)
        return o

    def dma_multi(self, eng, key, fns, r=(), w=(), nobar=False):
        first = self.op(eng, fns[0], r=r, w=w, dma=key, nobar=nobar)
        last = first
        for fn in fns[1:]:
            last = self.op(eng, fn, dma=key, nobar=nobar)
        if last is not first:
            for t in w:
                t.w = last
            for t in r:
                t.rd.append(last)
        return last

    def barrier(self, skip_out=False):
        lasts = []
        for e in ENGS:
            for o in reversed(self.prog[e]):
                if o.dsem is None:
                    lasts.append(o)
                    break
        for k, ops in self.dma_keys.items():
            if str(k).startswith("p_") or (skip_out and str(k).startswith("o")):
                continue
            lasts.append(ops[-1])
        for e in ENGS:
            if e == "pe":
                continue
            cur = self.pending_bar[e] or []
            self.pending_bar[e] = cur + lasts

    def finalize(self):
        def chan(o):
            return o.dsem if o.dsem is not None else o.eng

        for k, ops in self.dma_keys.items():
            for i, o in enumerate(ops):
                o.cidx = i
        for e in ENGS:
            i = 0
            for o in self.prog[e]:
                if o.dsem is None:
                    o.cidx = i
                    i += 1
        for e in ENGS:
            kn = {}
            for o in self.prog[e]:
                best = {}
                for d in o.deps:
                    c = chan(d)
                    if d.dsem is None and d.eng == e and e in ("pe", "sp"):
                        continue
                    if kn.get(c, -1) >= d.cidx:
                        continue
                    if c not in best or best[c].cidx < d.cidx:
                        best[c] = d
                for c, d in best.items():
                    kn[c] = d.cidx
                    d.need_inc = True
                    o.waits.append(d)
        for e in ENGS:
            v = 0
            for o in self.prog[e]:
                if o.dsem is None and o.need_inc:
                    v += 1
                    o.val = v
        for k, ops in self.dma_keys.items():
            for i, o in enumerate(ops):
                o.val = 16 * (i + 1)

    def emit(self):
        nc = self.nc
        self.finalize()
        with contextlib.ExitStack() as st:
            esem = {e: st.enter_context(nc.semaphore("s_" + e)) for e in ENGS}
            dsem = {k: st.enter_context(nc.semaphore("d_%s" % (str(k),))) for k in self.dma_keys}
            block = st.enter_context(nc.Block())

            def semof(o):
                return dsem[o.dsem] if o.dsem is not None else esem[o.eng]

            def run(e):
                def body(eng):
                    for o in self.prog[e]:
                        for d in o.waits:
                            eng.wait_ge(semof(d), d.val)
                        inst = o.fn(eng)
                        if o.dsem is not None:
                            inst.then_inc(dsem[o.dsem], 16)
                        elif o.need_inc:
                            inst.then_inc(esem[e], 1)
                return body

            block.tensor(run("pe"))
            block.scalar(run("act"))
            block.vector(run("dve"))
            block.gpsimd(run("pool"))
            block.sync(run("sp"))


class Arena:
    def __init__(self, nc, st, nbytes):
        self.n = nbytes
        self.t = st.enter_context(nc.sbuf_tensor("arena", [128, nbytes // 4], F32))
        self.off = 0

    def seek(self, kb):
        self.off = int(kb * 1024)

    def alloc(self, shape_free, dtype):
        esz = 2 if dtype == BF16 else 4
        n = int(np.prod(shape_free))
        nb = (n * esz + 31) // 32 * 32
        assert self.off + nb <= self.n, ("arena overflow", self.off, nb, self.n)
        a = self.t[:, self.off // 4:(self.off + nb) // 4]
        self.off += nb
        if dtype != F32:
            a = a.bitcast(dtype)
        a = a[:, 0:n]
        if len(shape_free) == 2:
            a = a.rearrange("p (a b) -> p a b", a=shape_free[0])
        elif len(shape_free) == 3:
            a = a.rearrange("p (a b c) -> p a b c", a=shape_free[0], b=shape_free[1])
        return a


def gate_nbrs(c):
    lo_b = (c * 128) // 80
    hi_b = (c * 128 + 127) // 80
    r0 = lo_b * 80
    r1 = hi_b * 80 + 79
    return list(range(r0 // 128, r1 // 128 + 1))


class _Stop(Exception):
    pass


def build_program(nseq, dbg=False, upto=99):
    nc = bass.Bass("TRN2", target_bir_lowering=False)
    S = Sched(nc)

    def MM(out, lhsT, rhs, start, stop, r, w):
        S.op("pe", lambda e: e.matmul(out, lhsT=lhsT, rhs=rhs, start=start, stop=stop), r=r, w=w)

    def TRN(out, in_, ident, r, w):
        S.op("pe", lambda e: e.transpose(out, in_, ident), r=r, w=w)

    def ACT(out, in_, func, r, w, scale=None, bias=None):
        kw = {}
        if scale is not None:
            kw["scale"] = scale
        if bias is not None:
            kw["bias"] = bias
        S.op("act", lambda e: e.activation(out=out, in_=in_, func=func, **kw), r=r, w=w)

    def STT(eng, out, in0, scalar, in1, op0, op1, r, w):
        S.op(eng, lambda e: e.scalar_tensor_tensor(out=out, in0=in0, scalar=scalar, in1=in1, op0=op0, op1=op1), r=r, w=w)

    def TT(eng, out, in0, in1, op, r, w):
        S.op(eng, lambda e: e.tensor_tensor(out=out, in0=in0, in1=in1, op=op), r=r, w=w)

    def TSC(eng, out, in0, scalar1, op0, r, w):
        S.op(eng, lambda e: e.tensor_scalar(out=out, in0=in0, scalar1=scalar1, scalar2=None, op0=op0), r=r, w=w)

    def RECIP(out, in_, r, w):
        S.op("dve", lambda e: e.reciprocal(out=out, in_=in_), r=r, w=w)

    def MEMSET(eng, ap, val, r, w):
        S.op(eng, lambda e: e.memset(ap, val), r=r, w=w)

    def COPY(eng, out, in_, r, w):
        S.op(eng, lambda e: e.tensor_copy(out=out, in_=in_), r=r, w=w)

    def SCAN(out, d0, d1, r, w):
        S.op("dve", lambda e: e.tensor_tensor_scan(out=out, data0=d0, data1=d1, initial=0.0, op0=ALU.mult, op1=ALU.add),
             r=r, w=w)

    def DMA(eng, key, out, in_, r=(), w=()):
        return S.op(eng, lambda e: e.dma_start(out=out, in_=in_), r=r, w=w, dma=key)

    def dma_fn(out, in_):
        return lambda e: e.dma_start(out=out, in_=in_)

    def din(name, shape, dt=F32):
        return nc.dram_tensor(name, list(shape), dt, kind="ExternalInput").ap()

    xT = din("xT", [nseq, D, S_LEN])
    w_in = din("w_in", [D, IN_COLS])
    w_up = din("w_up", [D, 2 * D_FF])
    w_down = din("w_down", [D_FF, D])
    w_o = din("w_o", [D, D])
    w_lo = din("w_lru_out", [D_RNN, D])
    w_ao = din("w_att_out", [512, D])
    lru_w = din("lru_w", [4, 16, 80, 80])
    pvec = din("pvec", [128, 240])
    cbf = din("cbf", [128, 2896])
    csd = din("cs", [2, 128, S_LEN])
    yT = nc.dram_tensor("yT", [nseq, D, S_LEN], F32, kind="ExternalOutput").ap()

    def dscr(name, shape):
        return nc.dram_tensor(name, list(shape), BF16, kind="Internal").ap()

    b_in = dscr("b_in", [D, IN_COLS])
    b_up = dscr("b_up", [D, 2 * D_FF])
    b_down = dscr("b_down", [D_FF, D])
    b_o = dscr("b_o", [D, D])
    b_lo = dscr("b_lo", [D_RNN, D])
    b_ao = dscr("b_ao", [512, D])
    b_g = dscr("b_g", [4, D_RNN, D_RNN])
    dbg_outs = {}

    st = contextlib.ExitStack()
    with st:
        ar = Arena(nc, st, 200 * 1024)
        ps = [st.enter_context(nc.psum_tensor("ps%d" % i, [128, 512], F32)) for i in range(8)]
        pst = [Tk() for _ in range(8)]
        pstate = {"b": 0, "q": 0}

        def bank(excl=()):
            b = pstate["b"]
            while b in excl:
                b = (b + 1) % 8
            pstate["b"] = (b + 1) % 8
            return b

        def quad():
            q = pstate["q"]
            pstate["q"] = 1 - q
            return [4 * q + i for i in range(4)]

        ar.seek(0)
        pv = ar.alloc([240], F32)
        t_pv = Tk()
        der = ar.alloc([96], F32)
        t_der = Tk()
        cb = ar.alloc([2896], BF16)
        t_cb = Tk()
        ring = [(ar.alloc([4096], BF16), Tk()) for _ in range(3)]
        rstate = {"i": 0}
        P_END = ar.off
        assert P_END <= 31 * 1024, P_END

        PV_G1, PV_G2 = 0, 8
        PV_LCW, PV_LCB = 16, 56
        PV_LAM, PV_BA, PV_BX = 66, 86, 106
        PV_FCW, PV_FCB = 126, 198
        PV_QG, PV_QGP = 222, 228
        PV_EPS, PV_ONE = 234, 235
        DR_HC, DR_C2, DR_HBA, DR_HBX = 0, 20, 40, 60
        CB_ONES, CB_BLK, CB_PERM, CB_O64, CB_MASK, CB_L0, CB_L1, CB_ID, CB_IND = 0, 128, 256, 384, 448, 2496, 2624, 2752, 2880

        def col(a, i):
            return a[:, i:i + 1]

        def tl(t):
            return slice(t * TS, (t + 1) * TS)

        t_bin, t_bup, t_bdown, t_bo, t_blo, t_bao, t_bg = (Tk() for _ in range(7))

        def prep_cast(key, dst, src, rows, tk, nsplit):
            step = rows // nsplit
            fns = [dma_fn(dst[i * step:(i + 1) * step, :], src[i * step:(i + 1) * step, :]) for i in range(nsplit)]
            S.dma_multi("pool", key, fns, w=[tk])

        DMA("sp", "c_pv", pv, pvec, w=[t_pv])
        DMA("pool", "c_cb", cb, cbf, w=[t_cb])
        t_bin_grp = [Tk(), Tk(), Tk()]
        BIN_COLS = [(0, 2560), (2560, C_G), (C_G, IN_COLS)]
        for gi_, (c0_, c1_) in enumerate(BIN_COLS[:1]):
            fns_ = [dma_fn(b_in[i * 256:(i + 1) * 256, c0_:c1_], w_in[i * 256:(i + 1) * 256, c0_:c1_]) for i in range(4)]
            S.dma_multi("pool", "p_in%d" % gi_, fns_, w=[t_bin_grp[gi_]])
        zt = ring[0][0]
        MEMSET("dve", zt[:, 0:D_RNN], 0.0, r=[], w=[ring[0][1]])
        fns = [dma_fn(b_g[dg, k * 128:(k + 1) * 128, :], zt[:, 0:D_RNN]) for dg in range(4) for k in range(10)]
        t_bgz = Tk()
        S.dma_multi("sp", "p_gz", fns, r=[ring[0][1]], w=[t_bgz])
        fns = [dma_fn(b_g[dg, b * 80:(b + 1) * 80, b * 80:(b + 1) * 80], lru_w[dg, b]) for dg in range(4) for b in range(16)]
        S.dma_multi("pool", "p_g", fns, r=[t_bgz], w=[t_bg])
        for gi_, (c0_, c1_) in list(enumerate(BIN_COLS))[1:]:
            fns_ = [dma_fn(b_in[i * 256:(i + 1) * 256, c0_:c1_], w_in[i * 256:(i + 1) * 256, c0_:c1_]) for i in range(4)]
            S.dma_multi("pool", "p_in%d" % gi_, fns_, w=[t_bin_grp[gi_]])
        prep_cast("p_lo", b_lo, w_lo, D_RNN, t_blo, 2)
        prep_cast("p_ao", b_ao, w_ao, 512, t_bao, 1)
        prep_cast("p_o", b_o, w_o, D, t_bo, 2)
        prep_cast("p_up", b_up, w_up, D, t_bup, 8)
        prep_cast("p_down", b_down, w_down, D_FF, t_bdown, 4)

        ACT(der[:, 0:20], pv[:, PV_LAM:PV_LAM + 20], AF.Exp, r=[t_pv], w=[t_der], scale=-1.0)
        ACT(der[:, 0:20], der[:, 0:20], AF.Ln, r=[t_pv, t_der], w=[t_der], bias=col(pv, PV_ONE))
        TSC("dve", der[:, DR_C2:DR_C2 + 20], der[:, 0:20], -8.0, ALU.mult, r=[t_der], w=[t_der])
        TSC("dve", der[:, DR_HC:DR_HC + 20], der[:, 0:20], -4.0, ALU.mult, r=[t_der], w=[t_der])
        TSC("dve", der[:, DR_HBA:DR_HBA + 40], pv[:, PV_BA:PV_BA + 40], 0.5, ALU.mult, r=[t_pv, t_der], w=[t_der])

        def load_w(src, t_src, k0, nk, c0, ncols):
            if t_src is t_bin:
                t_src = t_bin_grp[0] if c0 < 2560 else (t_bin_grp[1] if c0 < C_G else t_bin_grp[2])
            slot = rstate["i"] % 3
            rstate["i"] += 1
            buf, tk = ring[slot]
            view = buf[:, 0:nk * ncols].rearrange("p (k c) -> p k c", k=nk)
            srcv = src[k0 * 128:(k0 + nk) * 128, c0:c0 + ncols].rearrange("(k p) c -> p k c", p=128)
            S.op("sp", dma_fn(view, srcv), r=[t_src], w=[tk], dma="w%d" % slot, nobar=True)
            return view, tk

        def load_gate_w(c):
            nb = gate_nbrs(c)
            nk = len(nb)
            slot = rstate["i"] % 3
            rstate["i"] += 1
            buf, tk = ring[slot]
            view = buf[:, 0:4 * nk * 128].rearrange("p (g k c) -> p g k c", g=4, k=nk)
            fns = []
            for dg in range(4):
                srcv = b_g[dg, nb[0] * 128:(nb[0] + nk) * 128, c * 128:(c + 1) * 128].rearrange("(k p) c -> p k c", p=128)
                fns.append(dma_fn(view[:, dg], srcv))
            S.dma_multi("sp", "w%d" % slot, fns, r=[t_bg], w=[tk], nobar=True)
            return view, tk, nb

        def proj4(wt, t_w, nk, j, src, src_tks, q=None, t_outer=True):
            if q is None:
                q = quad()
            order = [(k, t) for t in range(NT) for k in range(nk)] if t_outer else [(k, t) for k in range(nk) for t in range(NT)]
            for (k, t) in order:
                MM(ps[q[t]][:], wt[:, k, j * 128:(j + 1) * 128], src[:, k, tl(t)], k == 0, k == nk - 1,
                   r=[t_w, src_tks[k][t]], w=[pst[q[t]]])
            return q

        dbgc = {"i": 0}

        def dump(name, ap, tks, shape, dt=F32):
            if not dbg:
                return
            o = nc.dram_tensor("dbg_" + name, list(shape), dt, kind="ExternalOutput").ap()
            dbg_outs[name] = 1
            DMA("sp", "dbg%d" % dbgc["i"], o, ap, r=tks, w=[Tk()])
            dbgc["i"] += 1

        KB_XN = 31
        KB_MA = 63
        KB_SCR = 95
        ar.seek(KB_XN)
        xn = ar.alloc([8, S_LEN], BF16)
        ar.seek(KB_MA)
        ma = ar.alloc([8, S_LEN], BF16)
        ar.seek(136)
        x1 = ar.alloc([8, S_LEN], F32)

        def seq(s):
            t_xn = [[Tk() for _ in range(NT)] for _ in range(8)]
            t_ma = [[Tk() for _ in range(NT)] for _ in range(8)]
            t_x1 = [[Tk() for _ in range(NT)] for _ in range(8)]
            S.barrier(skip_out=True)
            ar.seek(KB_SCR)
            xt = [ar.alloc([8, TS], F32) for _ in range(2)]
            t_xt = [Tk(), Tk()]
            sq = ar.alloc([8, TS], BF16)
            t_sq = Tk()
            assert ar.off <= 136 * 1024
            ar.seek(KB_MA)
            rs = ar.alloc([TS], F32)
            t_rs = Tk()
            for t in range(NT):
                b = t % 2
                DMA("sp", "x%d" % b, xt[b], xT[s, :, tl(t)].rearrange("(k p) n -> p k n", p=128), w=[t_xt[b]])
                ACT(sq, xt[b], AF.Square, r=[t_xt[b]], w=[t_sq])
                pb = bank()
                for k in range(8):
                    MM(ps[pb][:], cb[:, CB_ONES:CB_ONES + 128], sq[:, k, :], k == 0, k == 7, r=[t_sq, t_cb], w=[pst[pb]])
                ACT(rs, ps[pb][:], AF.Sqrt, r=[pst[pb], t_pv], w=[t_rs], bias=col(pv, PV_EPS))
                RECIP(rs, rs, r=[t_rs], w=[t_rs])
                for k in range(8):
                    STT("dve", xn[:, k, tl(t)], xt[b][:, k, :], col(pv, PV_G1 + k), rs, ALU.mult, ALU.mult,
                        r=[t_xt[b], t_rs, t_pv], w=[t_xn[k][t]])
            if s == 0:
                dump("xn", xn, [t_xn[k][t] for k in range(8) for t in range(NT)], [128, 8, S_LEN], BF16)

            if upto <= 1:
                raise _Stop()
            S.barrier()
            ar.seek(KB_MA)
            Abuf = ar.alloc([2, S_LEN], F32)
            S2 = ar.alloc([2, S_LEN], F32)
            ar.seek(KB_SCR)
            y = ar.alloc([10, S_LEN], BF16)
            xcb = ar.alloc([5, S_LEN], BF16)
            lx = ar.alloc([S_LEN + 8], F32)
            U = ar.alloc([2, S_LEN], F32)
            gg = ar.alloc([S_LEN], F32)
            TR = [ar.alloc([TS], F32) for _ in range(2)]
            TI = [ar.alloc([TS], F32) for _ in range(2)]
            t_y = [[Tk() for _ in range(NT)] for _ in range(10)]
            t_lx, t_gg = Tk(), Tk()
            t_A = [[Tk() for _ in range(NT)] for _ in range(2)]
            t_S2 = [[Tk() for _ in range(NT)] for _ in range(2)]
            t_U = [[Tk() for _ in range(NT)] for _ in range(2)]
            A_all = t_A[0] + t_A[1]
            S2_all = t_S2[0] + t_S2[1]
            U_all = t_U[0] + t_U[1]
            t_TR = [Tk(), Tk()]
            t_TI = [Tk(), Tk()]
            MEMSET("dve", lx, 0.0, r=[], w=[t_lx])
            tcnt = 0
            t_xcb = [[Tk() for _ in range(NT)] for _ in range(5)]
            for sb in range(2):
                wt_a, tk_a = load_w(b_in, t_bin, 0, 8, sb * 640, 512)
                wt_b, tk_b = load_w(b_in, t_bin, 0, 8, sb * 640 + 512, 128)
                for cl in range(5):
                    c = sb * 5 + cl
                    wt, tkw, j = (wt_a, tk_a, cl) if cl < 4 else (wt_b, tk_b, 0)
                    q = proj4(wt, tkw, 8, j, xn, t_xn)
                    for t in range(NT):
                        ACT(lx[:, 2 + t * TS:2 + (t + 1) * TS], ps[q[t]][:], AF.Copy, r=[pst[q[t]]], w=[t_lx])
                    tmp = U[:, 0, :]
                    ACT(tmp, lx[:, 0:S_LEN], AF.Identity, r=[t_lx, t_pv], w=t_U[0],
                        scale=col(pv, PV_LCW + c * 4 + 0), bias=col(pv, PV_LCB + c))
                    for jj in (1, 2):
                        STT("dve", tmp, lx[:, jj:jj + S_LEN], col(pv, PV_LCW + c * 4 + jj), tmp, ALU.mult, ALU.add,
                            r=[t_lx, t_pv] + t_U[0], w=t_U[0])
                    STT("dve", xcb[:, cl, :], lx[:, 3:3 + S_LEN], col(pv, PV_LCW + c * 4 + 3), tmp, ALU.mult, ALU.add,
                        r=[t_lx, t_pv] + t_U[0], w=t_xcb[cl])
                if s == 0 and sb == 0:
                    dump("xcb", xcb, [t_xcb[c][t] for c in range(5) for t in range(NT)], [128, 5, S_LEN], BF16)
                if upto <= 2:
                    raise _Stop()
                for cl in range(5):
                    c = sb * 5 + cl
                    gw, tkg, nb = load_gate_w(c)
                    nk = len(nb)
                    for dr in range(2):
                        idx = dr * 10 + c
                        for t in range(NT):
                            pr, pi = bank(), bank()
                            for (pb, gi) in ((pr, 0), (pi, 1)):
                                for ki in range(nk):
                                    icl = nb[ki] - sb * 5
                                    MM(ps[pb][:], gw[:, dr * 2 + gi, ki, :], xcb[:, icl, tl(t)], ki == 0, ki == nk - 1,
                                       r=[tkg, t_xcb[icl][t]], w=[pst[pb]])
                            i2 = tcnt % 2
                            tcnt += 1
                            ACT(TR[i2], ps[pr][:], AF.Tanh, r=[pst[pr], t_der], w=[t_TR[i2]], scale=0.5, bias=col(der, DR_HBA + idx))
                            ACT(Abuf[:, dr, tl(t)], TR[i2], AF.Exp, r=[t_TR[i2], t_der], w=[t_A[dr][t]],
                                scale=col(der, DR_HC + idx), bias=col(der, DR_HC + idx))
                            TT("pool", S2[:, dr, tl(t)], Abuf[:, dr, tl(t)], Abuf[:, dr, tl(t)], ALU.mult,
                               r=[t_A[dr][t]], w=[t_S2[dr][t]])
                            ACT(U[:, dr, tl(t)], ps[pi][:], AF.Tanh, r=[pst[pi], t_der], w=[t_U[dr][t]], scale=0.5,
                                bias=col(der, DR_HBX + idx))
                            STT("dve", U[:, dr, tl(t)], U[:, dr, tl(t)], 1.0, xcb[:, cl, tl(t)], ALU.add, ALU.mult,
                                r=[t_U[dr][t], t_xcb[cl][t]], w=[t_U[dr][t]])
                    ACT(S2, S2, AF.Sqrt, r=[t_pv] + S2_all, w=S2_all, scale=-1.0, bias=col(pv, PV_ONE))
                    STT("dve", U, U, 0.5, S2, ALU.mult, ALU.mult, r=U_all + S2_all, w=U_all)
                    SCAN(S2[:, 0, :], Abuf[:, 0, :], U[:, 0, :], r=t_A[0] + t_U[0] + t_S2[0], w=t_S2[0])
                    SCAN(S2[:, 1, ::-1], Abuf[:, 1, ::-1], U[:, 1, ::-1], r=t_A[1] + t_U[1] + t_S2[1], w=t_S2[1])
                    wtg, tkwg = load_w(b_in, t_bin, 0, 8, D_RNN + c * 128, 128)
                    q = proj4(wtg, tkwg, 8, 0, xn, t_xn)
                    for t in range(NT):
                        ACT(gg[:, tl(t)], ps[q[t]][:], AF.Gelu_apprx_tanh, r=[pst[q[t]]], w=[t_gg])
                    TT("dve", lx[:, 2:2 + S_LEN], S2[:, 0, :], S2[:, 1, :], ALU.add, r=S2_all, w=[t_lx])
                    TT("dve", y[:, c, :], lx[:, 2:2 + S_LEN], gg, ALU.mult, r=[t_lx, t_gg], w=t_y[c])
            if s == 0:
                dump("y", y, [t_y[c][t] for c in range(10) for t in range(NT)], [128, 10, S_LEN], BF16)
            if upto <= 3:
                raise _Stop()
            S.barrier()
            TG = U[:, 0, :]
            t_TG = t_U[0][0]
            for oc2 in range(4):
                wl, tkl = load_w(b_lo, t_blo, 0, 10, oc2 * 256, 256)
                wg_, tkg_ = load_w(b_in, t_bin, 0, 8, C_G + oc2 * 256, 256)
                for j in range(2):
                    oc = oc2 * 2 + j
                    qg_ = proj4(wg_, tkg_, 8, j, xn, t_xn)
                    for t in range(NT):
                        ACT(TG[:, tl(t)], ps[qg_[t]][:], AF.Tanh, r=[pst[qg_[t]]], w=[t_TG], scale=0.5)
                    qa = proj4(wl, tkl, 10, j, y, t_y)
                    for t in range(NT):
                        STT("dve", ma[:, oc, tl(t)], TG[:, tl(t)], 1.0, ps[qa[t]][:], ALU.add, ALU.mult,
                            r=[t_TG, pst[qa[t]]], w=[t_ma[oc][t]])
            if s == 0:
                dump("ma", ma, [t_ma[c][t] for c in range(8) for t in range(NT)], [128, 8, S_LEN], BF16)

            if upto <= 4.05:
                raise _Stop()
            S.barrier()
            ar.seek(KB_SCR)
            cosT = ar.alloc([S_LEN], F32)
            sinT = ar.alloc([S_LEN], F32)
            accN = ar.alloc([S_LEN], F32)
            accD = ar.alloc([S_LEN], F32)
            QT = ar.alloc([S_LEN + 128], BF16)
            KTh = [ar.alloc([S_LEN + 128], BF16) for _ in range(2)]
            Vth = [ar.alloc([32, 128], BF16) for _ in range(2)]
            VT = ar.alloc([S_LEN + 128], BF16)
            t_VT = Tk()
            obf = ar.alloc([4, S_LEN], BF16)
            sqb = [ar.alloc([TS], BF16) for _ in range(2)]
            qrb = [ar.alloc([TS], BF16) for _ in range(2)]
            srp = [ar.alloc([8], F32) for _ in range(2)]
            Bc = [ar.alloc([8, 64], BF16) for _ in range(2)]
            t1b = [ar.alloc([TS], F32) for _ in range(4)]
            t2b = [ar.alloc([TS], F32) for _ in range(2)]
            Pb = [ar.alloc([TS], BF16) for _ in range(4)]
            assert ar.off <= 200 * 1024
            t_cs, t_accN, t_accD, t_QT, t_KT, t_V = Tk(), Tk(), Tk(), Tk(), Tk(), Tk()
            t_obf = [[Tk() for _ in range(NT)] for _ in range(4)]
            t_sqb, t_qrb, t_rst, t_t2, t_Bc = ([Tk(), Tk()] for _ in range(5))
            t_t1 = [Tk() for _ in range(4)]
            t_P = [Tk(), Tk(), Tk(), Tk()]
            t_Vt = [Tk() for _ in range(32)]
            S.dma_multi("act", "cs", [dma_fn(cosT, csd[0]), dma_fn(sinT, csd[1])], w=[t_cs])
            MEMSET("pool", QT, 0.0, r=[], w=[t_QT])
            MEMSET("pool", VT, 0.0, r=[], w=[t_VT])
            for h_ in range(2):
                MEMSET("pool", KTh[h_], 0.0, r=[], w=[t_KT])
                MEMSET("pool", Vth[h_], 0.0, r=[], w=t_Vt)
            ncnt = [0]
            pcnt = [0]
            bB = [0]
            bA = [0]

            def bankB():
                v = 4 + bB[0] % 4
                bB[0] += 1
                return v

            def bankA():
                v = bA[0] % 4
                bA[0] += 1
                return v

            QA = [0, 1, 2, 3]

            def combine_piece(hp_, t):
                RECIP(accD[:, tl(t)], accD[:, tl(t)], r=[t_accD], w=[t_accD])
                if t == NT - 1:
                    TT("pool", obf[:, hp_, :], accN, accD, ALU.mult, r=[t_accN, t_accD], w=t_obf[hp_])

            def combine(hp_):
                def mult(t):
                    TT("pool", obf[:, hp_, tl(t)], accN[:, tl(t)], accD[:, tl(t)], ALU.mult, r=[t_accN, t_accD],
                       w=[t_obf[hp_][t]])
                for t in range(NT):
                    RECIP(accD[:, tl(t)], accD[:, tl(t)], r=[t_accD], w=[t_accD])
                    if t >= 1:
                        mult(t - 1)
                mult(NT - 1)

            for hp in range(4):
                for g in range(3):
                    d = DIL[g]
                    L = S_LEN // d
                    ntl = L // 128 + 1
                    wq, tkq = load_w(b_in, t_bin, 0, 8, C_Q + g * 512 + hp * 128, 128)
                    wk, tkk = load_w(b_in, t_bin, 0, 8, C_K + g * 512 + hp * 128, 128)
                    wv, tkv = load_w(b_in, t_bin, 0, 8, C_V + g * 512 + hp * 128, 128)
                    QB = [4, 5, 6, 7]

                    def v_proj():
                        qv = proj4(wv, tkv, 8, 0, xn, t_xn, q=QB)
                        body = VT[:, 64:64 + S_LEN]
                        for t in range(NT):
                            if d == 1:
                                ov = body[:, tl(t)]
                                iv = ps[qv[t]][:]
                            else:
                                w_ = TS // d
                                ov = body.rearrange("p (m q) -> p m q", m=d)[:, :, t * w_:(t + 1) * w_]
                                iv = ps[qv[t]][:].rearrange("p (j m) -> p m j", m=d)
                            ACT(ov, iv, AF.Copy, r=[pst[qv[t]]], w=[t_VT])

                    def v_tiles():
                        ntiles = d * ntl
                        for g0 in range(0, ntiles, 8):
                            n = min(8, ntiles - g0)
                            pb = bankB()
                            psv = ps[pb][:].bitcast(BF16)
                            for si in range(n):
                                tid = g0 + si
                                c0 = (tid // ntl) * L + 128 * (tid % ntl)
                                TRN(psv[:, si * 128:(si + 1) * 128], VT[:, c0:c0 + 128], cb[:, CB_ID:CB_ID + 128],
                                    r=[t_VT, t_cb], w=[pst[pb]])
                            for h_ in range(2):
                                ACT(Vth[h_][:, g0:g0 + n, h_ * 64:(h_ + 1) * 64],
                                    psv[:, 0:n * 128].rearrange("p (s c) -> p s c", s=n)[:, :, h_ * 64:(h_ + 1) * 64], AF.Copy,
                                    r=[pst[pb]], w=[t_Vt[tid] for tid in range(g0, g0 + n)])

                    st_ = {}

                    def S0(which, q, t):
                        gi = which * 3 + g
                        i2 = t % 2
                        ACT(sqb[i2], ps[q[t]][:], AF.Square, r=[pst[q[t]]], w=[t_sqb[i2]])
                        ACT(qrb[i2], ps[q[t]][:], AF.Copy, r=[pst[q[t]]], w=[t_qrb[i2]])
                        STT("dve", t1b[t], ps[q[t]][:], col(pv, PV_QG + gi), cosT[:, tl(t)], ALU.mult, ALU.mult,
                            r=[pst[q[t]], t_pv, t_cs, t_qrb[i2]], w=[t_t1[t]])

                    def PEn(which, t):
                        i2 = t % 2
                        pm, pr = bankB(), bankB()
                        st_[(which, t)] = (pm, pr)
                        MM(ps[pr][:], cb[:, CB_PERM:CB_PERM + 128], qrb[i2], True, True, r=[t_cb, t_qrb[i2]], w=[pst[pr]])
                        for bl in range(4):
                            MM(ps[pm][:, 2 * bl:2 * bl + 2], sqb[i2][:, bl * 128:(bl + 1) * 128], cb[:, CB_IND:CB_IND + 2],
                               True, True, r=[t_cb, t_sqb[i2]], w=[pst[pm]])

                    def rest(which, pair):
                        gi = which * 3 + g
                        t_dst = t_QT if which == 0 else t_KT
                        for t in pair:
                            i2 = t % 2
                            pm, pr = st_[(which, t)]
                            STT("dve", t2b[i2], ps[pr][:], col(pv, PV_QGP + gi), sinT[:, tl(t)], ALU.mult, ALU.mult,
                                r=[pst[pr], t_pv, t_cs], w=[t_t2[i2]])
                            ACT(srp[i2], ps[pm][:, 0:8], AF.Sqrt, r=[pst[pm], t_pv], w=[t_rst[i2]], bias=col(pv, PV_EPS))
                        for t in pair:
                            i2 = t % 2
                            RECIP(srp[i2], srp[i2], r=[t_rst[i2]], w=[t_rst[i2]])
                            TT("pool", t1b[t], t1b[t], t2b[i2], ALU.add, r=[t_t1[t], t_t2[i2]], w=[t_t1[t]])
                        for t in pair:
                            i2 = t % 2
                            ACT(Bc[i2], srp[i2].unsqueeze(2).broadcast_to([128, 8, 64]), AF.Copy, r=[t_rst[i2]], w=[t_Bc[i2]])
                        for t in pair:
                            i2 = t % 2
                            pm, pr = st_[(which, t)]
                            pbc = ps[pm][:].bitcast(BF16)[:, 32:32 + TS]
                            for bl in range(4):
                                TRN(pbc[:, bl * 128:(bl + 1) * 128], Bc[i2][:, 2 * bl:2 * bl + 2, :].rearrange("p a b -> p (a b)"),
                                    cb[:, CB_ID:CB_ID + 128], r=[t_Bc[i2], t_cb, pst[pm]], w=[pst[pm]])
                        for t in pair:
                            pm, pr = st_[(which, t)]
                            pbc = ps[pm][:].bitcast(BF16)[:, 32:32 + TS]
                            parts = [(QT, 0, 128)] if which == 0 else [(KTh[0], 0, 64), (KTh[1], 64, 128)]
                            for (dstb, p0, p1) in parts:
                                body = dstb[p0:p1, 64:64 + S_LEN]
                                if d == 1:
                                    ov = body[:, tl(t)]
                                    i0v = t1b[t][p0:p1]
                                    i1v = pbc[p0:p1]
                                else:
                                    w_ = TS // d
                                    ov = body.rearrange("p (m q) -> p m q", m=d)[:, :, t * w_:(t + 1) * w_]
                                    i0v = t1b[t][p0:p1].rearrange("p (j m) -> p m j", m=d)
                                    i1v = pbc[p0:p1].rearrange("p (j m) -> p m j", m=d)
                                TT("dve", ov, i0v, i1v, ALU.mult, r=[t_t1[t], pst[pm]], w=[t_dst])

                    def v_evac(qv):
                        body = VT[:, 64:64 + S_LEN]
                        for t in range(NT):
                            if d == 1:
                                ov = body[:, tl(t)]
                                iv = ps[qv[t]][:]
                            else:
                                w_ = TS // d
                                ov = body.rearrange("p (m q) -> p m q", m=d)[:, :, t * w_:(t + 1) * w_]
                                iv = ps[qv[t]][:].rearrange("p (j m) -> p m j", m=d)
                            ACT(ov, iv, AF.Copy, r=[pst[qv[t]]], w=[t_VT])

                    qq = proj4(wq, tkq, 8, 0, xn, t_xn, q=QA, t_outer=True)
                    S0(0, qq, 0)
                    S0(0, qq, 1)
                    PEn(0, 0)
                    PEn(0, 1)
                    S0(0, qq, 2)
                    S0(0, qq, 3)
                    dfr = (g == 0 and hp > 0)
                    qk_ = proj4(wk, tkk, 8, 0, xn, t_xn, q=QA, t_outer=True)
                    if dfr:
                        combine_piece(hp - 1, 0)
                    rest(0, (0, 1))
                    PEn(0, 2)
                    PEn(0, 3)
                    if dfr:
                        combine_piece(hp - 1, 1)
                    rest(0, (2, 3))
                    if dfr:
                        combine_piece(hp - 1, 2)
                    S0(1, qk_, 0)
                    S0(1, qk_, 1)
                    PEn(1, 0)
                    PEn(1, 1)
                    S0(1, qk_, 2)
                    S0(1, qk_, 3)
                    if dfr:
                        combine_piece(hp - 1, 3)
                    qv_ = proj4(wv, tkv, 8, 0, xn, t_xn, q=QA, t_outer=True)
                    rest(1, (0, 1))
                    v_evac(qv_)
                    v_tiles()
                    PEn(1, 2)
                    PEn(1, 3)
                    rest(1, (2, 3))
                    if s == 0 and hp == 0:
                        dump("QT%d" % g, QT, [t_QT], [128, S_LEN + 128], BF16)
                        dump("KT%d" % g, KTh[0], [t_KT], [128, S_LEN + 128], BF16)
                    blocks = []
                    for qb in range(16):
                        pi0 = qb * 128
                        m = pi0 // L
                        i0 = pi0 % L
                        first = (i0 == 0)
                        last = (i0 == L - 128)
                        mk = (3 if last else 1) if first else (2 if last else 0)
                        tA = m * ntl + i0 // 128
                        blocks.append(dict(pi0=pi0, mk=mk, tA=tA, tB=tA + 1, kA=pi0, kB=pi0 + 128))

                    def s_stage(bk):
                        psb = bankA()
                        for h in range(2):
                            for (seg, kc0) in ((2 * h, bk["kA"]), (2 * h + 1, bk["kB"])):
                                MM(ps[psb][:, seg * 128:(seg + 1) * 128], KTh[h][:, kc0:kc0 + 128],
                                   QT[:, 64 + bk["pi0"]:64 + bk["pi0"] + 128], True, True, r=[t_KT, t_QT], w=[pst[psb]])
                        p3 = pcnt[0] % 4
                        pcnt[0] += 1
                        bk["p3"] = p3
                        ACT(Pb[p3], ps[psb][:], AF.Exp, r=[pst[psb]], w=[t_P[p3]], scale=0.125)
                        mk = bk["mk"]
                        TT("pool" if os.environ.get("K_PMASK") else "dve", Pb[p3], Pb[p3],
                           cb[:, CB_MASK + mk * 512:CB_MASK + (mk + 1) * 512], ALU.mult, r=[t_P[p3], t_cb], w=[t_P[p3]])

                    def pv_stage(bk, pn, pd, qi):
                        p3 = bk["p3"]
                        for (seg, h, tid) in ((0, 0, bk["tA"]), (1, 0, bk["tB"]), (2, 1, bk["tA"]), (3, 1, bk["tB"])):
                            MM(ps[pn][:, qi * 128:(qi + 1) * 128], Vth[h][:, tid, :],
                               Pb[p3][:, seg * 128:(seg + 1) * 128], seg == 0, seg == 3, r=[t_Vt[tid], t_P[p3]], w=[pst[pn]])
                        for (seg, h) in ((0, 0), (1, 0), (2, 1), (3, 1)):
                            cl_ = CB_L0 if h == 0 else CB_L1
                            MM(ps[pd][:, qi * 128:(qi + 1) * 128], cb[:, cl_:cl_ + 128],
                               Pb[p3][:, seg * 128:(seg + 1) * 128], seg == 0, seg == 3, r=[t_cb, t_P[p3]], w=[pst[pd]])

                    s_stage(blocks[0])
                    s_stage(blocks[1])
                    s_stage(blocks[2])
                    for qb in range(16):
                        qb4, qi = qb // 4, qb % 4
                        pn, pd = (4, 5) if qb4 % 2 == 0 else (6, 7)
                        pv_stage(blocks[qb], pn, pd, qi)
                        if qb + 3 < 16:
                            s_stage(blocks[qb + 3])
                        if qi == 3:
                            for (acc, t_acc, pb) in ((accN, t_accN, pn), (accD, t_accD, pd)):
                                if d == 1:
                                    av = acc[:, qb4 * 512:(qb4 + 1) * 512]
                                    pv_ = ps[pb][:]
                                elif d == 4:
                                    av = acc[:, qb4:S_LEN:4]
                                    pv_ = ps[pb][:]
                                else:
                                    av = acc.rearrange("p (j m) -> p m j", m=16)[:, 4 * qb4:4 * qb4 + 4, :]
                                    pv_ = ps[pb][:].rearrange("p (m j) -> p m j", m=4)
                                if g == 0:
                                    COPY("dve", av, pv_, r=[pst[pb]], w=[t_acc])
                                else:
                                    TT("dve", av, pv_, av, ALU.add, r=[pst[pb], t_acc], w=[t_acc])
            combine(3)
            if s == 0:
                dump("obf", obf, [t_obf[c][t] for c in range(4) for t in range(NT)], [128, 4, S_LEN], BF16)
            TG = accN
            t_TG = t_accN
            TB = accD
            t_TB = t_accD
            for oc2 in range(4):
                wa_, tka_ = load_w(b_ao, t_bao, 0, 4, oc2 * 256, 256)
                wg_, tkg_ = load_w(b_in, t_bin, 0, 8, C_G + D + oc2 * 256, 256)
                for j in range(2):
                    oc = oc2 * 2 + j
                    qg_ = proj4(wg_, tkg_, 8, j, xn, t_xn)
                    for t in range(NT):
                        ACT(TG[:, tl(t)], ps[qg_[t]][:], AF.Tanh, r=[pst[qg_[t]]], w=[t_TG], scale=0.5)
                    qa = proj4(wa_, tka_, 4, j, obf, t_obf)
                    for t in range(NT):
                        STT("dve", TB[:, tl(t)], TG[:, tl(t)], 1.0, ps[qa[t]][:], ALU.add, ALU.mult,
                            r=[t_TG, pst[qa[t]]], w=[t_TB])
                        TT("pool", ma[:, oc, tl(t)], ma[:, oc, tl(t)], TB[:, tl(t)], ALU.add,
                           r=[t_TB, t_ma[oc][t]], w=[t_ma[oc][t]])
            if s == 0:
                dump("m", ma, [t_ma[c][t] for c in range(8) for t in range(NT)], [128, 8, S_LEN], BF16)

            if upto <= 5:
                raise _Stop()
            S.barrier()
            xn2 = xn
            t_xn2 = t_xn
            ar.seek(KB_MA)
            hh = ar.alloc([12, S_LEN], BF16)
            graw = ar.alloc([S_LEN + 8], F32)
            gcv = ar.alloc([S_LEN], F32)
            assert ar.off <= 136 * 1024, ar.off
            t_hh = [[Tk() for _ in range(NT)] for _ in range(12)]
            t_graw, t_gcv = Tk(), Tk()
            sq2 = graw.bitcast(BF16)[:, 0:8 * TS].rearrange("p (k n) -> p k n", k=8)
            rs2 = gcv[:, 0:TS]
            wo_ = [load_w(b_o, t_bo, 0, 8, h_ * 512, 512) for h_ in range(2)]

            def norm2_tile(t):
                ACT(sq2, x1[:, :, tl(t)], AF.Square, r=[t_x1[k][t] for k in range(8)], w=[t_graw])
                pb = bank()
                for k in range(8):
                    MM(ps[pb][:], cb[:, CB_ONES:CB_ONES + 128], sq2[:, k, :], k == 0, k == 7, r=[t_graw, t_cb], w=[pst[pb]])
                ACT(rs2, ps[pb][:], AF.Sqrt, r=[pst[pb], t_pv], w=[t_gcv], bias=col(pv, PV_EPS))
                RECIP(rs2, rs2, r=[t_gcv], w=[t_gcv])
                for k in range(8):
                    STT("dve", xn2[:, k, tl(t)], x1[:, k, tl(t)], col(pv, PV_G2 + k), rs2, ALU.mult, ALU.mult,
                        r=[t_x1[k][t], t_gcv, t_pv], w=[t_xn2[k][t]])

            for oc in range(8):
                DMA("sp", "xr%d" % oc, x1[:, oc, :], xT[s, oc * 128:(oc + 1) * 128, :], w=t_x1[oc])
            for t in range(NT):
                for oc in range(8):
                    wt_, tk_ = wo_[oc // 4]
                    j = oc % 4
                    pb = bank()
                    for k in range(8):
                        MM(ps[pb][:], wt_[:, k, j * 128:(j + 1) * 128], ma[:, k, tl(t)], k == 0, k == 7,
                           r=[tk_, t_ma[k][t]], w=[pst[pb]])
                    STT("dve", x1[:, oc, tl(t)], ps[pb][:], 0.5, x1[:, oc, tl(t)], ALU.mult, ALU.add,
                        r=[pst[pb], t_x1[oc][t]], w=[t_x1[oc][t]])
                if t > 0:
                    norm2_tile(t - 1)
            norm2_tile(NT - 1)
            if s == 0:
                dump("x1", x1, [t_x1[c][t] for c in range(8) for t in range(NT)], [128, 8, S_LEN], F32)

            if upto <= 6:
                raise _Stop()
            MEMSET("dve", graw, 0.0, r=[t_gcv], w=[t_graw])
            for half in range(2):
                for grp in range(3):
                    cbase = half * 12 + grp * 4
                    wgt, tkgt = load_w(b_up, t_bup, 0, 8, cbase * 128, 512)
                    wvl, tkvl = load_w(b_up, t_bup, 0, 8, D_FF + cbase * 128, 512)
                    for j in range(4):
                        c = cbase + j
                        i = grp * 4 + j
                        q = proj4(wgt, tkgt, 8, j, xn2, t_xn2)
                        for t in range(NT):
                            ACT(graw[:, 1 + t * TS:1 + (t + 1) * TS], ps[q[t]][:], AF.Copy, r=[pst[q[t]]], w=[t_graw])
                        ACT(gcv, graw[:, 0:S_LEN], AF.Identity, r=[t_graw, t_pv], w=[t_gcv],
                            scale=col(pv, PV_FCW + c * 3 + 0), bias=col(pv, PV_FCB + c))
                        for jj in (1, 2):
                            STT("dve", gcv, graw[:, jj:jj + S_LEN], col(pv, PV_FCW + c * 3 + jj), gcv, ALU.mult, ALU.add,
                                r=[t_graw, t_gcv, t_pv], w=[t_gcv])
                        ACT(gcv, gcv, AF.Gelu_apprx_tanh, r=[t_gcv], w=[t_gcv])
                        q2 = proj4(wvl, tkvl, 8, j, xn2, t_xn2)
                        for t in range(NT):
                            TT("dve", hh[:, i, tl(t)], ps[q2[t]][:], gcv[:, tl(t)], ALU.mult, r=[pst[q2[t]], t_gcv],
                               w=[t_hh[i][t]] + ([t_ma[i][t]] if i < 8 else []))
                if s == 0 and half == 0:
                    dump("hh", hh, [t_hh[c][t] for c in range(12) for t in range(NT)], [128, 12, S_LEN], BF16)
                for oc2 in range(4):
                    wd_, tkd_ = load_w(b_down, t_bdown, half * 12, 12, oc2 * 256, 256)
                    for j in range(2):
                        oc = oc2 * 2 + j
                        q = proj4(wd_, tkd_, 12, j, hh, t_hh)
                        for t in range(NT):
                            TT("dve", x1[:, oc, tl(t)], ps[q[t]][:], x1[:, oc, tl(t)], ALU.add,
                               r=[pst[q[t]], t_x1[oc][t]], w=[t_x1[oc][t]])
                        if half == 1:
                            DMA("act", "o%d_%d" % (s % 2, oc), yT[s, oc * 128:(oc + 1) * 128, :], x1[:, oc, :], r=t_x1[oc], w=[Tk()])

        for s in range(nseq):
            try:
                if upto >= 1:
                    seq(s)
            except _Stop:
                pass
        S.barrier()
        fin = ar.t[:, KB_SCR * 256:KB_SCR * 256 + 1]
        MEMSET("dve", fin, 0.0, r=[], w=[Tk()])
        S.emit()
    return nc, list(dbg_outs.keys())


def _consts():
    cbf = np.zeros((128, 2896), np.float32)
    cbf[:, 0:128] = 1.0 / 1024.0
    blk = np.zeros((128, 128), np.float32)
    blk[0:64, 0:64] = 1.0 / 64.0
    blk[64:128, 64:128] = 1.0 / 64.0
    cbf[:, 128:256] = blk
    perm = np.zeros((128, 128), np.float32)
    for m in range(128):
        dl = m % 64
        if dl < 8:
            perm[m + 8, m] = 1.0
        elif dl < 16:
            perm[m - 8, m] = 1.0
    cbf[:, 256:384] = perm
    cbf[:, 384:448] = 1.0
    cbf[:, 2496:2496 + 64] = 1.0
    cbf[:, 2624 + 64:2624 + 128] = 1.0
    cbf[:, 2752:2880] = np.eye(128, dtype=np.float32)
    cbf[0:64, 2880] = 1.0 / 64.0
    cbf[64:128, 2881] = 1.0 / 64.0
    kk = np.arange(128)[:, None]
    qq = np.arange(128)[None, :]
    A_gen = (kk >= qq)
    B_gen = (kk <= qq)
    A_first = (kk >= np.maximum(qq, 64))
    B_last = (kk <= np.minimum(qq, 63))
    combos = [(A_gen, B_gen), (A_first, B_gen), (A_gen, B_last), (A_first, B_last)]
    for i, (a, b) in enumerate(combos):
        tile = np.concatenate([a, b, a, b], axis=1).astype(np.float32)
        cbf[:, 448 + i * 512:448 + (i + 1) * 512] = tile
    pos = np.arange(S_LEN, dtype=np.float32)
    inv = (np.float32(500000.0) ** (-np.arange(0, 16, 2, dtype=np.float32) / np.float32(16))).astype(np.float32)
    ang = pos[:, None] * inv[None, :]
    cos = np.cos(ang).astype(np.float32)
    sin = np.sin(ang).astype(np.float32)
    cs = np.zeros((2, 128, S_LEN), np.float32)
    cs[0] = 1.0
    for p in range(128):
        dl = p % 64
        if dl < 8:
            cs[0, p] = cos[:, dl]
            cs[1, p] = -sin[:, dl]
        elif dl < 16:
            cs[0, p] = cos[:, dl - 8]
            cs[1, p] = sin[:, dl - 8]
    return cbf, cs


def _pvec(inp):
    pv = np.zeros((128, 240), np.float32)

    def pm(v, n):
        return np.ascontiguousarray(np.asarray(v, np.float32).reshape(n, 128).T)

    pv[:, 0:8] = pm(inp["norm1_g"][0], 8)
    pv[:, 8:16] = pm(inp["norm2_g"][0], 8)
    lcw = np.asarray(inp["lru_conv_w"][0], np.float32)
    pv[:, 16:56] = np.ascontiguousarray(lcw.reshape(4, 10, 128).transpose(2, 1, 0)).reshape(128, 40)
    pv[:, 56:66] = pm(inp["lru_conv_b"][0], 10)
    for (off, name) in ((66, "lru_lambda"), (86, "lru_ba"), (106, "lru_bx")):
        v = np.asarray(inp[name][0], np.float32)
        pv[:, off:off + 20] = np.ascontiguousarray(v.reshape(2, 10, 128).transpose(2, 0, 1)).reshape(128, 20)
    fcw = np.asarray(inp["ffn_conv_w"][0], np.float32)
    pv[:, 126:198] = np.ascontiguousarray(fcw.reshape(3, 24, 128).transpose(2, 1, 0)).reshape(128, 72)
    pv[:, 198:222] = pm(inp["ffn_conv_b"][0], 24)
    src = np.arange(64)
    src[0:8] = np.arange(8, 16)
    src[8:16] = np.arange(0, 8)
    for which, name in enumerate(("q_norm_g", "k_norm_g")):
        gq = np.asarray(inp[name][0], np.float32)
        for g in range(3):
            pv[:, 222 + which * 3 + g] = np.tile(gq[g], 2)
            pv[:, 228 + which * 3 + g] = np.tile(gq[g][src], 2)
    pv[:, 234] = EPS
    pv[:, 235] = 1.0
    return pv


_PROG = {}


def kernel(**inp):
    xp = np.asarray(inp["x_prompt"], np.float32)
    xs = np.asarray(inp["x_sample"], np.float32)
    nseq = 5
    if nseq not in _PROG:
        _PROG[nseq] = build_program(nseq)[0]
    nc = _PROG[nseq]
    cbf, cs = _consts()
    pv = _pvec(inp)
    lru_w = np.ascontiguousarray(np.stack([inp["lru_wa"][0][0], inp["lru_wx"][0][0], inp["lru_wa"][0][1], inp["lru_wx"][0][1]],
                                          axis=0).astype(np.float32))
    shared = {
        "w_in": np.ascontiguousarray(inp["w_in"][0], np.float32),
        "w_up": np.ascontiguousarray(inp["w_up"][0], np.float32),
        "w_down": np.ascontiguousarray(inp["w_down"][0], np.float32),
        "w_o": np.ascontiguousarray(inp["w_o"][0], np.float32),
        "w_lru_out": np.ascontiguousarray(inp["w_lru_out"][0], np.float32),
        "w_att_out": np.ascontiguousarray(inp["w_att_out"][0], np.float32),
        "lru_w": lru_w, "pvec": pv, "cbf": cbf, "cs": cs,
    }
    in_maps = []
    for i in range(NCORES):
        seqs = [xp[4 * i + j] for j in range(4)] + [xs[i]]
        xT = np.ascontiguousarray(np.stack([q.T for q in seqs], axis=0))
        m = dict(shared)
        m["xT"] = xT
        in_maps.append(m)
    res = run_bass_kernel_spmd(nc, in_maps, core_ids=list(range(NCORES)))
    yp = np.empty_like(xp)
    ys = np.empty_like(xs)
    for i in range(NCORES):
        yT = np.asarray(res.results[i]["yT"])
        for j in range(4):
            yp[4 * i + j] = yT[j].T
        ys[i] = yT[4].T
    return (yp, ys)
```
